# Optimizing a Trainium2 kernel written in Bass

```python
import functools
import jax, jax.numpy as jnp
from jax import lax
import numpy as np


D_MODEL = 1024
BATCH = 2
SEQ = 8192
DEPTH = 4
DEC_BATCH = 32
DEC_SEQ = 16
PAST_LEN = 1024

CHUNK = 64
N_PREV_CHUNKS = 8
BAND = (N_PREV_CHUNKS + 1) * CHUNK
CACHE_A = min(N_PREV_CHUNKS * CHUNK, PAST_LEN)
D_MIX = D_MODEL
HEAD_DIM = 64
A_HEADS = 8
A_HEAD_DIM = HEAD_DIM
D_A = A_HEADS * A_HEAD_DIM
D_B = D_MIX // 4
D_C = D_MIX // 4
N_MIX_HEADS = D_MIX // HEAD_DIM
CONV_W = 3
POOL_WINDOWS = (2, 4, 8, 16)
POOL_GROUP = D_C // len(POOL_WINDOWS)
POOL_HIST = max(POOL_WINDOWS) - 1
REL_CLIP = 128
X_HEADS = 4
X_HEAD_DIM = D_MODEL // X_HEADS
N_MEM = 256
D_FF = 2816
D_IN = 3 * D_A + 3 * D_B + D_C
SPLITS = (D_A, 2 * D_A, 3 * D_A, 3 * D_A + D_B, 3 * D_A + 2 * D_B, 3 * D_A + 3 * D_B)
EPS = 1e-6
NEG = -1e30

kernel_name = 'hybrid_streaming_encoder_step'


def rmsnorm(x, g):
    xf = x.astype(jnp.float32)
    y = xf * lax.rsqrt(jnp.mean(xf * xf, axis=-1, keepdims=True) + EPS)
    return (y * g.astype(jnp.float32)).astype(x.dtype)


def head_norm(x, g):
    b, t, _ = x.shape
    xf = x.astype(jnp.float32).reshape(b, t, N_MIX_HEADS, HEAD_DIM)
    y = xf * lax.rsqrt(jnp.mean(xf * xf, axis=-1, keepdims=True) + EPS)
    return (y.reshape(b, t, D_MIX) * g.astype(jnp.float32)).astype(x.dtype)


def swiglu(h, w_gate, w_up, w_down):
    return (jax.nn.silu(h @ w_gate) * (h @ w_up)) @ w_down


def band_attention(q, k, v, rel, valid, rel_bias):
    s = jnp.einsum('bcqhd,bckhd->bchqk', q, k).astype(jnp.float32) * (A_HEAD_DIM ** -0.5)
    idx = jnp.clip(rel, -REL_CLIP, REL_CLIP) + REL_CLIP
    s = s + rel_bias.astype(jnp.float32)[:, idx][None, None]
    s = jnp.where(valid[None, :, None, None, :], s, NEG)
    p = jax.nn.softmax(s, axis=-1).astype(v.dtype)
    return jnp.einsum('bchqk,bckhd->bcqhd', p, v)


def attn_prompt(q, k, v, rel_bias):
    b, t, h, d = q.shape
    nc = t // CHUNK
    pad = N_PREV_CHUNKS * CHUNK
    k_full = jnp.pad(k, ((0, 0), (pad, 0), (0, 0), (0, 0)))
    v_full = jnp.pad(v, ((0, 0), (pad, 0), (0, 0), (0, 0)))

    def band(a):
        a = a.reshape(b, nc + N_PREV_CHUNKS, CHUNK, h, d)
        return jnp.concatenate([a[:, j:j + nc] for j in range(N_PREV_CHUNKS + 1)], axis=2)

    k_pos = (jnp.arange(nc)[:, None] - N_PREV_CHUNKS) * CHUNK + jnp.arange(BAND)[None, :]
    rel = pad + jnp.arange(CHUNK)[:, None] - jnp.arange(BAND)[None, :]
    out = band_attention(q.reshape(b, nc, CHUNK, h, d), band(k_full), band(v_full), rel, k_pos >= 0, rel_bias)
    return out.reshape(b, t, h * d), k_full[:, -CACHE_A:], v_full[:, -CACHE_A:]


def attn_sample(q, k, v, rel_bias, cache_k, cache_v):
    b, t, h, d = q.shape
    k_full = jnp.concatenate([cache_k.astype(k.dtype), k], axis=1)
    v_full = jnp.concatenate([cache_v.astype(v.dtype), v], axis=1)
    n_keys = CACHE_A + t
    rel = CACHE_A + jnp.arange(t)[:, None] - jnp.arange(n_keys)[None, :]
    valid = jnp.ones((1, n_keys), dtype=bool)
    out = band_attention(q[:, None], k_full[:, None], v_full[:, None], rel, valid, rel_bias)[:, 0]
    return out.reshape(b, t, h * d), k_full[:, -CACHE_A:], v_full[:, -CACHE_A:]


def short_conv(u, hist, w):
    t = u.shape[1]
    uf = jnp.concatenate([hist.astype(u.dtype), u], axis=1)
    y = uf[:, 0:t] * w[0]
    for i in range(1, CONV_W):
        y = y + uf[:, i:i + t] * w[i]
    return y, uf[:, -(CONV_W - 1):]


def multi_pool(u, hist, start_pos):
    t = u.shape[1]
    uf = jnp.concatenate([hist.astype(u.dtype), u], axis=1)
    cs = jnp.pad(jnp.cumsum(uf.astype(jnp.float32), axis=1), ((0, 0), (1, 0), (0, 0)))
    pos = start_pos + jnp.arange(t)
    hi = POOL_HIST + 1
    outs = []
    for g, w in enumerate(POOL_WINDOWS):
        sl = slice(g * POOL_GROUP, (g + 1) * POOL_GROUP)
        win_sum = cs[:, hi:hi + t, sl] - cs[:, hi - w:hi - w + t, sl]
        cnt = jnp.minimum(w, pos + 1).astype(jnp.float32)[None, :, None]
        outs.append(win_sum / cnt)
    pooled = jnp.concatenate(outs, axis=-1) - u.astype(jnp.float32)
    return pooled.astype(u.dtype), uf[:, -POOL_HIST:]


def memory_kv(mem, g_mem, w_xk, w_xv):
    b = mem.shape[0]
    mn = rmsnorm(mem, g_mem)
    mk = (mn @ w_xk).reshape(b, N_MEM, X_HEADS, X_HEAD_DIM)
    mv = (mn @ w_xv).reshape(b, N_MEM, X_HEADS, X_HEAD_DIM)
    return mk, mv


def cross_attn(h, mem_k, mem_v, w_xq, w_xo):
    b, t, _ = h.shape
    q = (h @ w_xq).reshape(b, t, X_HEADS, X_HEAD_DIM)
    s = jnp.einsum('bthd,bmhd->bhtm', q, mem_k.astype(h.dtype)).astype(jnp.float32) * (X_HEAD_DIM ** -0.5)
    p = jax.nn.softmax(s, axis=-1).astype(h.dtype)
    o = jnp.einsum('bhtm,bmhd->bthd', p, mem_v.astype(h.dtype)).reshape(b, t, D_MODEL)
    return o @ w_xo


def trunk_layer(x, start_pos, attn_fn, conv_hist, pool_hist, mem_k, mem_v,
                g_ffn1, w_ffn1_gate, w_ffn1_up, w_ffn1_down, g_mix, w_in, rel_bias, conv_w,
                pool_w, pool_scale, g_heads, w_out, g_xattn, w_xq, w_xo,
                g_ffn2, w_ffn2_gate, w_ffn2_up, w_ffn2_down):
    b, t, _ = x.shape
    x = x + 0.5 * swiglu(rmsnorm(x, g_ffn1), w_ffn1_gate, w_ffn1_up, w_ffn1_down)
    z = rmsnorm(x, g_mix) @ w_in
    q, k, v, bg, cg, hc, up = jnp.split(z, SPLITS, axis=-1)
    shp = (b, t, A_HEADS, A_HEAD_DIM)
    a_out, new_k, new_v = attn_fn(q.reshape(shp), k.reshape(shp), v.reshape(shp), rel_bias)
    conv_y, new_conv = short_conv(cg * hc, conv_hist, conv_w)
    b_out = bg * conv_y
    pooled, new_pool = multi_pool(up, pool_hist, start_pos)
    c_out = jnp.einsum('btgc,gcd->btgd', pooled.reshape(b, t, len(POOL_WINDOWS), POOL_GROUP), pool_w)
    c_out = c_out.reshape(b, t, D_C) * pool_scale
    mix = head_norm(jnp.concatenate([a_out, b_out, c_out], axis=-1), g_heads)
    x = x + mix @ w_out
    x = x + cross_attn(rmsnorm(x, g_xattn), mem_k, mem_v, w_xq, w_xo)
    x = x + 0.5 * swiglu(rmsnorm(x, g_ffn2), w_ffn2_gate, w_ffn2_up, w_ffn2_down)
    return x, new_k, new_v, new_conv, new_pool


def setup_inputs(seed: int = 0) -> dict:
    key = jax.random.key(seed)
    keys = iter(jax.random.split(key, 40))
    f32 = jnp.float32

    def nrm(shape, scale):
        return jax.random.normal(next(keys), shape, f32) * scale

    def gain(shape, s=0.05):
        return 1.0 + jax.random.normal(next(keys), shape, f32) * s

    return {
        'x_prompt': nrm((BATCH, SEQ, D_MODEL), 1.0),
        'x_sample': nrm((DEC_BATCH, DEC_SEQ, D_MODEL), 1.0),
        'mem_prompt': nrm((BATCH, N_MEM, D_MODEL), 1.0),
        'cache_attn_k': nrm((DEPTH, DEC_BATCH, CACHE_A, A_HEADS, A_HEAD_DIM), 1.0),
        'cache_attn_v': nrm((DEPTH, DEC_BATCH, CACHE_A, A_HEADS, A_HEAD_DIM), 1.0),
        'state_conv': nrm((DEPTH, DEC_BATCH, CONV_W - 1, D_B), 1.0),
        'state_pool': nrm((DEPTH, DEC_BATCH, POOL_HIST, D_C), 1.0),
        'cache_mem_k': nrm((DEPTH, DEC_BATCH, N_MEM, X_HEADS, X_HEAD_DIM), 1.0),
        'cache_mem_v': nrm((DEPTH, DEC_BATCH, N_MEM, X_HEADS, X_HEAD_DIM), 1.0),
        'g_ffn1': gain((DEPTH, D_MODEL)),
        'w_ffn1_gate': nrm((DEPTH, D_MODEL, D_FF), D_MODEL ** -0.5),
        'w_ffn1_up': nrm((DEPTH, D_MODEL, D_FF), D_MODEL ** -0.5),
        'w_ffn1_down': nrm((DEPTH, D_FF, D_MODEL), D_FF ** -0.5),
        'g_mix': gain((DEPTH, D_MODEL)),
        'w_in': nrm((DEPTH, D_MODEL, D_IN), D_MODEL ** -0.5),
        'rel_bias': nrm((DEPTH, A_HEADS, 2 * REL_CLIP + 1), 0.5),
        'conv_w': nrm((DEPTH, CONV_W, D_B), 0.5),
        'pool_w': nrm((DEPTH, len(POOL_WINDOWS), POOL_GROUP, POOL_GROUP), POOL_GROUP ** -0.5),
        'pool_scale': gain((DEPTH, D_C), 0.1),
        'g_heads': gain((DEPTH, D_MIX)),
        'w_out': nrm((DEPTH, D_MIX, D_MODEL), D_MIX ** -0.5),
        'g_xattn': gain((DEPTH, D_MODEL)),
        'g_mem': gain((DEPTH, D_MODEL)),
        'w_xq': nrm((DEPTH, D_MODEL, D_MODEL), D_MODEL ** -0.5),
        'w_xk': nrm((DEPTH, D_MODEL, D_MODEL), D_MODEL ** -0.5),
        'w_xv': nrm((DEPTH, D_MODEL, D_MODEL), D_MODEL ** -0.5),
        'w_xo': nrm((DEPTH, D_MODEL, D_MODEL), D_MODEL ** -0.5),
        'g_ffn2': gain((DEPTH, D_MODEL)),
        'w_ffn2_gate': nrm((DEPTH, D_MODEL, D_FF), D_MODEL ** -0.5),
        'w_ffn2_up': nrm((DEPTH, D_MODEL, D_FF), D_MODEL ** -0.5),
        'w_ffn2_down': nrm((DEPTH, D_FF, D_MODEL), D_FF ** -0.5),
        'g_final': gain((D_MODEL,)),
    }


def reference(x_prompt, x_sample, mem_prompt, cache_attn_k, cache_attn_v, state_conv, state_pool,
              cache_mem_k, cache_mem_v, g_ffn1, w_ffn1_gate, w_ffn1_up, w_ffn1_down, g_mix, w_in,
              rel_bias, conv_w, pool_w, pool_scale, g_heads, w_out, g_xattn, g_mem, w_xq, w_xk, w_xv,
              w_xo, g_ffn2, w_ffn2_gate, w_ffn2_up, w_ffn2_down, g_final):
    stacked = (g_ffn1, w_ffn1_gate, w_ffn1_up, w_ffn1_down, g_mix, w_in, rel_bias, conv_w,
               pool_w, pool_scale, g_heads, w_out, g_xattn, w_xq, w_xo,
               g_ffn2, w_ffn2_gate, w_ffn2_up, w_ffn2_down)

    bp = x_prompt.shape[0]
    xp = x_prompt
    pk, pv, pc, pp, pmk, pmv = [], [], [], [], [], []
    for l in range(DEPTH):
        lw = [w[l] for w in stacked]
        mk, mv = memory_kv(mem_prompt, g_mem[l], w_xk[l], w_xv[l])
        conv_hist = jnp.zeros((bp, CONV_W - 1, D_B), x_prompt.dtype)
        pool_hist = jnp.zeros((bp, POOL_HIST, D_C), x_prompt.dtype)
        xp, nk, nv, nc, npl = trunk_layer(xp, 0, attn_prompt, conv_hist, pool_hist, mk, mv, *lw)
        pk.append(nk); pv.append(nv); pc.append(nc); pp.append(npl); pmk.append(mk); pmv.append(mv)
    y_prompt = rmsnorm(xp, g_final)

    xs = x_sample
    sk, sv, sc, sp = [], [], [], []
    for l in range(DEPTH):
        lw = [w[l] for w in stacked]
        attn_fn = functools.partial(attn_sample, cache_k=cache_attn_k[l], cache_v=cache_attn_v[l])
        xs, nk, nv, nc, npl = trunk_layer(xs, PAST_LEN, attn_fn, state_conv[l], state_pool[l],
                                          cache_mem_k[l], cache_mem_v[l], *lw)
        sk.append(nk); sv.append(nv); sc.append(nc); sp.append(npl)
    y_sample = rmsnorm(xs, g_final)

    return (y_prompt, y_sample,
            jnp.stack(pk), jnp.stack(pv), jnp.stack(pc), jnp.stack(pp), jnp.stack(pmk), jnp.stack(pmv),
            jnp.stack(sk), jnp.stack(sv), jnp.stack(sc), jnp.stack(sp))
```

```python
import numpy as np
import concourse.bass as bass
import concourse.mybir as mybir
from concourse.bass_utils import run_bass_kernel_spmd

F32 = mybir.dt.float32
BF16 = mybir.dt.bfloat16
AF = mybir.ActivationFunctionType
ALU = mybir.AluOpType

L = 4
D = 1024
DFF = 2816
NCORE = 8
SEG = 2048
TP = 1024
NS = 64
TOK = TP + NS
WBC = 16 + TP + 128
KTW = 512 + TOK
EPS = 1e-6
NEGM = -30000.0
TILES = [(0, 512), (512, 512), (1024, 64)]
DBG = dict(passes=None, stages=None)

G_FFN1, G_MIX, G_HEADS, G_XATTN, G_MEM, G_FFN2 = range(6)
def gcol(l, which, c): return (l * 6 + which) * 8 + c
SM_GFINAL = L * 6 * 8
SM_CONV = SM_GFINAL + 8
SM_PSCALE = SM_CONV + L * 3 * 2
SM_INVW = SM_PSCALE + L * 2
SM_INVCNT = SM_INVW + 2
SM_HMASK = SM_INVCNT + 32
SM_TFLAG = SM_HMASK + 1
SM_EPS = SM_TFLAG + 1
SM_ZERO = SM_EPS + 1
SM_N = SM_ZERO + 1
CB_IDENT, CB_BLK64, CB_MEAN, CB_ONES, CB_ZERO, CB_SM0, CB_SM4 = range(7)


class Prog:
    ENG = ["pe", "act", "dve", "pool", "sp"]

    def __init__(self, nc, sems, dma_sems):
        self.nc = nc
        self.ops = {e: [] for e in self.ENG}
        self.cnt = {e: 0 for e in self.ENG}
        self.sem = sems
        self.dma_sems = dma_sems
        self.dma_cnt = {q: [0] * len(dma_sems[q]) for q in dma_sems}
        self.dma_rr = {q: 0 for q in dma_sems}
        self.waited = {e: {} for e in self.ENG}
        self.last_w = {}
        self.last_r = {}

    def _deps(self, eng, reads, writes):
        toks = []
        for r in reads:
            t = self.last_w.get(r)
            if t is not None:
                toks.append(t)
        for w in writes:
            t = self.last_w.get(w)
            if t is not None:
                toks.append(t)
            toks.extend(self.last_r.get(w, []))
        need = {}
        for (key, semh, val, src) in toks:
            if src == eng and eng == "pe":
                continue
            if self.waited[eng].get(key, 0) >= val:
                continue
            if need.get(key, (None, 0))[1] < val:
                need[key] = (semh, val)
        for key, (semh, val) in need.items():
            self.waited[eng][key] = val
        return list(need.values())

    def _record(self, tok, reads, writes):
        for w in writes:
            self.last_w[w] = tok
            self.last_r[w] = []
        for r in reads:
            self.last_r.setdefault(r, []).append(tok)

    def op(self, eng, fn, reads=(), writes=()):
        waits = self._deps(eng, reads, writes)
        self.cnt[eng] += 1
        tok = (eng, self.sem[eng], self.cnt[eng], eng)
        self.ops[eng].append((waits, fn, (self.sem[eng], 1)))
        self._record(tok, reads, writes)

    def dma(self, q, out, in_, reads=(), writes=()):
        i = self.dma_rr[q]
        self.dma_rr[q] = (i + 1) % len(self.dma_sems[q])
        semh = self.dma_sems[q][i]
        key = (q, i)
        waits = self._deps(q, reads, writes)
        prev = self.dma_cnt[q][i]
        if prev > 0 and self.waited[q].get(key, 0) < prev:
            waits.append((semh, prev))
            self.waited[q][key] = prev
        self.dma_cnt[q][i] = prev + 16
        tok = (key, semh, prev + 16, None)
        self.ops[q].append((waits, lambda e, o=out, s=in_: e.dma_start(out=o, in_=s), (semh, 16)))
        self._record(tok, reads, writes)

    def barrier(self):
        for e in self.ENG:
            waits = []
            for o in self.ENG:
                if o != e and self.cnt[o] > 0 and self.waited[e].get(o, 0) < self.cnt[o]:
                    waits.append((self.sem[o], self.cnt[o]))
                    self.waited[e][o] = self.cnt[o]
            for q in self.dma_sems:
                for i, semh in enumerate(self.dma_sems[q]):
                    c = self.dma_cnt[q][i]
                    if c > 0 and self.waited[e].get((q, i), 0) < c:
                        waits.append((semh, c))
                        self.waited[e][(q, i)] = c
            if waits:
                self.ops[e].append((waits, None, None))

    def final_waits(self):
        waits = []
        for q in self.dma_sems:
            for i, semh in enumerate(self.dma_sems[q]):
                if self.dma_cnt[q][i] > 0:
                    waits.append((semh, self.dma_cnt[q][i]))
        for e in self.ENG:
            if self.cnt[e] > 0 and e != "sp":
                waits.append((self.sem[e], self.cnt[e]))
        self.ops["sp"].append((waits, None, None))

    def emit(self, block):
        handles = {"pe": block.tensor, "act": block.scalar, "dve": block.vector,
                   "pool": block.gpsimd, "sp": block.sync}
        for eng in self.ENG:
            ops = self.ops[eng]
            if not ops:
                continue

            def body(e, ops=ops):
                for waits, fn, inc in ops:
                    for semh, val in waits:
                        e.wait_ge(semh, val)
                    if fn is not None:
                        ins = fn(e)
                        ins.then_inc(inc[0], inc[1])
            handles[eng](body)


def build_program():
    nc = bass.Bass("TRN2", target_bir_lowering=False)
    dt = nc.dram_tensor

    def din(name, shape, dtype=F32):
        return dt(name, list(shape), dtype, kind="ExternalInput").ap()

    def dout(name, shape, dtype=F32):
        return dt(name, list(shape), dtype, kind="ExternalOutput").ap()

    xT_in = din("xT_in", [4, D, TP])
    xsT_in = din("xsT_in", [D, NS])
    memT_in = din("memT_in", [D, 256])
    small_in = din("small_in", [128, SM_N])
    cbf_in = din("cbf_in", [128, 7 * 128])
    bias_in = din("bias_in", [L, 640, 8, 128])
    sbias_in = din("sbias_in", [L, 640, 8, 16])
    poolw_in = din("poolw_in", [L, 128, 2, 128])
    shu_in = din("shu_in", [L, 128, 2, 4, 16])
    shp_in = din("shp_in", [L, 128, 2, 4, 16])
    ck_in = din("ck_in", [L, 4, 512, 512])
    cv_in = din("cv_in", [L, 4, 512, 512])
    cmk_in = din("cmk_in", [L, 4, 256, 1024])
    cmv_in = din("cmv_in", [L, 4, 256, 1024])
    w_g1 = din("w_g1", [L, D, DFF]); w_u1 = din("w_u1", [L, D, DFF]); w_d1 = din("w_d1", [L, DFF, D])
    w_g2 = din("w_g2", [L, D, DFF]); w_u2 = din("w_u2", [L, D, DFF]); w_d2 = din("w_d2", [L, DFF, D])
    w_in = din("w_in", [L, D, 2560]); w_out = din("w_out", [L, D, D])
    w_xq = din("w_xq", [L, D, D]); w_xk = din("w_xk", [L, D, D]); w_xv = din("w_xv", [L, D, D]); w_xo = din("w_xo", [L, D, D])

    o_yT = dout("o_yT", [D, SEG + NS])
    o_pk = dout("o_pk", [L, 512, 512]); o_pv = dout("o_pv", [L, 512, 512])
    o_pc = dout("o_pc", [L, 128, 2, 16]); o_pp = dout("o_pp", [L, 128, 2, 16])
    o_mkT = dout("o_mkT", [L, D, 256]); o_mv = dout("o_mv", [L, 256, D])
    o_sk = dout("o_sk", [L, 4, 512, 512]); o_sv = dout("o_sv", [L, 4, 512, 512])
    o_sc = dout("o_sc", [L, 128, 2, 4, 16]); o_sp = dout("o_sp", [L, 128, 2, 4, 16])

    car_k = [dt(f"car_k{l}", [128, 4, 512], BF16).ap() for l in range(L)]
    car_v = [dt(f"car_v{l}", [128, 4, 512], BF16).ap() for l in range(L)]
    car_u = [dt(f"car_u{l}", [128, 2, 16], F32).ap() for l in range(L)]
    car_p = [dt(f"car_p{l}", [128, 2, 16], F32).ap() for l in range(L)]

    import contextlib
    with contextlib.ExitStack() as es:
        def sb(name, shape, dtype):
            return es.enter_context(nc.sbuf_tensor(name, list(shape), dtype))

        xT = sb("xT", [128, 8, TOK], F32)
        hT = sb("hT", [128, 8, TOK], BF16)
        bcraw = sb("bcraw", [128, 2 * 2 * WBC * 2], BF16)
        upb = sb("upb", [128, 2, WBC], F32)
        tmpA = sb("tmpA", [128, 2, WBC], F32)
        pooled = sb("pooled", [128, 2, WBC], BF16)
        qT = sb("qT", [128, 8, TOK], BF16)
        kT = sb("kT", [128, 4, KTW], BF16)
        vtok = sb("vtok", [128, 12, 512], BF16)
        biasraw = sb("biasraw", [128, 5 * 8 * 128], BF16)
        pT = sb("pT", [128, 2, 1024], BF16)
        memk = sb("memk", [128, 8, 256], BF16)
        memv = sb("memv", [128, 2, 1024], BF16)
        svnew = sb("svnew", [128, 4, 512], BF16)
        ring = sb("ring", [128, 4, 4096], BF16)
        small = sb("small", [128, SM_N], F32)
        cbf = sb("cbf", [128, 7, 128], BF16)
        poolw = sb("poolw", [128, 2, 128], BF16)
        sqb = sb("sqb", [128, 2, 2, 512], BF16)
        nrm1 = sb("nrm1", [128, 512], F32)
        nrm2 = sb("nrm2", [128, 512], F32)
        stg = sb("stg", [128, 2, 512], F32)
        psum = es.enter_context(nc.psum_tensor("psum", [128, 8, 512], F32))

        bgcg = bcraw[:, :].bitcast(F32).rearrange("p (a c n) -> p a c n", a=2, c=2)
        bg = bgcg[:, 0]
        cg = bgcg[:, 1]
        actT = bcraw[:, 0:8 * TOK].rearrange("p (c n) -> p c n", c=8)
        bias = biasraw[:, :].rearrange("p (j h q) -> p j h q", j=5, h=8)
        skT = biasraw[:, 0:4 * 528].rearrange("p (c n) -> p c n", c=4)
        sV = biasraw[:, 2112:2112 + 2048].rearrange("p (k n) -> p k n", k=4)
        knat = pT[:, :, :].rearrange("p a n -> p (a n)").rearrange("p (k n) -> p k n", k=4)
        sbias = memv[:, 0, 0:640].rearrange("p (j h q) -> p j h q", j=5, h=8)
        spT = sqb[:, 0, :, :].rearrange("p a n -> p (a n)")[:, 0:640].rearrange("p (k h q) -> p k h q", k=5, h=8)

        sem_names = ["pe", "act", "dve", "pool", "sp"]
        sems = {e: es.enter_context(nc.semaphore("s_" + e)) for e in sem_names}
        dma_sems = {q: [es.enter_context(nc.semaphore(f"d_{q}{i}")) for i in range(n)]
                    for q, n in (("pool", 12), ("sp", 16))}
        block = es.enter_context(nc.Block())
        P = Prog(nc, sems, dma_sems)

        def smc(col, n=1, p0=0, p1=128):
            return small[p0:p1, col:col + n]

        def cb(i, p0=0, p1=128, c0=0, c1=128):
            return cbf[p0:p1, i, c0:c1]

        def bank(b):
            return psum[:, b, :]

        P.dma("sp", small[:, :], small_in[:, :], writes=["small"])
        P.dma("pool", cbf[:, :, :].rearrange("p a n -> p (a n)"), cbf_in[:, :], writes=["cbf"])

        zf = [(xT, "p a n -> p (a n)"), (hT, "p a n -> p (a n)"), (bcraw, None), (upb, "p a n -> p (a n)"), (tmpA, "p a n -> p (a n)"),
              (pooled, "p a n -> p (a n)"), (qT, "p a n -> p (a n)"), (kT, "p a n -> p (a n)"), (vtok, "p a n -> p (a n)"), (biasraw, None),
              (pT, "p a n -> p (a n)"), (memk, "p a n -> p (a n)"), (memv, "p a n -> p (a n)"), (svnew, "p a n -> p (a n)"),
              (sqb, "p a b n -> p (a b n)"), (nrm1, None), (nrm2, None), (stg, "p a n -> p (a n)")]
        for zi, (zt, pat) in enumerate(zf):
            zv = zt[:, :] if pat is None else zt[:].rearrange(pat)
            P.op("dve" if zi % 2 == 0 else "pool", lambda e, zv=zv: e.memset(zv, 0.0), writes=[("z", zi)])
        zkeys = [("z", zi) for zi in range(len(zf))]
        for l in range(L):
            P.dma("sp", car_k[l][:, :, :], kT[:, :, 0:512], reads=zkeys, writes=[("car_k", l)])
            P.dma("sp", car_v[l][:, :, :], vtok[:, 0:4, :], reads=zkeys, writes=[("car_v", l)])
            P.dma("sp", car_u[l][:, :, :], upb[:, :, 0:16], reads=zkeys, writes=[("car_u", l)])
            P.dma("sp", car_p[l][:, :, :], upb[:, :, 0:16], reads=zkeys, writes=[("car_p", l)])
        P.barrier()
        ring_i = [0]

        def load_w(src_ap, kc, ncols):
            s = ring_i[0]
            ring_i[0] = (s + 1) % 4
            dst = ring[:, s, 0:kc * ncols].rearrange("p (k n) -> p k n", k=kc)
            P.dma("pool", dst, src_ap.rearrange("(k p) n -> p k n", p=128), writes=[("ring", s)])
            return s, dst

        bank_rr = [0]

        def next_bank():
            b = bank_rr[0]
            bank_rr[0] = (b + 1) % 8
            return b

        def rmsnorm(tiles, gwhich_col0, out_fn=None):
            for ti in tiles:
                c0, n = TILES[ti]
                b = next_bank()
                for qd in range(4):
                    half = qd % 2
                    P.op("act", lambda e, half=half, qd=qd, c0=c0, n=n: e.activation(
                        out=sqb[:, half, :, 0:n], in_=xT[:, 2 * qd:2 * qd + 2, c0:c0 + n], func=AF.Square),
                        reads=[("xT", ti)], writes=[("sqb", half)])
                    for k2 in range(2):
                        k = 2 * qd + k2
                        P.op("pe", lambda e, half=half, k2=k2, k=k, b=b, n=n: e.matmul(
                            psum[:, b, 0:n], lhsT=cb(CB_MEAN), rhs=sqb[:, half, k2, 0:n], start=(k == 0), stop=(k == 7)),
                            reads=[("sqb", half), "cbf"], writes=[("ps", b)])
                tb, tkey = ((nrm1, "nrm1"), (nrm2, "nrm2"))[ti % 2]
                P.op("act", lambda e, b=b, n=n, tb=tb: e.activation(out=tb[:, 0:n], in_=psum[:, b, 0:n], func=AF.Sqrt,
                                                                    bias=smc(SM_EPS), scale=1.0),
                     reads=[("ps", b), "small"], writes=[tkey])
                P.op("dve", lambda e, n=n, tb=tb: e.reciprocal(out=tb[:, 0:n], in_=tb[:, 0:n]),
                     reads=[tkey], writes=[tkey])
                if out_fn is not None:
                    out_fn(ti, c0, n, tb, tkey)
                    continue
                for k in range(8):
                    P.op("dve", lambda e, k=k, c0=c0, n=n, tb=tb: e.scalar_tensor_tensor(
                        out=hT[:, k, c0:c0 + n], in0=xT[:, k, c0:c0 + n], scalar=smc(gwhich_col0 + k),
                        in1=tb[:, 0:n], op0=ALU.mult, op1=ALU.mult),
                        reads=[("xT", ti), tkey, "small"], writes=[("hT", ti)])

        def linear_fm(wsrc, src_buf, src_key, kc_n, m_chunks, tiles, evac):
            if not tiles:
                return
            for mg in range((m_chunks + 3) // 4):
                s, wt = load_w(wsrc(mg), kc_n, 512)
                for ti in tiles:
                    c0, n = TILES[ti]
                    for mi in range(min(4, m_chunks - mg * 4)):
                        m = mg * 4 + mi
                        b = next_bank()
                        for k in range(kc_n):
                            P.op("pe", lambda e, k=k, b=b, n=n, c0=c0, wt=wt, mi=mi: e.matmul(
                                psum[:, b, 0:n], lhsT=wt[:, k, mi * 128:(mi + 1) * 128], rhs=src_buf[:, k, c0:c0 + n],
                                start=(k == 0), stop=(k == kc_n - 1)),
                                reads=[("ring", s), (src_key, ti)], writes=[("ps", b)])
                        evac(m, ti, c0, n, b)

        def add_to_x(scale):
            def evac(m, ti, c0, n, b):
                P.op("dve", lambda e, m=m, c0=c0, n=n, b=b: e.scalar_tensor_tensor(
                    out=xT[:, m, c0:c0 + n], in0=psum[:, b, 0:n], scalar=scale, in1=xT[:, m, c0:c0 + n],
                    op0=ALU.mult, op1=ALU.add),
                    reads=[("ps", b), ("xT", ti)], writes=[("xT", ti)])
            return evac

        def ffn(l, wg, wu, wd, gwhich, tiles):
            if not tiles:
                return
            rmsnorm(tiles, gcol(l, gwhich, 0))
            groups = [(0, 8), (8, 16), (16, 22)]
            for (g0, g1) in groups:
                for sub in range(g0, g1, 4):
                    nch = min(4, g1 - sub)
                    sg, wgt = load_w(wg[l, :, sub * 128:(sub + nch) * 128], 8, nch * 128)
                    su, wut = load_w(wu[l, :, sub * 128:(sub + nch) * 128], 8, nch * 128)
                    for mi in range(nch):
                        mloc = sub - g0 + mi
                        for ti in tiles:
                            c0, n = TILES[ti]
                            ba = next_bank()
                            bb = next_bank()
                            for k in range(8):
                                P.op("pe", lambda e, k=k, ba=ba, n=n, c0=c0, wgt=wgt, mi=mi: e.matmul(
                                    psum[:, ba, 0:n], lhsT=wgt[:, k, mi * 128:(mi + 1) * 128], rhs=hT[:, k, c0:c0 + n],
                                    start=(k == 0), stop=(k == 7)),
                                    reads=[("ring", sg), ("hT", ti)], writes=[("ps", ba)])
                            for k in range(8):
                                P.op("pe", lambda e, k=k, bb=bb, n=n, c0=c0, wut=wut, mi=mi: e.matmul(
                                    psum[:, bb, 0:n], lhsT=wut[:, k, mi * 128:(mi + 1) * 128], rhs=hT[:, k, c0:c0 + n],
                                    start=(k == 0), stop=(k == 7)),
                                    reads=[("ring", su), ("hT", ti)], writes=[("ps", bb)])
                            P.op("act", lambda e, ba=ba, n=n: e.activation(out=nrm1[:, 0:n], in_=psum[:, ba, 0:n], func=AF.Silu),
                                 reads=[("ps", ba)], writes=["nrm1"])
                            P.op("dve", lambda e, bb=bb, n=n, c0=c0, mloc=mloc: e.tensor_tensor(
                                out=actT[:, mloc, c0:c0 + n], in0=psum[:, bb, 0:n], in1=nrm1[:, 0:n], op=ALU.mult),
                                reads=[("ps", bb), "nrm1"], writes=[("actT", ti), "bc"])
                slots = []
                for sub in range(g0, g1, 4):
                    nch = min(4, g1 - sub)
                    s, wt = load_w(wd[l, sub * 128:(sub + nch) * 128, :], nch, 1024)
                    slots.append((s, wt, nch))
                nk = g1 - g0
                for ti in tiles:
                    c0, n = TILES[ti]
                    for m2 in range(8):
                        b = next_bank()
                        kk = 0
                        for (s, wt, nch) in slots:
                            for k4 in range(nch):
                                P.op("pe", lambda e, k4=k4, kk=kk, b=b, n=n, c0=c0, wt=wt, m2=m2, nk=nk: e.matmul(
                                    psum[:, b, 0:n], lhsT=wt[:, k4, m2 * 128:(m2 + 1) * 128], rhs=actT[:, kk, c0:c0 + n],
                                    start=(kk == 0), stop=(kk == nk - 1)),
                                    reads=[("ring", s), ("actT", ti), "bc"], writes=[("ps", b)])
                                kk += 1
                        add_to_x(0.5)(m2, ti, c0, n, b)

        def bccol(ti):
            c0, n = TILES[ti]
            return 16 + c0, n

        def w_in_stage(l, ps_, tilesA):
            if not tilesA:
                return
            has_s = 2 in tilesA
            ptiles = [t for t in tilesA if t < 2]

            def bc_dst(buf, mloc, ti):
                if ti < 2:
                    c, n = bccol(ti)
                    return buf[:, mloc, c:c + n]
                return buf[:, mloc, 16 + TP:16 + TP + 128].rearrange("p (b s) -> p b s", b=4)[:, :, 16:32]

            def ps_src(b, ti, n):
                if ti < 2:
                    return psum[:, b, 0:n]
                return psum[:, b, 0:64].rearrange("p (b s) -> p b s", b=4)

            def evac_q(m, ti, c0, n, b):
                P.op("act", lambda e: e.activation(out=qT[0:64, 2 * m, c0:c0 + n], in_=psum[0:64, b, 0:n], func=AF.Copy, scale=0.125),
                     reads=[("ps", b)], writes=[("qT", ti)])
                P.op("act", lambda e: e.activation(out=qT[64:128, 2 * m + 1, c0:c0 + n], in_=psum[64:128, b, 0:n], func=AF.Copy, scale=0.125),
                     reads=[("ps", b)], writes=[("qT", ti)])

            def evac_k(m, ti, c0, n, b):
                P.op("act", lambda e: e.activation(out=kT[:, m, 512 + c0:512 + c0 + n], in_=psum[:, b, 0:n], func=AF.Copy),
                     reads=[("ps", b)], writes=[("kT", ti + 1)])

            def evac_bc(m, ti, c0, n, b):
                mm = m
                buf = bg if mm < 2 else cg
                P.op("act", lambda e: e.activation(out=bc_dst(buf, mm % 2, ti), in_=ps_src(b, ti, n), func=AF.Copy),
                     reads=[("ps", b)], writes=[("bc", ti), "bc"])

            def evac_hu(m, ti, c0, n, b):
                if m < 2:
                    P.op("dve", lambda e: e.tensor_tensor(out=bc_dst(cg, m, ti), in0=ps_src(b, ti, n), in1=bc_dst(cg, m, ti), op=ALU.mult),
                         reads=[("ps", b), ("bc", ti)], writes=[("bc", ti), "bc"])
                else:
                    P.op("act", lambda e: e.activation(out=bc_dst(upb, m - 2, ti), in_=ps_src(b, ti, n), func=AF.Copy),
                         reads=[("ps", b)], writes=[("up", ti)])

            wl = w_in[l]
            sub = DBG.get("sub")
            son = lambda nm: sub is None or nm in sub
            if son("q"):
                linear_fm(lambda mg: wl[:, 0:512], hT, "hT", 8, 4, tilesA, evac_q)
            if not son("k"):
                return
            s, wt = load_w(wl[:, 512:1024], 8, 512)
            for mi in range(4):
                for ti in tilesA:
                    c0, n = TILES[ti]
                    b = next_bank()
                    for k in range(8):
                        P.op("pe", lambda e, k=k, b=b, n=n, c0=c0, mi=mi, wt=wt: e.matmul(
                            psum[:, b, 0:n], lhsT=wt[:, k, mi * 128:(mi + 1) * 128], rhs=hT[:, k, c0:c0 + n],
                            start=(k == 0), stop=(k == 7)), reads=[("ring", s), ("hT", ti)], writes=[("ps", b)])
                    evac_k(mi, ti, c0, n, b)
            tok_out(l, ps_, s, wt, tilesA, o_pk, o_sk, False)
            if not son("v"):
                return
            s, wt = load_w(wl[:, 1024:1536], 8, 512)
            for ti in ptiles:
                c0, n = TILES[ti]
                for j in range(4):
                    b = next_bank()
                    cc = c0 + j * 128
                    for k in range(8):
                        P.op("pe", lambda e, k=k, b=b, cc=cc, wt=wt: e.matmul(
                            psum[:, b, :], lhsT=hT[:, k, cc:cc + 128], rhs=wt[:, k, :], start=(k == 0), stop=(k == 7)),
                            reads=[("ring", s), ("hT", ti)], writes=[("ps", b)])
                    vt = 4 + ti * 4 + j
                    P.op("dve", lambda e, b=b, vt=vt: e.tensor_copy(out=vtok[:, vt, :], in_=psum[:, b, :]),
                         reads=[("ps", b)], writes=[("vtok", ti + 1)])
            if has_s:
                for bq in range(4):
                    b = next_bank()
                    cc = TP + 16 * bq
                    for k in range(8):
                        P.op("pe", lambda e, k=k, b=b, cc=cc, wt=wt: e.matmul(
                            psum[0:16, b, :], lhsT=hT[:, k, cc:cc + 16], rhs=wt[:, k, :], start=(k == 0), stop=(k == 7)),
                            reads=[("ring", s), ("hT", 2)], writes=[("ps", b)])
                    P.op("dve", lambda e, b=b, bq=bq: e.tensor_copy(out=svnew[0:16, bq, :], in_=psum[0:16, b, :]),
                         reads=[("ps", b)], writes=["svnew"])
            tok_out(l, ps_, s, wt, tilesA, o_pv, o_sv, True)
            if not son("bc"):
                return
            linear_fm(lambda mg: wl[:, 1536:2048], hT, "hT", 8, 4, tilesA, evac_bc)
            if not son("hu"):
                return
            linear_fm(lambda mg: wl[:, 2048:2560], hT, "hT", 8, 4, tilesA, evac_hu)

        def tok_out(l, ps_, s, wt, tilesA, o_p, o_s, is_v):
            if ps_["name"] != "O2":
                return
            for j in range(4):
                b = next_bank()
                cc = 512 + j * 128
                for k in range(8):
                    P.op("pe", lambda e, k=k, b=b, cc=cc: e.matmul(
                        psum[:, b, :], lhsT=hT[:, k, cc:cc + 128], rhs=wt[:, k, :], start=(k == 0), stop=(k == 7)),
                        reads=[("ring", s), ("hT", 1)], writes=[("ps", b)])
                sj = j % 2
                P.op("act", lambda e, b=b, sj=sj: e.activation(out=stg[:, sj, :], in_=psum[:, b, :], func=AF.Copy),
                     reads=[("ps", b)], writes=[("stg", sj)])
                P.dma("sp", o_p[l, j * 128:(j + 1) * 128, :], stg[:, sj, :], reads=[("stg", sj)], writes=[("o_p", is_v, l, j)])
            b = next_bank()
            for k in range(8):
                P.op("pe", lambda e, k=k, b=b: e.matmul(
                    psum[0:64, b, :], lhsT=hT[:, k, TP:TP + 64], rhs=wt[:, k, :], start=(k == 0), stop=(k == 7)),
                    reads=[("ring", s), ("hT", 2)], writes=[("ps", b)])
            P.op("act", lambda e, b=b: e.activation(out=stg[0:64, 0, :], in_=psum[0:64, b, :], func=AF.Copy),
                 reads=[("ps", b)], writes=[("stg", 0)])
            for bq in range(4):
                P.dma("sp", o_s[l, bq, 496:512, :], stg[16 * bq:16 * bq + 16, 0, :], reads=[("stg", 0)],
                      writes=[("o_s", is_v, l, bq)])

        def carry_out(l, ps_, tilesA):
            if 1 not in tilesA or ps_["name"] == "O2":
                return
            P.dma("sp", car_k[l][:, :, :], kT[:, :, 512 + 512:512 + 1024], reads=[("kT", 2)], writes=[("car_k", l)])
            P.dma("sp", car_v[l][:, :, :], vtok[:, 8:12, :], reads=[("vtok", 2)], writes=[("car_v", l)])
            P.dma("sp", car_u[l][:, :, :], cg[:, :, 16 + TP - 16:16 + TP], reads=[("bc", 1), "bc"], writes=[("car_u", l)])
            P.dma("sp", car_p[l][:, :, :], upb[:, :, 16 + TP - 16:16 + TP], reads=[("up", 1)], writes=[("car_p", l)])

        def carry_in(l, ps_, tilesR):
            if 0 not in tilesR or ps_["name"] == "H1":
                return
            import os
            cv_ = os.environ.get("CIN", "k,v,u,p,f").split(",")
            if "k" in cv_:
                P.dma("sp", kT[:, :, 0:512], car_k[l][:, :, :], reads=[("car_k", l)], writes=[("kT", 0)])
            if "v" in cv_:
                P.dma("sp", vtok[:, 0:4, :], car_v[l][:, :, :], reads=[("car_v", l)], writes=[("vtok", 0)])
            if "u" in cv_:
                P.dma("sp", cg[:, :, 0:16], car_u[l][:, :, :], reads=[("car_u", l)], writes=[("bch", 0), "bc"])
            if "p" in cv_:
                P.dma("sp", upb[:, :, 0:16], car_p[l][:, :, :], reads=[("car_p", l)], writes=[("uph", 0)])
            if ps_["name"] == "O1" and "f" in cv_:
                P.op("act", lambda e: e.activation(out=cg[:, :, 0:16], in_=cg[:, :, 0:16], func=AF.Copy, scale=smc(SM_TFLAG)),
                     reads=[("bch", 0), "small"], writes=[("bch", 0), "bc"])
                P.op("act", lambda e: e.activation(out=upb[:, :, 0:16], in_=upb[:, :, 0:16], func=AF.Copy, scale=smc(SM_TFLAG)),
                     reads=[("uph", 0), "small"], writes=[("uph", 0)])

        hn_par = [0]

        def headnorm_bc(l, src, srctok, chunk0, dti, c, n, dstc0):
            for cc in range(2):
                b = next_bank()
                par = hn_par[0]
                hn_par[0] ^= 1
                tb, tkey = ((nrm1, "nrm1"), (nrm2, "nrm2"))[par]
                P.op("act", lambda e, cc=cc, par=par: e.activation(out=sqb[:, par, 0, 0:n], in_=src[:, cc, c:c + n], func=AF.Square),
                     reads=[srctok], writes=[("sqb", par)])
                P.op("pe", lambda e, b=b, par=par: e.matmul(psum[:, b, 0:n], lhsT=cb(CB_BLK64), rhs=sqb[:, par, 0, 0:n], start=True, stop=True),
                     reads=[("sqb", par), "cbf"], writes=[("ps", b)])
                P.op("act", lambda e, b=b, tb=tb: e.activation(out=tb[:, 0:n], in_=psum[:, b, 0:n], func=AF.Sqrt, bias=smc(SM_EPS), scale=1.0),
                     reads=[("ps", b), "small"], writes=[tkey])
                P.op("dve", lambda e, tb=tb: e.reciprocal(out=tb[:, 0:n], in_=tb[:, 0:n]), reads=[tkey], writes=[tkey])
                P.op("dve", lambda e, cc=cc, tb=tb: e.scalar_tensor_tensor(
                    out=hT[:, chunk0 + cc, dstc0:dstc0 + n], in0=src[:, cc, c:c + n], scalar=smc(gcol(l, G_HEADS, chunk0 + cc)),
                    in1=tb[:, 0:n], op0=ALU.mult, op1=ALU.mult),
                    reads=[srctok, tkey, "small"], writes=[("hT", dti)])

        def bc_stage(l, ps_, tilesR):
            if not tilesR:
                return
            pt = [t for t in tilesR if t < 2]
            has_s = 2 in tilesR
            regs = []
            if pt:
                a = 16 + TILES[pt[0]][0]
                bnd = 16 + TILES[pt[-1]][0] + TILES[pt[-1]][1]
                regs.append(("p", a, bnd))
            if has_s:
                regs.append(("s", 0, 0))
                sview_u = cg[:, :, 16 + TP:16 + TP + 128].rearrange("p c (b s) -> p c b s", b=4)
                sview_p = upb[:, :, 16 + TP:16 + TP + 128].rearrange("p c (b s) -> p c b s", b=4)
                for c in range(2):
                    P.dma("sp", sview_u[:, c, :, 0:16], shu_in[l, :, c, :, :], writes=[("bch", 2), "bc"])
                    P.dma("sp", sview_p[:, c, :, 0:16], shp_in[l, :, c, :, :], writes=[("uph", 2)])

            def view(buf, reg, c, sh, lo):
                kind, a, bnd = reg
                if kind == "p":
                    return buf[:, c, a - lo - sh:bnd - sh]
                v = buf[:, c, 16 + TP:16 + TP + 128].rearrange("p (b s) -> p b s", b=4)
                return v[:, :, 16 - lo - sh:32 - sh]

            rk = [("bc", t) for t in tilesR] + [("bch", 0), ("bch", 2), "bc"]
            uk = [("up", t) for t in tilesR] + [("uph", 0), ("uph", 2)]
            if ps_["name"] == "O2":
                P.dma("sp", o_pc[l, :, :, :], cg[:, :, 16 + TP - 16:16 + TP], reads=[("bc", 1), "bc"], writes=[("o_pc", l)])
                P.dma("sp", o_pp[l, :, :, :], upb[:, :, 16 + TP - 16:16 + TP], reads=[("up", 1)], writes=[("o_pp", l)])
                if has_s:
                    for c in range(2):
                        P.dma("sp", o_sc[l, :, c, :, :], cg[:, c, 16 + TP:16 + TP + 128].rearrange("p (b s) -> p b s", b=4)[:, :, 16:32],
                              reads=[("bc", 2), "bc"], writes=[("o_sc", l, c)])
                        P.dma("sp", o_sp[l, :, c, :, :], upb[:, c, 16 + TP:16 + TP + 128].rearrange("p (b s) -> p b s", b=4)[:, :, 16:32],
                              reads=[("up", 2)], writes=[("o_sp", l, c)])
            for reg in regs:
                for c in range(2):
                    cw = lambda i, c=c: smc(SM_CONV + (l * 3 + i) * 2 + c)
                    P.op("dve", lambda e, reg=reg, c=c, cw=cw: e.tensor_scalar(out=view(tmpA, reg, c, 0, 0), in0=view(cg, reg, c, 0, 0),
                                                                               scalar1=cw(2), scalar2=None, op0=ALU.mult),
                         reads=rk + ["small"], writes=["tmpA"])
                    for i, sh in ((1, 1), (0, 2)):
                        P.op("dve", lambda e, reg=reg, c=c, cw=cw, i=i, sh=sh: e.scalar_tensor_tensor(
                            out=view(tmpA, reg, c, 0, 0), in0=view(cg, reg, c, sh, 0), scalar=cw(i), in1=view(tmpA, reg, c, 0, 0),
                            op0=ALU.mult, op1=ALU.add), reads=rk + ["tmpA", "small"], writes=["tmpA"])
                    P.op("dve", lambda e, reg=reg, c=c: e.tensor_tensor(out=view(bg, reg, c, 0, 0), in0=view(bg, reg, c, 0, 0),
                                                                        in1=view(tmpA, reg, c, 0, 0), op=ALU.mult),
                         reads=rk + ["tmpA"], writes=[("bcb", 0), "bc"])
            for reg in regs:
                for c in range(2):
                    P.op("dve", lambda e, reg=reg, c=c: e.tensor_tensor(out=view(tmpA, reg, c, 0, 14), in0=view(upb, reg, c, 0, 14),
                                                                        in1=view(upb, reg, c, 1, 14), op=ALU.add),
                         reads=uk + [("bcb", 0)], writes=["tmpA"])
                    P.op("dve", lambda e, reg=reg, c=c: e.tensor_tensor(out=view(cg, reg, c, 0, 12), in0=view(tmpA, reg, c, 0, 12),
                                                                        in1=view(tmpA, reg, c, 2, 12), op=ALU.add),
                         reads=["tmpA", ("car_u", l), ("o_pc", l), ("o_sc", l, 0), ("o_sc", l, 1)], writes=["cgs", "bc"])
                for (p0, p1, sbuf_) in ((0, 64, tmpA), (64, 128, cg)):
                    P.op("dve", lambda e, reg=reg, p0=p0, p1=p1, sbuf_=sbuf_: e.scalar_tensor_tensor(
                        out=view(pooled, reg, 0, 0, 0)[p0:p1], in0=view(sbuf_, reg, 0, 0, 0)[p0:p1], scalar=smc(SM_INVW, 1, p0, p1),
                        in1=view(upb, reg, 0, 0, 0)[p0:p1], op0=ALU.mult, op1=ALU.subtract),
                        reads=["tmpA", "cgs", "small"] + uk, writes=["pooled"])
                P.op("dve", lambda e, reg=reg: e.tensor_tensor(out=view(tmpA, reg, 1, 0, 8), in0=view(cg, reg, 1, 0, 8),
                                                                in1=view(cg, reg, 1, 4, 8), op=ALU.add),
                     reads=["cgs", "pooled"], writes=["tmpA"])
                P.op("dve", lambda e, reg=reg: e.tensor_tensor(out=view(cg, reg, 1, 0, 0), in0=view(tmpA, reg, 1, 0, 0),
                                                                in1=view(tmpA, reg, 1, 8, 0), op=ALU.add),
                     reads=["tmpA"], writes=["cgs", "bc"])
                for (p0, p1, sbuf_) in ((0, 64, tmpA), (64, 128, cg)):
                    P.op("dve", lambda e, reg=reg, p0=p0, p1=p1, sbuf_=sbuf_: e.scalar_tensor_tensor(
                        out=view(pooled, reg, 1, 0, 0)[p0:p1], in0=view(sbuf_, reg, 1, 0, 0)[p0:p1], scalar=smc(SM_INVW + 1, 1, p0, p1),
                        in1=view(upb, reg, 1, 0, 0)[p0:p1], op0=ALU.mult, op1=ALU.subtract),
                        reads=["tmpA", "cgs", "small"] + uk, writes=["pooled"])
                if reg[0] == "p" and ps_["name"] == "O1" and reg[1] == 16 and not DBG.get("nocorr"):
                    for c, lo_buf, hi_buf in ((0, tmpA, cg), (1, tmpA, cg)):
                        for (p0, p1, sbuf_) in ((0, 64, lo_buf), (64, 128, hi_buf)):
                            P.op("dve", lambda e, c=c, p0=p0, p1=p1, sbuf_=sbuf_: e.tensor_tensor(
                                out=nrm1[p0:p1, 0:16], in0=sbuf_[p0:p1, c, 16:32], in1=small[p0:p1, SM_INVCNT + 16 * c:SM_INVCNT + 16 * c + 16], op=ALU.mult),
                                reads=["tmpA", "cgs", "small"], writes=["nrm1"])
                            P.op("dve", lambda e, c=c, p0=p0, p1=p1: e.tensor_tensor(
                                out=pooled[p0:p1, c, 16:32], in0=nrm1[p0:p1, 0:16], in1=upb[p0:p1, c, 16:32], op=ALU.subtract),
                                reads=["nrm1"] + uk, writes=["pooled"])
            for ti in tilesR:
                for c in range(2):
                    b = next_bank()
                    if ti < 2:
                        cc, n = bccol(ti)
                        rhs = pooled[:, c, cc:cc + n]
                        dst = upb[:, c, cc:cc + n]
                        src = psum[:, b, 0:n]
                    else:
                        n = 64
                        rhs = pooled[:, c, 16 + TP:16 + TP + 128].rearrange("p (b s) -> p b s", b=4)[:, :, 16:32]
                        dst = upb[:, c, 16 + TP:16 + TP + 128].rearrange("p (b s) -> p b s", b=4)[:, :, 16:32]
                        src = psum[:, b, 0:64].rearrange("p (b s) -> p b s", b=4)
                    P.op("pe", lambda e, c=c, b=b, rhs=rhs, n=n: e.matmul(psum[:, b, 0:n], lhsT=poolw[:, c, :], rhs=rhs, start=True, stop=True),
                         reads=["pooled", "poolw"], writes=[("ps", b)])
                    P.op("dve", lambda e, c=c, dst=dst, src=src: e.tensor_scalar(out=dst, in0=src, scalar1=smc(SM_PSCALE + l * 2 + c), scalar2=None, op0=ALU.mult),
                         reads=[("ps", b), "small", ("car_p", l), ("o_pp", l), ("o_sp", l, 0), ("o_sp", l, 1)], writes=[("up", ti)])
            for ti in tilesR:
                if ti < 2:
                    cc, n = bccol(ti)
                    headnorm_bc(l, bg, ("bcb", 0), 4, ti, cc, n, TILES[ti][0])
                    headnorm_bc(l, upb, ("up", ti), 6, ti, cc, n, TILES[ti][0])
                else:
                    for (src, key, ch0) in ((bg, ("bcb", 0), 4), (upb, ("up", 2), 6)):
                        for c in range(2):
                            P.op("dve", lambda e, src=src, c=c: e.tensor_copy(
                                out=tmpA[:, c, 0:64].rearrange("p (b s) -> p b s", b=4),
                                in_=src[:, c, 16 + TP:16 + TP + 128].rearrange("p (b s) -> p b s", b=4)[:, :, 16:32]),
                                reads=[key, "pooled"], writes=["tmpA"])
                        headnorm_bc(l, tmpA, "tmpA", ch0, 2, 0, 64, TP)

        def attn_epilogue(l, bn, bd, ncols, dst_cols, ti):
            w = 4 * ncols
            b = next_bank_misc()
            P.op("act", lambda e: e.activation(out=sqb[:, 1, 0, 0:w], in_=psum[:, bn, 0:w], func=AF.Square),
                 reads=[("ps", bn)], writes=[("sqb", 1)])
            P.op("act", lambda e: e.activation(out=nrm2[:, 0:w], in_=psum[:, bd, 0:w], func=AF.Square, scale=1e-3),
                 reads=[("ps", bd)], writes=["nrm2"])
            P.op("pe", lambda e: e.matmul(psum[:, b, 0:w], lhsT=cb(CB_BLK64), rhs=sqb[:, 1, 0, 0:w], start=True, stop=True),
                 reads=[("sqb", 1), "cbf"], writes=[("ps", b)])
            P.op("dve", lambda e: e.tensor_tensor(out=nrm1[:, 0:w], in0=psum[:, b, 0:w], in1=nrm2[:, 0:w], op=ALU.add),
                 reads=[("ps", b), "nrm2"], writes=["nrm1"])
            P.op("act", lambda e: e.activation(out=nrm2[:, 0:w], in_=nrm1[:, 0:w], func=AF.Sqrt),
                 reads=["nrm1"], writes=["nrm2"])
            P.op("dve", lambda e: e.reciprocal(out=nrm1[:, 0:w], in_=nrm2[:, 0:w]), reads=["nrm2"], writes=["nrm1"])
            for hp in range(4):
                P.op("dve", lambda e, hp=hp: e.scalar_tensor_tensor(
                    out=hT[:, hp, dst_cols:dst_cols + ncols], in0=psum[:, bn, hp * ncols:(hp + 1) * ncols],
                    scalar=smc(gcol(l, G_HEADS, hp)), in1=nrm1[:, hp * ncols:(hp + 1) * ncols], op0=ALU.mult, op1=ALU.mult),
                    reads=[("ps", bn), "nrm1", "small"], writes=[("hT", ti)])

        misc_rr = [0]

        def next_bank_misc():
            b = misc_rr[0]
            misc_rr[0] = (b + 1) % 4
            return b

        def attn_prompt(l, ps_, tilesR):
            pt = [t for t in tilesR if t < 2]
            if not pt:
                return
            P.dma("pool", biasraw[:, :].rearrange("p (j x) -> p j x", j=5),
                  bias_in[l].rearrange("(j p) h q -> p j (h q)", p=128), writes=["bias"])
            for (j, cbi) in ((0, CB_SM0), (4, CB_SM4)):
                for h in range(8):
                    P.op("dve", lambda e, j=j, h=h, cbi=cbi: e.tensor_tensor(out=bias[:, j, h, :], in0=bias[:, j, h, :], in1=cb(cbi), op=ALU.add),
                         reads=["bias", "cbf"], writes=["bias"])
            gpar = 0
            for ti in pt:
                for g4 in range(4):
                    gi = ti * 4 + g4
                    q0 = gi * 128
                    bn = 4 + 2 * gpar
                    bd = 5 + 2 * gpar
                    gpar ^= 1
                    for bz in (bn, bd):
                        P.op("pe", lambda e, bz=bz: e.matmul(psum[:, bz, :], lhsT=cb(CB_ZERO), rhs=cbf[:, 0:4, :].rearrange("p a n -> p (a n)"), start=True, stop=False),
                             reads=["cbf"], writes=[("ps", bz)])
                    def emit_S(jt, gi=gi, q0=q0, ti=ti):
                        kt = gi + jt
                        kreg = ("kT", 0) if kt < 4 else ("kT", 1 + (kt - 4) // 4)
                        masked = (ps_["name"] == "O1" and kt < 4)
                        pbuf = jt % 2
                        for hb in range(2):
                            b = next_bank_misc()
                            P.op("pe", lambda e, b=b, jt=jt, hb=hb: e.matmul(
                                psum[:, b, :], lhsT=cb(CB_IDENT), rhs=bias[:, jt, hb * 4:hb * 4 + 4, :].rearrange("p h q -> p (h q)"),
                                start=True, stop=False), reads=["bias", "cbf"], writes=[("ps", b)])
                            for h4 in range(4):
                                h = hb * 4 + h4
                                P.op("pe", lambda e, b=b, h=h, h4=h4, kt=kt, q0=q0: e.matmul(
                                    psum[:, b, h4 * 128:(h4 + 1) * 128], lhsT=kT[:, h // 2, kt * 128:(kt + 1) * 128],
                                    rhs=qT[:, h, q0:q0 + 128], start=False, stop=(h4 == 3)),
                                    reads=[kreg, ("qT", ti)], writes=[("ps", b)])
                            mcol = SM_HMASK if masked else SM_ZERO
                            P.op("act", lambda e, b=b, pbuf=pbuf, hb=hb, mcol=mcol: e.activation(
                                out=pT[:, pbuf, hb * 512:(hb + 1) * 512], in_=psum[:, b, :], func=AF.Exp, bias=smc(mcol), scale=1.0),
                                reads=[("ps", b), "small"], writes=[("pT", pbuf, hb)])

                    def emit_PV(jt, gi=gi, bn=bn, bd=bd):
                        kt = gi + jt
                        vreg = ("vtok", 0) if kt < 4 else ("vtok", 1 + (kt - 4) // 4)
                        pbuf = jt % 2
                        for h in range(8):
                            hp, r0 = h // 2, (h % 2) * 64
                            P.op("pe", lambda e, h=h, hp=hp, r0=r0, kt=kt, pbuf=pbuf, bn=bn, jt=jt: e.matmul(
                                psum[r0:r0 + 64, bn, hp * 128:(hp + 1) * 128], lhsT=vtok[:, kt, h * 64:(h + 1) * 64],
                                rhs=pT[:, pbuf, h * 128:(h + 1) * 128], start=False, stop=(jt == 4 and h >= 6)),
                                reads=[vreg, ("pT", pbuf, h // 4)], writes=[("ps", bn)])
                            P.op("pe", lambda e, h=h, hp=hp, r0=r0, pbuf=pbuf, bd=bd, jt=jt: e.matmul(
                                psum[r0:r0 + 64, bd, hp * 128:(hp + 1) * 128], lhsT=cb(CB_ONES, 0, 128, 0, 64),
                                rhs=pT[:, pbuf, h * 128:(h + 1) * 128], start=False, stop=(jt == 4 and h >= 6)),
                                reads=["cbf", ("pT", pbuf, h // 4)], writes=[("ps", bd)])

                    emit_S(0)
                    for jt in range(5):
                        if jt + 1 < 5:
                            emit_S(jt + 1)
                        emit_PV(jt)
                    attn_epilogue(l, bn, bd, 128, q0, ti)

        def attn_sample(l, ps_, tilesR):
            if 2 not in tilesR:
                return
            P.dma("pool", sbias[:, :, :, :].rearrange("p j h q -> p j (h q)"),
                  sbias_in[l].rearrange("(j p) h q -> p j (h q)", p=128), writes=["memk"])
            cs_i = 0
            for bq in range(4):
                for (src_c, dst_o) in ((ck_in, o_sk), (cv_in, o_sv)):
                    for r0 in range(0, 496, 128):
                        nr = min(128, 496 - r0)
                        sj = cs_i % 2
                        cs_i += 1
                        P.dma("sp", stg[0:nr, sj, :], src_c[l, bq, 16 + r0:16 + r0 + nr, :], writes=[("stg", sj)])
                        P.dma("sp", dst_o[l, bq, r0:r0 + nr, :], stg[0:nr, sj, :], reads=[("stg", sj)], writes=[("o_sc_rows", l, bq, r0, id(dst_o))])
                P.dma("pool", knat[:, :, :], ck_in[l, bq].rearrange("(k p) n -> p k n", p=128),
                      writes=[("pT", 0, 0), ("pT", 0, 1), ("pT", 1, 0), ("pT", 1, 1)])
                P.dma("pool", sV[:, :, :], cv_in[l, bq].rearrange("(k p) n -> p k n", p=128), writes=["bias"])
                for c in range(4):
                    b = next_bank_misc()
                    pv = psum[:, b, :].bitcast(BF16)
                    for k4 in range(4):
                        P.op("pe", lambda e, c=c, k4=k4, pv=pv: e.matmul(pv[:, k4 * 128:(k4 + 1) * 128], lhsT=knat[:, k4, c * 128:(c + 1) * 128],
                                                                         rhs=cb(CB_IDENT), start=True, stop=True, is_transpose=True),
                             reads=[("pT", 0, 0), ("pT", 0, 1), ("pT", 1, 0), ("pT", 1, 1), "cbf"], writes=[("ps", b)])
                    P.op("dve", lambda e, c=c, pv=pv: e.tensor_copy(out=skT[:, c, 0:512], in_=pv[:, 0:512]),
                         reads=[("ps", b)], writes=["bias"])
                bn, bd = 4, 5
                for bz in (bn, bd):
                    P.op("pe", lambda e, bz=bz: e.matmul(psum[:, bz, 0:64], lhsT=cb(CB_ZERO), rhs=cbf[:, 0, 0:64], start=True, stop=False),
                         reads=["cbf"], writes=[("ps", bz)])
                qc = TP + 16 * bq
                for kt in range(5):
                    b = next_bank_misc()
                    np_ = 128 if kt < 4 else 16
                    P.op("pe", lambda e, b=b, kt=kt, np_=np_: e.matmul(
                        psum[0:np_, b, 0:128], lhsT=cb(CB_IDENT, 0, np_, 0, np_), rhs=sbias[0:np_, kt, :, :].rearrange("p h q -> p (h q)"),
                        start=True, stop=False), reads=["memk", "cbf"], writes=[("ps", b)])
                    for h in range(8):
                        r0 = (h % 2) * 64
                        if kt < 4:
                            lhs = skT[:, h // 2, kt * 128:(kt + 1) * 128]
                        else:
                            lhs = kT[:, h // 2, 512 + qc:512 + qc + 16]
                        P.op("pe", lambda e, b=b, h=h, r0=r0, lhs=lhs, np_=np_, qc=qc: e.matmul(
                            psum[0:np_, b, h * 16:(h + 1) * 16], lhsT=lhs, rhs=qT[:, h, qc:qc + 16],
                            start=False, stop=(h == 7)), reads=["bias", ("kT", 3), ("qT", 2)], writes=[("ps", b)])
                    P.op("act", lambda e, b=b, kt=kt, np_=np_: e.activation(
                        out=spT[0:np_, kt, :, :].rearrange("p h q -> p (h q)"), in_=psum[0:np_, b, 0:128], func=AF.Exp),
                        reads=[("ps", b)], writes=[("sqb", 0)])
                    for h in range(8):
                        hp, r0 = h // 2, (h % 2) * 64
                        if kt < 4:
                            lv = sV[:, kt, h * 64:(h + 1) * 64]
                        else:
                            lv = svnew[0:16, bq, h * 64:(h + 1) * 64]
                        P.op("pe", lambda e, h=h, hp=hp, r0=r0, lv=lv, kt=kt, np_=np_: e.matmul(
                            psum[r0:r0 + 64, bn, hp * 16:(hp + 1) * 16], lhsT=lv, rhs=spT[0:np_, kt, h, :], start=False, stop=(kt == 4 and h >= 6)),
                            reads=["bias", "svnew", ("sqb", 0)], writes=[("ps", bn)])
                        P.op("pe", lambda e, h=h, hp=hp, r0=r0, kt=kt, np_=np_: e.matmul(
                            psum[r0:r0 + 64, bd, hp * 16:(hp + 1) * 16], lhsT=cb(CB_ONES, 0, np_, 0, 64), rhs=spT[0:np_, kt, h, :],
                            start=False, stop=(kt == 4 and h >= 6)), reads=["cbf", ("sqb", 0)], writes=[("ps", bd)])
                attn_epilogue(l, bn, bd, 16, qc, 2)

        def xattn(l, ps_, tilesR):
            if not tilesR:
                return
            pt = [t for t in tilesR if t < 2]
            has_s = 2 in tilesR
            xsub = DBG.get("xsub")
            xon = lambda nm: xsub is None or nm in xsub
            rmsnorm(tilesR, gcol(l, G_XATTN, 0))

            def evac_qx(m, ti, c0, n, b):
                P.op("act", lambda e: e.activation(out=actT[:, m, c0:c0 + n], in_=psum[:, b, 0:n], func=AF.Copy, scale=1.0 / 16.0),
                     reads=[("ps", b)], writes=[("actT", ti), "bc"])
            linear_fm(lambda mg: w_xq[l][:, mg * 512:(mg + 1) * 512], hT, "hT", 8, 8, tilesR, evac_qx)
            if not xon("mem"):
                return

            def attend(c0, n, ti, mk, mv, mkey):
                def s_part(h):
                    ph = h % 2
                    bs = [next_bank() for _ in range(2)]
                    for mt in range(2):
                        for dc in range(2):
                            P.op("pe", lambda e, mt=mt, dc=dc, h=h, b=bs[mt]: e.matmul(
                                psum[:, b, 0:n], lhsT=mk[:, h * 2 + dc, mt * 128:(mt + 1) * 128], rhs=actT[:, h * 2 + dc, c0:c0 + n],
                                start=(dc == 0), stop=(dc == 1)), reads=[mkey, ("actT", ti), "bc"], writes=[("ps", bs[mt])])
                        P.op("act", lambda e, mt=mt, b=bs[mt], ph=ph: e.activation(out=pT[:, mt, ph * 512:ph * 512 + n], in_=psum[:, b, 0:n], func=AF.Exp),
                             reads=[("ps", bs[mt])], writes=[("pT", mt, ph)])

                def pv_part(h):
                    ph = h % 2
                    bd = next_bank()
                    bnum = [next_bank() for _ in range(2)]
                    for mt in range(2):
                        P.op("pe", lambda e, mt=mt, bd=bd, ph=ph: e.matmul(psum[:, bd, 0:n], lhsT=cb(CB_ONES), rhs=pT[:, mt, ph * 512:ph * 512 + n],
                                                                          start=(mt == 0), stop=(mt == 1)),
                             reads=["cbf", ("pT", mt, ph)], writes=[("ps", bd)])
                    for dc in range(2):
                        for mt in range(2):
                            P.op("pe", lambda e, mt=mt, dc=dc, h=h, bq_=bnum[dc], ph=ph: e.matmul(
                                psum[:, bq_, 0:n], lhsT=mv[:, mt, (h * 2 + dc) * 128:(h * 2 + dc + 1) * 128], rhs=pT[:, mt, ph * 512:ph * 512 + n],
                                start=(mt == 0), stop=(mt == 1)), reads=[mkey, ("pT", mt, ph)], writes=[("ps", bnum[dc])])
                    P.op("dve", lambda e, bd=bd: e.reciprocal(out=nrm1[:, 0:n], in_=psum[:, bd, 0:n]), reads=[("ps", bd)], writes=["nrm1"])
                    for dc in range(2):
                        P.op("dve", lambda e, dc=dc, h=h, bq_=bnum[dc]: e.tensor_tensor(out=hT[:, h * 2 + dc, c0:c0 + n], in0=psum[:, bq_, 0:n],
                                                                                      in1=nrm1[:, 0:n], op=ALU.mult),
                             reads=[("ps", bnum[dc]), "nrm1"], writes=[("hT", ti)])

                s_part(0)
                for h in range(4):
                    if h + 1 < 4:
                        s_part(h + 1)
                    pv_part(h)

            if pt:
                memf = tmpA[:, :, :].rearrange("p c n -> p (c n)")[:, 0:2048].rearrange("p (c n) -> p c n", c=8)
                P.dma("sp", memf, memT_in.rearrange("(c p) n -> p c n", p=128), writes=["tmpA"])
                b = next_bank()
                for qd in range(4):
                    half = qd % 2
                    P.op("act", lambda e, half=half, qd=qd: e.activation(out=sqb[:, half, :, 0:256], in_=memf[:, 2 * qd:2 * qd + 2, :], func=AF.Square),
                         reads=["tmpA"], writes=[("sqb", half)])
                    for k2 in range(2):
                        k = 2 * qd + k2
                        P.op("pe", lambda e, half=half, k2=k2, k=k, b=b: e.matmul(psum[:, b, 0:256], lhsT=cb(CB_MEAN), rhs=sqb[:, half, k2, 0:256],
                                                                           start=(k == 0), stop=(k == 7)),
                             reads=[("sqb", half), "cbf"], writes=[("ps", b)])
                P.op("act", lambda e, b=b: e.activation(out=nrm1[:, 0:256], in_=psum[:, b, 0:256], func=AF.Sqrt, bias=smc(SM_EPS), scale=1.0),
                     reads=[("ps", b), "small"], writes=["nrm1"])
                P.op("dve", lambda e: e.reciprocal(out=nrm2[:, 0:256], in_=nrm1[:, 0:256]), reads=["nrm1"], writes=["nrm2"])
                mnT = pooled[:, :, :].rearrange("p c n -> p (c n)")[:, 0:2048].rearrange("p (c n) -> p c n", c=8)
                for k in range(8):
                    P.op("dve", lambda e, k=k: e.scalar_tensor_tensor(out=mnT[:, k, :], in0=memf[:, k, :], scalar=smc(gcol(l, G_MEM, k)),
                                                                      in1=nrm2[:, 0:256], op0=ALU.mult, op1=ALU.mult),
                         reads=["tmpA", "nrm2", "small"], writes=["pooled"])
                write_out = ps_["name"] == "O1"
                for mg in range(2):
                    s, wt = load_w(w_xk[l][:, mg * 512:(mg + 1) * 512], 8, 512)
                    for mi in range(4):
                        m = mg * 4 + mi
                        b = next_bank()
                        for k in range(8):
                            P.op("pe", lambda e, k=k, mi=mi, wt=wt, b=b: e.matmul(psum[:, b, 0:256], lhsT=wt[:, k, mi * 128:(mi + 1) * 128], rhs=mnT[:, k, :],
                                                                                  start=(k == 0), stop=(k == 7)),
                                 reads=[("ring", s), "pooled"], writes=[("ps", b)])
                        P.op("act", lambda e, m=m, b=b: e.activation(out=memk[:, m, :], in_=psum[:, b, 0:256], func=AF.Copy),
                             reads=[("ps", b)], writes=["memk"])
                        if write_out:
                            sj = m % 2
                            P.op("dve", lambda e, b=b, sj=sj: e.tensor_copy(out=stg[:, sj, 0:256], in_=psum[:, b, 0:256]),
                                 reads=[("ps", b)], writes=[("stg", sj), ("ps", b)])
                            P.dma("sp", o_mkT[l, m * 128:(m + 1) * 128, :], stg[:, sj, 0:256], reads=[("stg", sj)], writes=[("o_mkT", l, m)])
                for mg in range(2):
                    s, wt = load_w(w_xv[l][:, mg * 512:(mg + 1) * 512], 8, 512)
                    for mt in range(2):
                        b = next_bank()
                        for k in range(8):
                            P.op("pe", lambda e, k=k, mt=mt, wt=wt, b=b: e.matmul(psum[:, b, :], lhsT=mnT[:, k, mt * 128:(mt + 1) * 128], rhs=wt[:, k, :],
                                                                                  start=(k == 0), stop=(k == 7)),
                                 reads=[("ring", s), "pooled"], writes=[("ps", b)])
                        P.op("act", lambda e, mt=mt, mg=mg, b=b: e.activation(out=memv[:, mt, mg * 512:(mg + 1) * 512], in_=psum[:, b, :], func=AF.Copy),
                             reads=[("ps", b)], writes=["memk"])
                        if write_out:
                            sj = mt
                            P.op("dve", lambda e, b=b, sj=sj: e.tensor_copy(out=stg[:, sj, :], in_=psum[:, b, :]),
                                 reads=[("ps", b)], writes=[("stg", sj), ("ps", b)])
                            P.dma("sp", o_mv[l, mt * 128:(mt + 1) * 128, mg * 512:(mg + 1) * 512], stg[:, sj, :], reads=[("stg", sj)],
                                  writes=[("o_mv", l, mt, mg)])
                for ti in pt:
                    c0, n = TILES[ti]
                    if xon("att"):
                        attend(c0, n, ti, memk, memv, "memk")
            if has_s:
                for bq in range(4):
                    P.dma("pool", memv[:, :, :], cmv_in[l, bq].rearrange("(k p) n -> p k n", p=128), writes=["memk"])
                    mkn = vtok[:, 0:4, :].rearrange("p a n -> p (a n)").rearrange("p (k n) -> p k n", k=2)
                    P.dma("pool", mkn, cmk_in[l, bq].rearrange("(k p) n -> p k n", p=128), writes=[("vtok", 0)])
                    for c in range(8):
                        b = next_bank()
                        pv = psum[:, b, :].bitcast(BF16)
                        for mt in range(2):
                            P.op("pe", lambda e, c=c, mt=mt, pv=pv: e.matmul(pv[:, mt * 128:(mt + 1) * 128], lhsT=mkn[:, mt, c * 128:(c + 1) * 128],
                                                                             rhs=cb(CB_IDENT), start=True, stop=True, is_transpose=True),
                                 reads=[("vtok", 0), "cbf"], writes=[("ps", b)])
                        P.op("dve", lambda e, c=c, pv=pv: e.tensor_copy(out=memk[:, c, :], in_=pv[:, 0:256]), reads=[("ps", b)], writes=["memk"])
                    attend(TP + 16 * bq, 16, 2, memk, memv, "memk")
            if xon("xo"):
                linear_fm(lambda mg: w_xo[l][:, mg * 512:(mg + 1) * 512], hT, "hT", 8, 8, tilesR, add_to_x(1.0))

        def final_norm(ps_, tiles):
            pi = ps_["xidx"]
            if pi < 2:
                return

            def out_fn(ti, c0, n, tb, tkey):
                for k in range(8):
                    sj = k % 2
                    P.op("dve", lambda e, k=k, sj=sj: e.scalar_tensor_tensor(out=stg[:, sj, 0:n], in0=xT[:, k, c0:c0 + n], scalar=smc(SM_GFINAL + k),
                                                                            in1=tb[:, 0:n], op0=ALU.mult, op1=ALU.mult),
                         reads=[("xT", ti), tkey, "small"], writes=[("stg", sj)])
                    oc = (pi - 2) * TP + c0 if ti < 2 else SEG
                    P.dma("sp", o_yT[k * 128:(k + 1) * 128, oc:oc + n], stg[:, sj, 0:n], reads=[("stg", sj)], writes=[("o_y", pi, ti, k)])
            rmsnorm(tiles, 0, out_fn)

        ALLT = [0, 1]
        passes = [
            dict(name="H1", xidx=0, A=[[0, 1], [1], [], []], R=[[1], [], [], []]),
            dict(name="H2", xidx=1, A=[[0, 1], [0, 1], [0, 1], [1]], R=[[0, 1], [0, 1], [1], []]),
            dict(name="O1", xidx=2, A=[ALLT] * 4, R=[ALLT] * 4),
            dict(name="O2", xidx=3, A=[[0, 1, 2]] * 4, R=[[0, 1, 2]] * 4),
        ]
        for ps_ in passes:
            if DBG["passes"] is not None and ps_["name"] not in DBG["passes"]:
                continue
            st = DBG["stages"]
            on = lambda name: st is None or name in st
            allk = [("xT", 0), ("xT", 1)]
            P.dma("sp", xT[:, :, 0:TP], xT_in[ps_["xidx"]].rearrange("(c p) n -> p c n", p=128), writes=allk)
            if ps_["name"] == "O2":
                P.dma("sp", xT[:, :, TP:TOK], xsT_in.rearrange("(c p) n -> p c n", p=128), writes=[("xT", 2)])
            for l in range(L):
                A = ps_["A"][l]
                R = ps_["R"][l]
                if not A:
                    continue
                if on("ffn1"):
                    ffn(l, w_g1, w_u1, w_d1, G_FFN1, A)
                if on("win"):
                    rmsnorm(A, gcol(l, G_MIX, 0))
                    if DBG.get("sub") is None or "cin" in DBG["sub"]:
                        carry_in(l, ps_, R)
                    w_in_stage(l, ps_, A)
                    if DBG.get("sub") is None or "cout" in DBG["sub"]:
                        carry_out(l, ps_, A)
                if not R:
                    continue
                P.dma("pool", poolw[:, :, :], poolw_in[l], writes=["poolw"])
                if on("attn"):
                    attn_prompt(l, ps_, R)
                if on("sattn"):
                    attn_sample(l, ps_, R)
                if on("bc"):
                    bc_stage(l, ps_, R)
                if on("wout"):
                    linear_fm(lambda mg: w_out[l][:, mg * 512:(mg + 1) * 512], hT, "hT", 8, 8, R, add_to_x(1.0))
                if on("xattn"):
                    xattn(l, ps_, R)
                if on("ffn2"):
                    ffn(l, w_g2, w_u2, w_d2, G_FFN2, R)
            final_norm(ps_, ps_["A"][L - 1])
        P.final_waits()
        P.emit(block)
    return nc


def _fm(v):
    return np.ascontiguousarray(v.reshape(8, 128).T)


def kernel(x_prompt, x_sample, mem_prompt, cache_attn_k, cache_attn_v, state_conv, state_pool,
           cache_mem_k, cache_mem_v, g_ffn1, w_ffn1_gate, w_ffn1_up, w_ffn1_down, g_mix, w_in,
           rel_bias, conv_w, pool_w, pool_scale, g_heads, w_out, g_xattn, g_mem, w_xq, w_xk, w_xv,
           w_xo, g_ffn2, w_ffn2_gate, w_ffn2_up, w_ffn2_down, g_final):
    f32 = np.float32
    A = lambda a: np.ascontiguousarray(np.asarray(a, dtype=f32))
    x_prompt, x_sample, mem_prompt = A(x_prompt), A(x_sample), A(mem_prompt)
    cache_attn_k, cache_attn_v = A(cache_attn_k), A(cache_attn_v)
    cache_mem_k, cache_mem_v = A(cache_mem_k), A(cache_mem_v)
    state_conv, state_pool = A(state_conv), A(state_pool)
    rel_bias = A(rel_bias)

    cbf = np.zeros((128, 7, 128), f32)
    cbf[:, CB_IDENT] = np.eye(128, dtype=f32)
    cbf[0:64, CB_BLK64, 0:64] = 1.0 / 64
    cbf[64:128, CB_BLK64, 64:128] = 1.0 / 64
    cbf[:, CB_MEAN] = 1.0 / 1024
    cbf[:, CB_ONES] = 1.0
    cbf[0:64, CB_SM0, 64:128] = NEGM
    cbf[64:128, CB_SM4, 0:64] = NEGM
    cbf = cbf.reshape(128, 7 * 128)

    kj = np.arange(640)[:, None]
    idx_p = np.clip(512 + np.arange(128)[None, :] - kj, -128, 128) + 128
    bias_t = np.ascontiguousarray(rel_bias[:, :, idx_p].transpose(0, 2, 1, 3))
    idx_s = np.clip(512 + np.arange(16)[None, :] - kj, -128, 128) + 128
    sbias_t = np.ascontiguousarray(rel_bias[:, :, idx_s].transpose(0, 2, 1, 3))

    pw = np.zeros((L, 128, 2, 128), f32)
    for c in range(2):
        for g in range(2):
            pw[:, g * 64:(g + 1) * 64, c, g * 64:(g + 1) * 64] = np.asarray(pool_w, f32)[:, 2 * c + g]

    gl = [g_ffn1, g_mix, g_heads, g_xattn, g_mem, g_ffn2]
    windows = [2, 4, 8, 16]
    in_maps = []
    shared = dict(
        cbf_in=cbf, bias_in=bias_t, sbias_in=sbias_t, poolw_in=pw,
        w_g1=A(w_ffn1_gate), w_u1=A(w_ffn1_up), w_d1=A(w_ffn1_down),
        w_g2=A(w_ffn2_gate), w_u2=A(w_ffn2_up), w_d2=A(w_ffn2_down),
        w_in=A(w_in), w_out=A(w_out), w_xq=A(w_xq), w_xk=A(w_xk), w_xv=A(w_xv), w_xo=A(w_xo),
    )
    for c in range(NCORE):
        sq, seg = c // 4, c % 4
        S = seg * SEG
        small = np.zeros((128, SM_N), f32)
        for l in range(L):
            for wi, g in enumerate(gl):
                small[:, gcol(l, wi, 0):gcol(l, wi, 0) + 8] = _fm(np.asarray(g, f32)[l])
            for i in range(3):
                cwv = np.asarray(conv_w, f32)[l, i].reshape(2, 128).T
                small[:, SM_CONV + (l * 3 + i) * 2:SM_CONV + (l * 3 + i) * 2 + 2] = cwv
            small[:, SM_PSCALE + l * 2:SM_PSCALE + l * 2 + 2] = np.asarray(pool_scale, f32)[l].reshape(2, 128).T
        small[:, SM_GFINAL:SM_GFINAL + 8] = _fm(np.asarray(g_final, f32))
        for ch in range(2):
            for g in range(2):
                w = windows[2 * ch + g]
                small[g * 64:(g + 1) * 64, SM_INVW + ch] = 1.0 / w
                for pos in range(16):
                    cnt = min(w, pos + 1) if seg == 0 else w
                    small[g * 64:(g + 1) * 64, SM_INVCNT + ch * 16 + pos] = 1.0 / cnt
        small[:, SM_HMASK] = NEGM if seg == 0 else 0.0
        small[:, SM_TFLAG] = 0.0 if seg == 0 else 1.0
        small[:, SM_EPS] = EPS
        small[:, SM_ZERO] = 0.0

        xT = np.zeros((4, D, TP), f32)
        for p in range(4):
            t0 = S - 2048 + p * TP
            if t0 >= 0:
                xT[p] = x_prompt[sq, t0:t0 + TP, :].T
        bs = slice(4 * c, 4 * c + 4)
        xs = x_sample[bs].reshape(NS, D)
        shu = np.zeros((L, 128, 2, 4, 16), f32)
        shp = np.zeros((L, 128, 2, 4, 16), f32)
        sc = state_conv[:, bs]
        sp_ = state_pool[:, bs]
        shu[:, :, :, :, 14:16] = sc.reshape(L, 4, 2, 2, 128).transpose(0, 4, 3, 1, 2)
        shp[:, :, :, :, 1:16] = sp_.reshape(L, 4, 15, 2, 128).transpose(0, 4, 3, 1, 2)
        m = dict(shared)
        m.update(
            xT_in=xT, xsT_in=np.ascontiguousarray(xs.T), memT_in=np.ascontiguousarray(mem_prompt[sq].T),
            small_in=small, shu_in=shu, shp_in=shp,
            ck_in=np.ascontiguousarray(cache_attn_k[:, bs].reshape(L, 4, 512, 512)),
            cv_in=np.ascontiguousarray(cache_attn_v[:, bs].reshape(L, 4, 512, 512)),
            cmk_in=np.ascontiguousarray(cache_mem_k[:, bs].reshape(L, 4, 256, 1024)),
            cmv_in=np.ascontiguousarray(cache_mem_v[:, bs].reshape(L, 4, 256, 1024)),
        )
        in_maps.append(m)

    nc = build_program()
    res = run_bass_kernel_spmd(nc, in_maps, core_ids=list(range(NCORE)))
    R = res.results

    y_prompt = np.zeros((2, 8192, D), f32)
    y_sample = np.zeros((32, 16, D), f32)
    nk_p = np.zeros((L, 2, 512, 8, 64), f32); nv_p = np.zeros_like(nk_p)
    nc_p = np.zeros((L, 2, 2, 256), f32); np_p = np.zeros((L, 2, 15, 256), f32)
    mk_p = np.zeros((L, 2, 256, 4, 256), f32); mv_p = np.zeros_like(mk_p)
    nk_s = np.zeros((L, 32, 512, 8, 64), f32); nv_s = np.zeros_like(nk_s)
    nc_s = np.zeros((L, 32, 2, 256), f32); np_s = np.zeros((L, 32, 15, 256), f32)
    for c in range(NCORE):
        sq, seg = c // 4, c % 4
        r = R[c]
        yT = np.asarray(r["o_yT"])
        y_prompt[sq, seg * SEG:(seg + 1) * SEG] = yT[:, 0:SEG].T
        y_sample[4 * c:4 * c + 4] = yT[:, SEG:SEG + NS].T.reshape(4, 16, D)
        if seg == 3:
            nk_p[:, sq] = np.asarray(r["o_pk"]).reshape(L, 512, 8, 64)
            nv_p[:, sq] = np.asarray(r["o_pv"]).reshape(L, 512, 8, 64)
            pc = np.asarray(r["o_pc"])
            pp = np.asarray(r["o_pp"])
            nc_p[:, sq] = pc[:, :, :, 14:16].transpose(0, 3, 2, 1).reshape(L, 2, 256)
            np_p[:, sq] = pp[:, :, :, 1:16].transpose(0, 3, 2, 1).reshape(L, 15, 256)
        if seg == 0:
            mk_p[:, sq] = np.asarray(r["o_mkT"]).transpose(0, 2, 1).reshape(L, 256, 4, 256)
            mv_p[:, sq] = np.asarray(r["o_mv"]).reshape(L, 256, 4, 256)
        nk_s[:, 4 * c:4 * c + 4] = np.asarray(r["o_sk"]).reshape(L, 4, 512, 8, 64)
        nv_s[:, 4 * c:4 * c + 4] = np.asarray(r["o_sv"]).reshape(L, 4, 512, 8, 64)
        scc = np.asarray(r["o_sc"])
        spp = np.asarray(r["o_sp"])
        nc_s[:, 4 * c:4 * c + 4] = scc[:, :, :, :, 14:16].transpose(0, 3, 4, 2, 1).reshape(L, 4, 2, 256)
        np_s[:, 4 * c:4 * c + 4] = spp[:, :, :, :, 1:16].transpose(0, 3, 4, 2, 1).reshape(L, 4, 15, 256)
    return (y_prompt, y_sample, nk_p, nv_p, nc_p, np_p, mk_p, mv_p, nk_s, nv_s, nc_s, np_s)
```

```python
import numpy as np
import concourse.bass as bass
import concourse.mybir as mybir
from concourse.bass_utils import run_bass_kernel_spmd

F32 = mybir.dt.float32
BF16 = mybir.dt.bfloat16
AF = mybir.ActivationFunctionType
ALU = mybir.AluOpType

L = 4
D = 1024
DFF = 2816
NCORE = 8
SEG = 2048
TP = 1024
NS = 64
TOK = TP + NS
WBC = 16 + TP + 128
KTW = 512 + TOK
EPS = 1e-6
NEGM = -30000.0
TILES = [(0, 512), (512, 512), (1024, 64)]
DBG = dict(passes=None, stages=None)

G_FFN1, G_MIX, G_HEADS, G_XATTN, G_MEM, G_FFN2 = range(6)
def gcol(l, which, c): return (l * 6 + which) * 8 + c
SM_GFINAL = L * 6 * 8
SM_CONV = SM_GFINAL + 8
SM_PSCALE = SM_CONV + L * 3 * 2
SM_INVW = SM_PSCALE + L * 2
SM_INVCNT = SM_INVW + 2
SM_HMASK = SM_INVCNT + 32
SM_TFLAG = SM_HMASK + 1
SM_EPS = SM_TFLAG + 1
SM_ZERO = SM_EPS + 1
SM_N = SM_ZERO + 1
CB_IDENT, CB_BLK64, CB_MEAN, CB_ONES, CB_ZERO, CB_SM0, CB_SM4 = range(7)


class Prog:
    ENG = ["pe", "act", "dve", "pool", "sp"]

    def __init__(self, nc, sems, dma_sems):
        self.nc = nc
        self.ops = {e: [] for e in self.ENG}
        self.cnt = {e: 0 for e in self.ENG}
        self.sem = sems
        self.dma_sems = dma_sems
        self.dma_cnt = {q: [0] * len(dma_sems[q]) for q in dma_sems}
        self.dma_rr = {q: 0 for q in dma_sems}
        self.waited = {e: {} for e in self.ENG}
        self.last_w = {}
        self.last_r = {}

    def _deps(self, eng, reads, writes):
        toks = []
        for r in reads:
            t = self.last_w.get(r)
            if t is not None:
                toks.append(t)
        for w in writes:
            t = self.last_w.get(w)
            if t is not None:
                toks.append(t)
            toks.extend(self.last_r.get(w, []))
        need = {}
        for (key, semh, val, src) in toks:
            if src == eng and eng == "pe":
                continue
            if self.waited[eng].get(key, 0) >= val:
                continue
            if need.get(key, (None, 0))[1] < val:
                need[key] = (semh, val)
        for key, (semh, val) in need.items():
            self.waited[eng][key] = val
        return list(need.values())

    def _record(self, tok, reads, writes):
        for w in writes:
            self.last_w[w] = tok
            self.last_r[w] = []
        for r in reads:
            self.last_r.setdefault(r, []).append(tok)

    def op(self, eng, fn, reads=(), writes=()):
        waits = self._deps(eng, reads, writes)
        self.cnt[eng] += 1
        tok = (eng, self.sem[eng], self.cnt[eng], eng)
        self.ops[eng].append((waits, fn, (self.sem[eng], 1)))
        self._record(tok, reads, writes)

    def dma(self, q, out, in_, reads=(), writes=()):
        i = self.dma_rr[q]
        self.dma_rr[q] = (i + 1) % len(self.dma_sems[q])
        semh = self.dma_sems[q][i]
        key = (q, i)
        waits = self._deps(q, reads, writes)
        prev = self.dma_cnt[q][i]
        if prev > 0 and self.waited[q].get(key, 0) < prev:
            waits.append((semh, prev))
            self.waited[q][key] = prev
        self.dma_cnt[q][i] = prev + 16
        tok = (key, semh, prev + 16, None)
        self.ops[q].append((waits, lambda e, o=out, s=in_: e.dma_start(out=o, in_=s), (semh, 16)))
        self._record(tok, reads, writes)

    def barrier(self):
        for e in self.ENG:
            waits = []
            for o in self.ENG:
                if o != e and self.cnt[o] > 0 and self.waited[e].get(o, 0) < self.cnt[o]:
                    waits.append((self.sem[o], self.cnt[o]))
                    self.waited[e][o] = self.cnt[o]
            for q in self.dma_sems:
                for i, semh in enumerate(self.dma_sems[q]):
                    c = self.dma_cnt[q][i]
                    if c > 0 and self.waited[e].get((q, i), 0) < c:
                        waits.append((semh, c))
                        self.waited[e][(q, i)] = c
            if waits:
                self.ops[e].append((waits, None, None))

    def final_waits(self):
        waits = []
        for q in self.dma_sems:
            for i, semh in enumerate(self.dma_sems[q]):
                if self.dma_cnt[q][i] > 0:
                    waits.append((semh, self.dma_cnt[q][i]))
        for e in self.ENG:
            if self.cnt[e] > 0 and e != "sp":
                waits.append((self.sem[e], self.cnt[e]))
        self.ops["sp"].append((waits, None, None))

    def emit(self, block):
        handles = {"pe": block.tensor, "act": block.scalar, "dve": block.vector,
                   "pool": block.gpsimd, "sp": block.sync}
        for eng in self.ENG:
            ops = self.ops[eng]
            if not ops:
                continue

            def body(e, ops=ops):
                for waits, fn, inc in ops:
                    for semh, val in waits:
                        e.wait_ge(semh, val)
                    if fn is not None:
                        ins = fn(e)
                        ins.then_inc(inc[0], inc[1])
            handles[eng](body)


def build_program():
    nc = bass.Bass("TRN2", target_bir_lowering=False)
    dt = nc.dram_tensor

    def din(name, shape, dtype=F32):
        return dt(name, list(shape), dtype, kind="ExternalInput").ap()

    def dout(name, shape, dtype=F32):
        return dt(name, list(shape), dtype, kind="ExternalOutput").ap()

    xT_in = din("xT_in", [4, D, TP])
    xsT_in = din("xsT_in", [D, NS])
    memT_in = din("memT_in", [D, 256])
    small_in = din("small_in", [128, SM_N])
    cbf_in = din("cbf_in", [128, 7 * 128])
    bias_in = din("bias_in", [L, 640, 8, 128])
    sbias_in = din("sbias_in", [L, 640, 8, 16])
    poolw_in = din("poolw_in", [L, 128, 2, 128])
    shu_in = din("shu_in", [L, 128, 2, 4, 16])
    shp_in = din("shp_in", [L, 128, 2, 4, 16])
    ck_in = din("ck_in", [L, 4, 512, 512])
    cv_in = din("cv_in", [L, 4, 512, 512])
    cmk_in = din("cmk_in", [L, 4, 256, 1024])
    cmv_in = din("cmv_in", [L, 4, 256, 1024])
    w_g1 = din("w_g1", [L, D, DFF]); w_u1 = din("w_u1", [L, D, DFF]); w_d1 = din("w_d1", [L, DFF, D])
    w_g2 = din("w_g2", [L, D, DFF]); w_u2 = din("w_u2", [L, D, DFF]); w_d2 = din("w_d2", [L, DFF, D])
    w_in = din("w_in", [L, D, 2560]); w_out = din("w_out", [L, D, D])
    w_xq = din("w_xq", [L, D, D]); w_xk = din("w_xk", [L, D, D]); w_xv = din("w_xv", [L, D, D]); w_xo = din("w_xo", [L, D, D])

    o_yT = dout("o_yT", [D, SEG + NS])
    o_pk = dout("o_pk", [L, 512, 512]); o_pv = dout("o_pv", [L, 512, 512])
    o_pc = dout("o_pc", [L, 128, 2, 16]); o_pp = dout("o_pp", [L, 128, 2, 16])
    o_mkT = dout("o_mkT", [L, D, 256]); o_mv = dout("o_mv", [L, 256, D])
    o_sk = dout("o_sk", [L, 4, 512, 512]); o_sv = dout("o_sv", [L, 4, 512, 512])
    o_sc = dout("o_sc", [L, 128, 2, 4, 16]); o_sp = dout("o_sp", [L, 128, 2, 4, 16])

    car_k = [dt(f"car_k{l}", [128, 4, 512], BF16).ap() for l in range(L)]
    car_v = [dt(f"car_v{l}", [128, 4, 512], BF16).ap() for l in range(L)]
    car_u = [dt(f"car_u{l}", [128, 2, 16], F32).ap() for l in range(L)]
    car_p = [dt(f"car_p{l}", [128, 2, 16], F32).ap() for l in range(L)]

    import contextlib
    with contextlib.ExitStack() as es:
        def sb(name, shape, dtype):
            return es.enter_context(nc.sbuf_tensor(name, list(shape), dtype))

        xT = sb("xT", [128, 8, TOK], F32)
        hT = sb("hT", [128, 8, TOK], BF16)
        bcraw = sb("bcraw", [128, 2 * 2 * WBC * 2], BF16)
        upb = sb("upb", [128, 2, WBC], F32)
        tmpA = sb("tmpA", [128, 2, WBC], F32)
        pooled = sb("pooled", [128, 2, WBC], BF16)
        qT = sb("qT", [128, 8, TOK], BF16)
        kT = sb("kT", [128, 4, KTW], BF16)
        vtok = sb("vtok", [128, 12, 512], BF16)
        biasraw = sb("biasraw", [128, 5 * 8 * 128], BF16)
        pT = sb("pT", [128, 2, 1024], BF16)
        memk = sb("memk", [128, 8, 256], BF16)
        memv = sb("memv", [128, 2, 1024], BF16)
        svnew = sb("svnew", [128, 4, 512], BF16)
        ring = sb("ring", [128, 4, 4096], BF16)
        small = sb("small", [128, SM_N], F32)
        cbf = sb("cbf", [128, 7, 128], BF16)
        poolw = sb("poolw", [128, 2, 128], BF16)
        sqb = sb("sqb", [128, 2, 2, 512], BF16)
        nrm1 = sb("nrm1", [128, 512], F32)
        nrm2 = sb("nrm2", [128, 512], F32)
        stg = sb("stg", [128, 2, 512], F32)
        psum = es.enter_context(nc.psum_tensor("psum", [128, 8, 512], F32))

        bgcg = bcraw[:, :].bitcast(F32).rearrange("p (a c n) -> p a c n", a=2, c=2)
        bg = bgcg[:, 0]
        cg = bgcg[:, 1]
        actT = bcraw[:, 0:8 * TOK].rearrange("p (c n) -> p c n", c=8)
        bias = biasraw[:, :].rearrange("p (j h q) -> p j h q", j=5, h=8)
        skT = biasraw[:, 0:4 * 528].rearrange("p (c n) -> p c n", c=4)
        sV = biasraw[:, 2112:2112 + 2048].rearrange("p (k n) -> p k n", k=4)
        knat = pT[:, :, :].rearrange("p a n -> p (a n)").rearrange("p (k n) -> p k n", k=4)
        sbias = memv[:, 0, 0:640].rearrange("p (j h q) -> p j h q", j=5, h=8)
        spT = sqb[:, 0, :, :].rearrange("p a n -> p (a n)")[:, 0:640].rearrange("p (k h q) -> p k h q", k=5, h=8)

        sem_names = ["pe", "act", "dve", "pool", "sp"]
        sems = {e: es.enter_context(nc.semaphore("s_" + e)) for e in sem_names}
        dma_sems = {q: [es.enter_context(nc.semaphore(f"d_{q}{i}")) for i in range(n)]
                    for q, n in (("pool", 12), ("sp", 16))}
        block = es.enter_context(nc.Block())
        P = Prog(nc, sems, dma_sems)

        def smc(col, n=1, p0=0, p1=128):
            return small[p0:p1, col:col + n]

        def cb(i, p0=0, p1=128, c0=0, c1=128):
            return cbf[p0:p1, i, c0:c1]

        def bank(b):
            return psum[:, b, :]

        P.dma("sp", small[:, :], small_in[:, :], writes=["small"])
        P.dma("pool", cbf[:, :, :].rearrange("p a n -> p (a n)"), cbf_in[:, :], writes=["cbf"])

        zf = [(xT, "p a n -> p (a n)"), (hT, "p a n -> p (a n)"), (bcraw, None), (upb, "p a n -> p (a n)"), (tmpA, "p a n -> p (a n)"),
              (pooled, "p a n -> p (a n)"), (qT, "p a n -> p (a n)"), (kT, "p a n -> p (a n)"), (vtok, "p a n -> p (a n)"), (biasraw, None),
              (pT, "p a n -> p (a n)"), (memk, "p a n -> p (a n)"), (memv, "p a n -> p (a n)"), (svnew, "p a n -> p (a n)"),
              (sqb, "p a b n -> p (a b n)"), (nrm1, None), (nrm2, None), (stg, "p a n -> p (a n)")]
        for zi, (zt, pat) in enumerate(zf):
            zv = zt[:, :] if pat is None else zt[:].rearrange(pat)
            P.op("dve" if zi % 2 == 0 else "pool", lambda e, zv=zv: e.memset(zv, 0.0), writes=[("z", zi)])
        zkeys = [("z", zi) for zi in range(len(zf))]
        for l in range(L):
            P.dma("sp", car_k[l][:, :, :], kT[:, :, 0:512], reads=zkeys, writes=[("car_k", l)])
            P.dma("sp", car_v[l][:, :, :], vtok[:, 0:4, :], reads=zkeys, writes=[("car_v", l)])
            P.dma("sp", car_u[l][:, :, :], upb[:, :, 0:16], reads=zkeys, writes=[("car_u", l)])
            P.dma("sp", car_p[l][:, :, :], upb[:, :, 0:16], reads=zkeys, writes=[("car_p", l)])
        P.barrier()
        ring_i = [0]

        def load_w(src_ap, kc, ncols):
            s = ring_i[0]
            ring_i[0] = (s + 1) % 4
            dst = ring[:, s, 0:kc * ncols].rearrange("p (k n) -> p k n", k=kc)
            P.dma("pool", dst, src_ap.rearrange("(k p) n -> p k n", p=128), writes=[("ring", s)])
            return s, dst

        bank_rr = [0]

        def next_bank():
            b = bank_rr[0]
            bank_rr[0] = (b + 1) % 8
            return b

        def rmsnorm(tiles, gwhich_col0, out_fn=None):
            for ti in tiles:
                c0, n = TILES[ti]
                b = next_bank()
                for qd in range(4):
                    half = qd % 2
                    P.op("act", lambda e, half=half, qd=qd, c0=c0, n=n: e.activation(
                        out=sqb[:, half, :, 0:n], in_=xT[:, 2 * qd:2 * qd + 2, c0:c0 + n], func=AF.Square),
                        reads=[("xT", ti)], writes=[("sqb", half)])
                    for k2 in range(2):
                        k = 2 * qd + k2
                        P.op("pe", lambda e, half=half, k2=k2, k=k, b=b, n=n: e.matmul(
                            psum[:, b, 0:n], lhsT=cb(CB_MEAN), rhs=sqb[:, half, k2, 0:n], start=(k == 0), stop=(k == 7)),
                            reads=[("sqb", half), "cbf"], writes=[("ps", b)])
                tb, tkey = ((nrm1, "nrm1"), (nrm2, "nrm2"))[ti % 2]
                P.op("act", lambda e, b=b, n=n, tb=tb: e.activation(out=tb[:, 0:n], in_=psum[:, b, 0:n], func=AF.Sqrt,
                                                                    bias=smc(SM_EPS), scale=1.0),
                     reads=[("ps", b), "small"], writes=[tkey])
                P.op("dve", lambda e, n=n, tb=tb: e.reciprocal(out=tb[:, 0:n], in_=tb[:, 0:n]),
                     reads=[tkey], writes=[tkey])
                if out_fn is not None:
                    out_fn(ti, c0, n, tb, tkey)
                    continue
                for k in range(8):
                    P.op("dve", lambda e, k=k, c0=c0, n=n, tb=tb: e.scalar_tensor_tensor(
                        out=hT[:, k, c0:c0 + n], in0=xT[:, k, c0:c0 + n], scalar=smc(gwhich_col0 + k),
                        in1=tb[:, 0:n], op0=ALU.mult, op1=ALU.mult),
                        reads=[("xT", ti), tkey, "small"], writes=[("hT", ti)])

        def linear_fm(wsrc, src_buf, src_key, kc_n, m_chunks, tiles, evac):
            if not tiles:
                return
            for mg in range((m_chunks + 3) // 4):
                s, wt = load_w(wsrc(mg), kc_n, 512)
                for ti in tiles:
                    c0, n = TILES[ti]
                    for mi in range(min(4, m_chunks - mg * 4)):
                        m = mg * 4 + mi
                        b = next_bank()
                        for k in range(kc_n):
                            P.op("pe", lambda e, k=k, b=b, n=n, c0=c0, wt=wt, mi=mi: e.matmul(
                                psum[:, b, 0:n], lhsT=wt[:, k, mi * 128:(mi + 1) * 128], rhs=src_buf[:, k, c0:c0 + n],
                                start=(k == 0), stop=(k == kc_n - 1)),
                                reads=[("ring", s), (src_key, ti)], writes=[("ps", b)])
                        evac(m, ti, c0, n, b)

        def add_to_x(scale):
            def evac(m, ti, c0, n, b):
                P.op("dve", lambda e, m=m, c0=c0, n=n, b=b: e.scalar_tensor_tensor(
                    out=xT[:, m, c0:c0 + n], in0=psum[:, b, 0:n], scalar=scale, in1=xT[:, m, c0:c0 + n],
                    op0=ALU.mult, op1=ALU.add),
                    reads=[("ps", b), ("xT", ti)], writes=[("xT", ti)])
            return evac

        def ffn(l, wg, wu, wd, gwhich, tiles):
            if not tiles:
                return
            rmsnorm(tiles, gcol(l, gwhich, 0))
            groups = [(0, 8), (8, 16), (16, 22)]
            for (g0, g1) in groups:
                for sub in range(g0, g1, 4):
                    nch = min(4, g1 - sub)
                    sg, wgt = load_w(wg[l, :, sub * 128:(sub + nch) * 128], 8, nch * 128)
                    su, wut = load_w(wu[l, :, sub * 128:(sub + nch) * 128], 8, nch * 128)
                    for mi in range(nch):
                        mloc = sub - g0 + mi
                        for ti in tiles:
                            c0, n = TILES[ti]
                            ba = next_bank()
                            bb = next_bank()
                            for k in range(8):
                                P.op("pe", lambda e, k=k, ba=ba, n=n, c0=c0, wgt=wgt, mi=mi: e.matmul(
                                    psum[:, ba, 0:n], lhsT=wgt[:, k, mi * 128:(mi + 1) * 128], rhs=hT[:, k, c0:c0 + n],
                                    start=(k == 0), stop=(k == 7)),
                                    reads=[("ring", sg), ("hT", ti)], writes=[("ps", ba)])
                            for k in range(8):
                                P.op("pe", lambda e, k=k, bb=bb, n=n, c0=c0, wut=wut, mi=mi: e.matmul(
                                    psum[:, bb, 0:n], lhsT=wut[:, k, mi * 128:(mi + 1) * 128], rhs=hT[:, k, c0:c0 + n],
                                    start=(k == 0), stop=(k == 7)),
                                    reads=[("ring", su), ("hT", ti)], writes=[("ps", bb)])
                            P.op("act", lambda e, ba=ba, n=n: e.activation(out=nrm1[:, 0:n], in_=psum[:, ba, 0:n], func=AF.Silu),
                                 reads=[("ps", ba)], writes=["nrm1"])
                            P.op("dve", lambda e, bb=bb, n=n, c0=c0, mloc=mloc: e.tensor_tensor(
                                out=actT[:, mloc, c0:c0 + n], in0=psum[:, bb, 0:n], in1=nrm1[:, 0:n], op=ALU.mult),
                                reads=[("ps", bb), "nrm1"], writes=[("actT", ti), "bc"])
                slots = []
                for sub in range(g0, g1, 4):
                    nch = min(4, g1 - sub)
                    s, wt = load_w(wd[l, sub * 128:(sub + nch) * 128, :], nch, 1024)
                    slots.append((s, wt, nch))
                nk = g1 - g0
                for ti in tiles:
                    c0, n = TILES[ti]
                    for m2 in range(8):
                        b = next_bank()
                        kk = 0
                        for (s, wt, nch) in slots:
                            for k4 in range(nch):
                                P.op("pe", lambda e, k4=k4, kk=kk, b=b, n=n, c0=c0, wt=wt, m2=m2, nk=nk: e.matmul(
                                    psum[:, b, 0:n], lhsT=wt[:, k4, m2 * 128:(m2 + 1) * 128], rhs=actT[:, kk, c0:c0 + n],
                                    start=(kk == 0), stop=(kk == nk - 1)),
                                    reads=[("ring", s), ("actT", ti), "bc"], writes=[("ps", b)])
                                kk += 1
                        add_to_x(0.5)(m2, ti, c0, n, b)

        def bccol(ti):
            c0, n = TILES[ti]
            return 16 + c0, n

        def w_in_stage(l, ps_, tilesA):
            if not tilesA:
                return
            has_s = 2 in tilesA
            ptiles = [t for t in tilesA if t < 2]

            def bc_dst(buf, mloc, ti):
                if ti < 2:
                    c, n = bccol(ti)
                    return buf[:, mloc, c:c + n]
                return buf[:, mloc, 16 + TP:16 + TP + 128].rearrange("p (b s) -> p b s", b=4)[:, :, 16:32]

            def ps_src(b, ti, n):
                if ti < 2:
                    return psum[:, b, 0:n]
                return psum[:, b, 0:64].rearrange("p (b s) -> p b s", b=4)

            def evac_q(m, ti, c0, n, b):
                P.op("act", lambda e: e.activation(out=qT[0:64, 2 * m, c0:c0 + n], in_=psum[0:64, b, 0:n], func=AF.Copy, scale=0.125),
                     reads=[("ps", b)], writes=[("qT", ti)])
                P.op("act", lambda e: e.activation(out=qT[64:128, 2 * m + 1, c0:c0 + n], in_=psum[64:128, b, 0:n], func=AF.Copy, scale=0.125),
                     reads=[("ps", b)], writes=[("qT", ti)])

            def evac_k(m, ti, c0, n, b):
                P.op("act", lambda e: e.activation(out=kT[:, m, 512 + c0:512 + c0 + n], in_=psum[:, b, 0:n], func=AF.Copy),
                     reads=[("ps", b)], writes=[("kT", ti + 1)])

            def evac_bc(m, ti, c0, n, b):
                mm = m
                buf = bg if mm < 2 else cg
                P.op("act", lambda e: e.activation(out=bc_dst(buf, mm % 2, ti), in_=ps_src(b, ti, n), func=AF.Copy),
                     reads=[("ps", b)], writes=[("bc", ti), "bc"])

            def evac_hu(m, ti, c0, n, b):
                if m < 2:
                    P.op("dve", lambda e: e.tensor_tensor(out=bc_dst(cg, m, ti), in0=ps_src(b, ti, n), in1=bc_dst(cg, m, ti), op=ALU.mult),
                         reads=[("ps", b), ("bc", ti)], writes=[("bc", ti), "bc"])
                else:
                    P.op("act", lambda e: e.activation(out=bc_dst(upb, m - 2, ti), in_=ps_src(b, ti, n), func=AF.Copy),
                         reads=[("ps", b)], writes=[("up", ti)])

            wl = w_in[l]
            sub = DBG.get("sub")
            son = lambda nm: sub is None or nm in sub
            if son("q"):
                linear_fm(lambda mg: wl[:, 0:512], hT, "hT", 8, 4, tilesA, evac_q)
            if not son("k"):
                return
            s, wt = load_w(wl[:, 512:1024], 8, 512)
            for mi in range(4):
                for ti in tilesA:
                    c0, n = TILES[ti]
                    b = next_bank()
                    for k in range(8):
                        P.op("pe", lambda e, k=k, b=b, n=n, c0=c0, mi=mi, wt=wt: e.matmul(
                            psum[:, b, 0:n], lhsT=wt[:, k, mi * 128:(mi + 1) * 128], rhs=hT[:, k, c0:c0 + n],
                            start=(k == 0), stop=(k == 7)), reads=[("ring", s), ("hT", ti)], writes=[("ps", b)])
                    evac_k(mi, ti, c0, n, b)
            tok_out(l, ps_, s, wt, tilesA, o_pk, o_sk, False)
            if not son("v"):
                return
            s, wt = load_w(wl[:, 1024:1536], 8, 512)
            for ti in ptiles:
                c0, n = TILES[ti]
                for j in range(4):
                    b = next_bank()
                    cc = c0 + j * 128
                    for k in range(8):
                        P.op("pe", lambda e, k=k, b=b, cc=cc, wt=wt: e.matmul(
                            psum[:, b, :], lhsT=hT[:, k, cc:cc + 128], rhs=wt[:, k, :], start=(k == 0), stop=(k == 7)),
                            reads=[("ring", s), ("hT", ti)], writes=[("ps", b)])
                    vt = 4 + ti * 4 + j
                    P.op("dve", lambda e, b=b, vt=vt: e.tensor_copy(out=vtok[:, vt, :], in_=psum[:, b, :]),
                         reads=[("ps", b)], writes=[("vtok", ti + 1)])
            if has_s:
                for bq in range(4):
                    b = next_bank()
                    cc = TP + 16 * bq
                    for k in range(8):
                        P.op("pe", lambda e, k=k, b=b, cc=cc, wt=wt: e.matmul(
                            psum[0:16, b, :], lhsT=hT[:, k, cc:cc + 16], rhs=wt[:, k, :], start=(k == 0), stop=(k == 7)),
                            reads=[("ring", s), ("hT", 2)], writes=[("ps", b)])
                    P.op("dve", lambda e, b=b, bq=bq: e.tensor_copy(out=svnew[0:16, bq, :], in_=psum[0:16, b, :]),
                         reads=[("ps", b)], writes=["svnew"])
            tok_out(l, ps_, s, wt, tilesA, o_pv, o_sv, True)
            if not son("bc"):
                return
            linear_fm(lambda mg: wl[:, 1536:2048], hT, "hT", 8, 4, tilesA, evac_bc)
            if not son("hu"):
                return
            linear_fm(lambda mg: wl[:, 2048:2560], hT, "hT", 8, 4, tilesA, evac_hu)

        def tok_out(l, ps_, s, wt, tilesA, o_p, o_s, is_v):
            if ps_["name"] != "O2":
                return
            for j in range(4):
                b = next_bank()
                cc = 512 + j * 128
                for k in range(8):
                    P.op("pe", lambda e, k=k, b=b, cc=cc: e.matmul(
                        psum[:, b, :], lhsT=hT[:, k, cc:cc + 128], rhs=wt[:, k, :], start=(k == 0), stop=(k == 7)),
                        reads=[("ring", s), ("hT", 1)], writes=[("ps", b)])
                sj = j % 2
                P.op("act", lambda e, b=b, sj=sj: e.activation(out=stg[:, sj, :], in_=psum[:, b, :], func=AF.Copy),
                     reads=[("ps", b)], writes=[("stg", sj)])
                P.dma("sp", o_p[l, j * 128:(j + 1) * 128, :], stg[:, sj, :], reads=[("stg", sj)], writes=[("o_p", is_v, l, j)])
            b = next_bank()
            for k in range(8):
                P.op("pe", lambda e, k=k, b=b: e.matmul(
                    psum[0:64, b, :], lhsT=hT[:, k, TP:TP + 64], rhs=wt[:, k, :], start=(k == 0), stop=(k == 7)),
                    reads=[("ring", s), ("hT", 2)], writes=[("ps", b)])
            P.op("act", lambda e, b=b: e.activation(out=stg[0:64, 0, :], in_=psum[0:64, b, :], func=AF.Copy),
                 reads=[("ps", b)], writes=[("stg", 0)])
            for bq in range(4):
                P.dma("sp", o_s[l, bq, 496:512, :], stg[16 * bq:16 * bq + 16, 0, :], reads=[("stg", 0)],
                      writes=[("o_s", is_v, l, bq)])

        def carry_out(l, ps_, tilesA):
            if 1 not in tilesA or ps_["name"] == "O2":
                return
            P.dma("sp", car_k[l][:, :, :], kT[:, :, 512 + 512:512 + 1024], reads=[("kT", 2)], writes=[("car_k", l)])
            P.dma("sp", car_v[l][:, :, :], vtok[:, 8:12, :], reads=[("vtok", 2)], writes=[("car_v", l)])
            P.dma("sp", car_u[l][:, :, :], cg[:, :, 16 + TP - 16:16 + TP], reads=[("bc", 1), "bc"], writes=[("car_u", l)])
            P.dma("sp", car_p[l][:, :, :], upb[:, :, 16 + TP - 16:16 + TP], reads=[("up", 1)], writes=[("car_p", l)])

        def carry_in(l, ps_, tilesR):
            if 0 not in tilesR or ps_["name"] == "H1":
                return
            import os
            cv_ = os.environ.get("CIN", "k,v,u,p,f").split(",")
            if "k" in cv_:
                P.dma("sp", kT[:, :, 0:512], car_k[l][:, :, :], reads=[("car_k", l)], writes=[("kT", 0)])
            if "v" in cv_:
                P.dma("sp", vtok[:, 0:4, :], car_v[l][:, :, :], reads=[("car_v", l)], writes=[("vtok", 0)])
            if "u" in cv_:
                P.dma("sp", cg[:, :, 0:16], car_u[l][:, :, :], reads=[("car_u", l)], writes=[("bch", 0), "bc"])
            if "p" in cv_:
                P.dma("sp", upb[:, :, 0:16], car_p[l][:, :, :], reads=[("car_p", l)], writes=[("uph", 0)])
            if ps_["name"] == "O1" and "f" in cv_:
                P.op("act", lambda e: e.activation(out=cg[:, :, 0:16], in_=cg[:, :, 0:16], func=AF.Copy, scale=smc(SM_TFLAG)),
                     reads=[("bch", 0), "small"], writes=[("bch", 0), "bc"])
                P.op("act", lambda e: e.activation(out=upb[:, :, 0:16], in_=upb[:, :, 0:16], func=AF.Copy, scale=smc(SM_TFLAG)),
                     reads=[("uph", 0), "small"], writes=[("uph", 0)])

        hn_par = [0]

        def headnorm_bc(l, src, srctok, chunk0, dti, c, n, dstc0):
            for cc in range(2):
                b = next_bank()
                par = hn_par[0]
                hn_par[0] ^= 1
                tb, tkey = ((nrm1, "nrm1"), (nrm2, "nrm2"))[par]
                P.op("act", lambda e, cc=cc, par=par: e.activation(out=sqb[:, par, 0, 0:n], in_=src[:, cc, c:c + n], func=AF.Square),
                     reads=[srctok], writes=[("sqb", par)])
                P.op("pe", lambda e, b=b, par=par: e.matmul(psum[:, b, 0:n], lhsT=cb(CB_BLK64), rhs=sqb[:, par, 0, 0:n], start=True, stop=True),
                     reads=[("sqb", par), "cbf"], writes=[("ps", b)])
                P.op("act", lambda e, b=b, tb=tb: e.activation(out=tb[:, 0:n], in_=psum[:, b, 0:n], func=AF.Sqrt, bias=smc(SM_EPS), scale=1.0),
                     reads=[("ps", b), "small"], writes=[tkey])
                P.op("dve", lambda e, tb=tb: e.reciprocal(out=tb[:, 0:n], in_=tb[:, 0:n]), reads=[tkey], writes=[tkey])
                P.op("dve", lambda e, cc=cc, tb=tb: e.scalar_tensor_tensor(
                    out=hT[:, chunk0 + cc, dstc0:dstc0 + n], in0=src[:, cc, c:c + n], scalar=smc(gcol(l, G_HEADS, chunk0 + cc)),
                    in1=tb[:, 0:n], op0=ALU.mult, op1=ALU.mult),
                    reads=[srctok, tkey, "small"], writes=[("hT", dti)])

        def bc_stage(l, ps_, tilesR):
            if not tilesR:
                return
            pt = [t for t in tilesR if t < 2]
            has_s = 2 in tilesR
            regs = []
            if pt:
                a = 16 + TILES[pt[0]][0]
                bnd = 16 + TILES[pt[-1]][0] + TILES[pt[-1]][1]
                regs.append(("p", a, bnd))
            if has_s:
                regs.append(("s", 0, 0))
                sview_u = cg[:, :, 16 + TP:16 + TP + 128].rearrange("p c (b s) -> p c b s", b=4)
                sview_p = upb[:, :, 16 + TP:16 + TP + 128].rearrange("p c (b s) -> p c b s", b=4)
                for c in range(2):
                    P.dma("sp", sview_u[:, c, :, 0:16], shu_in[l, :, c, :, :], writes=[("bch", 2), "bc"])
                    P.dma("sp", sview_p[:, c, :, 0:16], shp_in[l, :, c, :, :], writes=[("uph", 2)])

            def view(buf, reg, c, sh, lo):
                kind, a, bnd = reg
                if kind == "p":
                    return buf[:, c, a - lo - sh:bnd - sh]
                v = buf[:, c, 16 + TP:16 + TP + 128].rearrange("p (b s) -> p b s", b=4)
                return v[:, :, 16 - lo - sh:32 - sh]

            rk = [("bc", t) for t in tilesR] + [("bch", 0), ("bch", 2), "bc"]
            uk = [("up", t) for t in tilesR] + [("uph", 0), ("uph", 2)]
            if ps_["name"] == "O2":
                P.dma("sp", o_pc[l, :, :, :], cg[:, :, 16 + TP - 16:16 + TP], reads=[("bc", 1), "bc"], writes=[("o_pc", l)])
                P.dma("sp", o_pp[l, :, :, :], upb[:, :, 16 + TP - 16:16 + TP], reads=[("up", 1)], writes=[("o_pp", l)])
                if has_s:
                    for c in range(2):
                        P.dma("sp", o_sc[l, :, c, :, :], cg[:, c, 16 + TP:16 + TP + 128].rearrange("p (b s) -> p b s", b=4)[:, :, 16:32],
                              reads=[("bc", 2), "bc"], writes=[("o_sc", l, c)])
                        P.dma("sp", o_sp[l, :, c, :, :], upb[:, c, 16 + TP:16 + TP + 128].rearrange("p (b s) -> p b s", b=4)[:, :, 16:32],
                              reads=[("up", 2)], writes=[("o_sp", l, c)])
            for reg in regs:
                for c in range(2):
                    cw = lambda i, c=c: smc(SM_CONV + (l * 3 + i) * 2 + c)
                    P.op("dve", lambda e, reg=reg, c=c, cw=cw: e.tensor_scalar(out=view(tmpA, reg, c, 0, 0), in0=view(cg, reg, c, 0, 0),
                                                                               scalar1=cw(2), scalar2=None, op0=ALU.mult),
                         reads=rk + ["small"], writes=["tmpA"])
                    for i, sh in ((1, 1), (0, 2)):
                        P.op("dve", lambda e, reg=reg, c=c, cw=cw, i=i, sh=sh: e.scalar_tensor_tensor(
                            out=view(tmpA, reg, c, 0, 0), in0=view(cg, reg, c, sh, 0), scalar=cw(i), in1=view(tmpA, reg, c, 0, 0),
                            op0=ALU.mult, op1=ALU.add), reads=rk + ["tmpA", "small"], writes=["tmpA"])
                    P.op("dve", lambda e, reg=reg, c=c: e.tensor_tensor(out=view(bg, reg, c, 0, 0), in0=view(bg, reg, c, 0, 0),
                                                                        in1=view(tmpA, reg, c, 0, 0), op=ALU.mult),
                         reads=rk + ["tmpA"], writes=[("bcb", 0), "bc"])
            for reg in regs:
                for c in range(2):
                    P.op("dve", lambda e, reg=reg, c=c: e.tensor_tensor(out=view(tmpA, reg, c, 0, 14), in0=view(upb, reg, c, 0, 14),
                                                                        in1=view(upb, reg, c, 1, 14), op=ALU.add),
                         reads=uk + [("bcb", 0)], writes=["tmpA"])
                    P.op("dve", lambda e, reg=reg, c=c: e.tensor_tensor(out=view(cg, reg, c, 0, 12), in0=view(tmpA, reg, c, 0, 12),
                                                                        in1=view(tmpA, reg, c, 2, 12), op=ALU.add),
                         reads=["tmpA", ("car_u", l), ("o_pc", l), ("o_sc", l, 0), ("o_sc", l, 1)], writes=["cgs", "bc"])
                for (p0, p1, sbuf_) in ((0, 64, tmpA), (64, 128, cg)):
                    P.op("dve", lambda e, reg=reg, p0=p0, p1=p1, sbuf_=sbuf_: e.scalar_tensor_tensor(
                        out=view(pooled, reg, 0, 0, 0)[p0:p1], in0=view(sbuf_, reg, 0, 0, 0)[p0:p1], scalar=smc(SM_INVW, 1, p0, p1),
                        in1=view(upb, reg, 0, 0, 0)[p0:p1], op0=ALU.mult, op1=ALU.subtract),
                        reads=["tmpA", "cgs", "small"] + uk, writes=["pooled"])
                P.op("dve", lambda e, reg=reg: e.tensor_tensor(out=view(tmpA, reg, 1, 0, 8), in0=view(cg, reg, 1, 0, 8),
                                                                in1=view(cg, reg, 1, 4, 8), op=ALU.add),
                     reads=["cgs", "pooled"], writes=["tmpA"])
                P.op("dve", lambda e, reg=reg: e.tensor_tensor(out=view(cg, reg, 1, 0, 0), in0=view(tmpA, reg, 1, 0, 0),
                                                                in1=view(tmpA, reg, 1, 8, 0), op=ALU.add),
                     reads=["tmpA"], writes=["cgs", "bc"])
                for (p0, p1, sbuf_) in ((0, 64, tmpA), (64, 128, cg)):
                    P.op("dve", lambda e, reg=reg, p0=p0, p1=p1, sbuf_=sbuf_: e.scalar_tensor_tensor(
                        out=view(pooled, reg, 1, 0, 0)[p0:p1], in0=view(sbuf_, reg, 1, 0, 0)[p0:p1], scalar=smc(SM_INVW + 1, 1, p0, p1),
                        in1=view(upb, reg, 1, 0, 0)[p0:p1], op0=ALU.mult, op1=ALU.subtract),
                        reads=["tmpA", "cgs", "small"] + uk, writes=["pooled"])
                if reg[0] == "p" and ps_["name"] == "O1" and reg[1] == 16 and not DBG.get("nocorr"):
                    for c, lo_buf, hi_buf in ((0, tmpA, cg), (1, tmpA, cg)):
                        for (p0, p1, sbuf_) in ((0, 64, lo_buf), (64, 128, hi_buf)):
                            P.op("dve", lambda e, c=c, p0=p0, p1=p1, sbuf_=sbuf_: e.tensor_tensor(
                                out=nrm1[p0:p1, 0:16], in0=sbuf_[p0:p1, c, 16:32], in1=small[p0:p1, SM_INVCNT + 16 * c:SM_INVCNT + 16 * c + 16], op=ALU.mult),
                                reads=["tmpA", "cgs", "small"], writes=["nrm1"])
                            P.op("dve", lambda e, c=c, p0=p0, p1=p1: e.tensor_tensor(
                                out=pooled[p0:p1, c, 16:32], in0=nrm1[p0:p1, 0:16], in1=upb[p0:p1, c, 16:32], op=ALU.subtract),
                                reads=["nrm1"] + uk, writes=["pooled"])
            for ti in tilesR:
                for c in range(2):
                    b = next_bank()
                    if ti < 2:
                        cc, n = bccol(ti)
                        rhs = pooled[:, c, cc:cc + n]
                        dst = upb[:, c, cc:cc + n]
                        src = psum[:, b, 0:n]
                    else:
                        n = 64
                        rhs = pooled[:, c, 16 + TP:16 + TP + 128].rearrange("p (b s) -> p b s", b=4)[:, :, 16:32]
                        dst = upb[:, c, 16 + TP:16 + TP + 128].rearrange("p (b s) -> p b s", b=4)[:, :, 16:32]
                        src = psum[:, b, 0:64].rearrange("p (b s) -> p b s", b=4)
                    P.op("pe", lambda e, c=c, b=b, rhs=rhs, n=n: e.matmul(psum[:, b, 0:n], lhsT=poolw[:, c, :], rhs=rhs, start=True, stop=True),
                         reads=["pooled", "poolw"], writes=[("ps", b)])
                    P.op("dve", lambda e, c=c, dst=dst, src=src: e.tensor_scalar(out=dst, in0=src, scalar1=smc(SM_PSCALE + l * 2 + c), scalar2=None, op0=ALU.mult),
                         reads=[("ps", b), "small", ("car_p", l), ("o_pp", l), ("o_sp", l, 0), ("o_sp", l, 1)], writes=[("up", ti)])
            for ti in tilesR:
                if ti < 2:
                    cc, n = bccol(ti)
                    headnorm_bc(l, bg, ("bcb", 0), 4, ti, cc, n, TILES[ti][0])
                    headnorm_bc(l, upb, ("up", ti), 6, ti, cc, n, TILES[ti][0])
                else:
                    for (src, key, ch0) in ((bg, ("bcb", 0), 4), (upb, ("up", 2), 6)):
                        for c in range(2):
                            P.op("dve", lambda e, src=src, c=c: e.tensor_copy(
                                out=tmpA[:, c, 0:64].rearrange("p (b s) -> p b s", b=4),
                                in_=src[:, c, 16 + TP:16 + TP + 128].rearrange("p (b s) -> p b s", b=4)[:, :, 16:32]),
                                reads=[key, "pooled"], writes=["tmpA"])
                        headnorm_bc(l, tmpA, "tmpA", ch0, 2, 0, 64, TP)

        def attn_epilogue(l, bn, bd, ncols, dst_cols, ti):
            w = 4 * ncols
            b = next_bank_misc()
            P.op("act", lambda e: e.activation(out=sqb[:, 1, 0, 0:w], in_=psum[:, bn, 0:w], func=AF.Square),
                 reads=[("ps", bn)], writes=[("sqb", 1)])
            P.op("act", lambda e: e.activation(out=nrm2[:, 0:w], in_=psum[:, bd, 0:w], func=AF.Square, scale=1e-3),
                 reads=[("ps", bd)], writes=["nrm2"])
            P.op("pe", lambda e: e.matmul(psum[:, b, 0:w], lhsT=cb(CB_BLK64), rhs=sqb[:, 1, 0, 0:w], start=True, stop=True),
                 reads=[("sqb", 1), "cbf"], writes=[("ps", b)])
            P.op("dve", lambda e: e.tensor_tensor(out=nrm1[:, 0:w], in0=psum[:, b, 0:w], in1=nrm2[:, 0:w], op=ALU.add),
                 reads=[("ps", b), "nrm2"], writes=["nrm1"])
            P.op("act", lambda e: e.activation(out=nrm2[:, 0:w], in_=nrm1[:, 0:w], func=AF.Sqrt),
                 reads=["nrm1"], writes=["nrm2"])
            P.op("dve", lambda e: e.reciprocal(out=nrm1[:, 0:w], in_=nrm2[:, 0:w]), reads=["nrm2"], writes=["nrm1"])
            for hp in range(4):
                P.op("dve", lambda e, hp=hp: e.scalar_tensor_tensor(
                    out=hT[:, hp, dst_cols:dst_cols + ncols], in0=psum[:, bn, hp * ncols:(hp + 1) * ncols],
                    scalar=smc(gcol(l, G_HEADS, hp)), in1=nrm1[:, hp * ncols:(hp + 1) * ncols], op0=ALU.mult, op1=ALU.mult),
                    reads=[("ps", bn), "nrm1", "small"], writes=[("hT", ti)])

        misc_rr = [0]

        def next_bank_misc():
            b = misc_rr[0]
            misc_rr[0] = (b + 1) % 4
            return b

        def attn_prompt(l, ps_, tilesR):
            pt = [t for t in tilesR if t < 2]
            if not pt:
                return
            P.dma("pool", biasraw[:, :].rearrange("p (j x) -> p j x", j=5),
                  bias_in[l].rearrange("(j p) h q -> p j (h q)", p=128), writes=["bias"])
            for (j, cbi) in ((0, CB_SM0), (4, CB_SM4)):
                for h in range(8):
                    P.op("dve", lambda e, j=j, h=h, cbi=cbi: e.tensor_tensor(out=bias[:, j, h, :], in0=bias[:, j, h, :], in1=cb(cbi), op=ALU.add),
                         reads=["bias", "cbf"], writes=["bias"])
            gpar = 0
            for ti in pt:
                for g4 in range(4):
                    gi = ti * 4 + g4
                    q0 = gi * 128
                    bn = 4 + 2 * gpar
                    bd = 5 + 2 * gpar
                    gpar ^= 1
                    for bz in (bn, bd):
                        P.op("pe", lambda e, bz=bz: e.matmul(psum[:, bz, :], lhsT=cb(CB_ZERO), rhs=cbf[:, 0:4, :].rearrange("p a n -> p (a n)"), start=True, stop=False),
                             reads=["cbf"], writes=[("ps", bz)])
                    def emit_S(jt, gi=gi, q0=q0, ti=ti):
                        kt = gi + jt
                        kreg = ("kT", 0) if kt < 4 else ("kT", 1 + (kt - 4) // 4)
                        masked = (ps_["name"] == "O1" and kt < 4)
                        pbuf = jt % 2
                        for hb in range(2):
                            b = next_bank_misc()
                            P.op("pe", lambda e, b=b, jt=jt, hb=hb: e.matmul(
                                psum[:, b, :], lhsT=cb(CB_IDENT), rhs=bias[:, jt, hb * 4:hb * 4 + 4, :].rearrange("p h q -> p (h q)"),
                                start=True, stop=False), reads=["bias", "cbf"], writes=[("ps", b)])
                            for h4 in range(4):
                                h = hb * 4 + h4
                                P.op("pe", lambda e, b=b, h=h, h4=h4, kt=kt, q0=q0: e.matmul(
                                    psum[:, b, h4 * 128:(h4 + 1) * 128], lhsT=kT[:, h // 2, kt * 128:(kt + 1) * 128],
                                    rhs=qT[:, h, q0:q0 + 128], start=False, stop=(h4 == 3)),
                                    reads=[kreg, ("qT", ti)], writes=[("ps", b)])
                            mcol = SM_HMASK if masked else SM_ZERO
                            P.op("act", lambda e, b=b, pbuf=pbuf, hb=hb, mcol=mcol: e.activation(
                                out=pT[:, pbuf, hb * 512:(hb + 1) * 512], in_=psum[:, b, :], func=AF.Exp, bias=smc(mcol), scale=1.0),
                                reads=[("ps", b), "small"], writes=[("pT", pbuf, hb)])

                    def emit_PV(jt, gi=gi, bn=bn, bd=bd):
                        kt = gi + jt
                        vreg = ("vtok", 0) if kt < 4 else ("vtok", 1 + (kt - 4) // 4)
                        pbuf = jt % 2
                        for h in range(8):
                            hp, r0 = h // 2, (h % 2) * 64
                            P.op("pe", lambda e, h=h, hp=hp, r0=r0, kt=kt, pbuf=pbuf, bn=bn, jt=jt: e.matmul(
                                psum[r0:r0 + 64, bn, hp * 128:(hp + 1) * 128], lhsT=vtok[:, kt, h * 64:(h + 1) * 64],
                                rhs=pT[:, pbuf, h * 128:(h + 1) * 128], start=False, stop=(jt == 4 and h >= 6)),
                                reads=[vreg, ("pT", pbuf, h // 4)], writes=[("ps", bn)])
                            P.op("pe", lambda e, h=h, hp=hp, r0=r0, pbuf=pbuf, bd=bd, jt=jt: e.matmul(
                                psum[r0:r0 + 64, bd, hp * 128:(hp + 1) * 128], lhsT=cb(CB_ONES, 0, 128, 0, 64),
                                rhs=pT[:, pbuf, h * 128:(h + 1) * 128], start=False, stop=(jt == 4 and h >= 6)),
                                reads=["cbf", ("pT", pbuf, h // 4)], writes=[("ps", bd)])

                    emit_S(0)
                    for jt in range(5):
                        if jt + 1 < 5:
                            emit_S(jt + 1)
                        emit_PV(jt)
                    attn_epilogue(l, bn, bd, 128, q0, ti)

        def attn_sample(l, ps_, tilesR):
            if 2 not in tilesR:
                return
            P.dma("pool", sbias[:, :, :, :].rearrange("p j h q -> p j (h q)"),
                  sbias_in[l].rearrange("(j p) h q -> p j (h q)", p=128), writes=["memk"])
            cs_i = 0
            for bq in range(4):
                for (src_c, dst_o) in ((ck_in, o_sk), (cv_in, o_sv)):
                    for r0 in range(0, 496, 128):
                        nr = min(128, 496 - r0)
                        sj = cs_i % 2
                        cs_i += 1
                        P.dma("sp", stg[0:nr, sj, :], src_c[l, bq, 16 + r0:16 + r0 + nr, :], writes=[("stg", sj)])
                        P.dma("sp", dst_o[l, bq, r0:r0 + nr, :], stg[0:nr, sj, :], reads=[("stg", sj)], writes=[("o_sc_rows", l, bq, r0, id(dst_o))])
                P.dma("pool", knat[:, :, :], ck_in[l, bq].rearrange("(k p) n -> p k n", p=128),
                      writes=[("pT", 0, 0), ("pT", 0, 1), ("pT", 1, 0), ("pT", 1, 1)])
                P.dma("pool", sV[:, :, :], cv_in[l, bq].rearrange("(k p) n -> p k n", p=128), writes=["bias"])
                for c in range(4):
                    b = next_bank_misc()
                    pv = psum[:, b, :].bitcast(BF16)
                    for k4 in range(4):
                        P.op("pe", lambda e, c=c, k4=k4, pv=pv: e.matmul(pv[:, k4 * 128:(k4 + 1) * 128], lhsT=knat[:, k4, c * 128:(c + 1) * 128],
                                                                         rhs=cb(CB_IDENT), start=True, stop=True, is_transpose=True),
                             reads=[("pT", 0, 0), ("pT", 0, 1), ("pT", 1, 0), ("pT", 1, 1), "cbf"], writes=[("ps", b)])
                    P.op("dve", lambda e, c=c, pv=pv: e.tensor_copy(out=skT[:, c, 0:512], in_=pv[:, 0:512]),
                         reads=[("ps", b)], writes=["bias"])
                bn, bd = 4, 5
                for bz in (bn, bd):
                    P.op("pe", lambda e, bz=bz: e.matmul(psum[:, bz, 0:64], lhsT=cb(CB_ZERO), rhs=cbf[:, 0, 0:64], start=True, stop=False),
                         reads=["cbf"], writes=[("ps", bz)])
                qc = TP + 16 * bq
                for kt in range(5):
                    b = next_bank_misc()
                    np_ = 128 if kt < 4 else 16
                    P.op("pe", lambda e, b=b, kt=kt, np_=np_: e.matmul(
                        psum[0:np_, b, 0:128], lhsT=cb(CB_IDENT, 0, np_, 0, np_), rhs=sbias[0:np_, kt, :, :].rearrange("p h q -> p (h q)"),
                        start=True, stop=False), reads=["memk", "cbf"], writes=[("ps", b)])
                    for h in range(8):
                        r0 = (h % 2) * 64
                        if kt < 4:
                            lhs = skT[:, h // 2, kt * 128:(kt + 1) * 128]
                        else:
                            lhs = kT[:, h // 2, 512 + qc:512 + qc + 16]
                        P.op("pe", lambda e, b=b, h=h, r0=r0, lhs=lhs, np_=np_, qc=qc: e.matmul(
                            psum[0:np_, b, h * 16:(h + 1) * 16], lhsT=lhs, rhs=qT[:, h, qc:qc + 16],
                            start=False, stop=(h == 7)), reads=["bias", ("kT", 3), ("qT", 2)], writes=[("ps", b)])
                    P.op("act", lambda e, b=b, kt=kt, np_=np_: e.activation(
                        out=spT[0:np_, kt, :, :].rearrange("p h q -> p (h q)"), in_=psum[0:np_, b, 0:128], func=AF.Exp),
                        reads=[("ps", b)], writes=[("sqb", 0)])
                    for h in range(8):
                        hp, r0 = h // 2, (h % 2) * 64
                        if kt < 4:
                            lv = sV[:, kt, h * 64:(h + 1) * 64]
                        else:
                            lv = svnew[0:16, bq, h * 64:(h + 1) * 64]
                        P.op("pe", lambda e, h=h, hp=hp, r0=r0, lv=lv, kt=kt, np_=np_: e.matmul(
                            psum[r0:r0 + 64, bn, hp * 16:(hp + 1) * 16], lhsT=lv, rhs=spT[0:np_, kt, h, :], start=False, stop=(kt == 4 and h >= 6)),
                            reads=["bias", "svnew", ("sqb", 0)], writes=[("ps", bn)])
                        P.op("pe", lambda e, h=h, hp=hp, r0=r0, kt=kt, np_=np_: e.matmul(
                            psum[r0:r0 + 64, bd, hp * 16:(hp + 1) * 16], lhsT=cb(CB_ONES, 0, np_, 0, 64), rhs=spT[0:np_, kt, h, :],
                            start=False, stop=(kt == 4 and h >= 6)), reads=["cbf", ("sqb", 0)], writes=[("ps", bd)])
                attn_epilogue(l, bn, bd, 16, qc, 2)

        def xattn_mem(l, ps_, tilesR):
            pt = [t for t in tilesR if t < 2]
            if pt:
                KK = [("kT", i) for i in range(4)]
                memf = kT[:, :, :].rearrange("p a n -> p (a n)").bitcast(F32)[:, 0:2048].rearrange("p (c n) -> p c n", c=8)
                P.dma("sp", memf, memT_in.rearrange("(c p) n -> p c n", p=128), writes=KK)
                b = next_bank()
                for qd in range(4):
                    half = qd % 2
                    P.op("act", lambda e, half=half, qd=qd: e.activation(out=sqb[:, half, :, 0:256], in_=memf[:, 2 * qd:2 * qd + 2, :], func=AF.Square),
                         reads=KK, writes=[("sqb", half)])
                    for k2 in range(2):
                        k = 2 * qd + k2
                        P.op("pe", lambda e, half=half, k2=k2, k=k, b=b: e.matmul(psum[:, b, 0:256], lhsT=cb(CB_MEAN), rhs=sqb[:, half, k2, 0:256],
                                                                           start=(k == 0), stop=(k == 7)),
                             reads=[("sqb", half), "cbf"], writes=[("ps", b)])
                P.op("act", lambda e, b=b: e.activation(out=nrm1[:, 0:256], in_=psum[:, b, 0:256], func=AF.Sqrt, bias=smc(SM_EPS), scale=1.0),
                     reads=[("ps", b), "small"], writes=["nrm1"])
                P.op("dve", lambda e: e.reciprocal(out=nrm2[:, 0:256], in_=nrm1[:, 0:256]), reads=["nrm1"], writes=["nrm2"])
                mnT = kT[:, :, :].rearrange("p a n -> p (a n)")[:, 4096:6144].rearrange("p (c n) -> p c n", c=8)
                for k in range(8):
                    P.op("dve", lambda e, k=k: e.scalar_tensor_tensor(out=mnT[:, k, :], in0=memf[:, k, :], scalar=smc(gcol(l, G_MEM, k)),
                                                                      in1=nrm2[:, 0:256], op0=ALU.mult, op1=ALU.mult),
                         reads=KK + ["nrm2", "small"], writes=KK)
                write_out = ps_["name"] == "O1"
                for mg in range(2):
                    s, wt = load_w(w_xk[l][:, mg * 512:(mg + 1) * 512], 8, 512)
                    for mi in range(4):
                        m = mg * 4 + mi
                        b = next_bank()
                        for k in range(8):
                            P.op("pe", lambda e, k=k, mi=mi, wt=wt, b=b: e.matmul(psum[:, b, 0:256], lhsT=wt[:, k, mi * 128:(mi + 1) * 128], rhs=mnT[:, k, :],
                                                                                  start=(k == 0), stop=(k == 7)),
                                 reads=[("ring", s)] + KK, writes=[("ps", b)])
                        P.op("act", lambda e, m=m, b=b: e.activation(out=memk[:, m, :], in_=psum[:, b, 0:256], func=AF.Copy),
                             reads=[("ps", b)], writes=["memk"])
                        if write_out:
                            sj = m % 2
                            P.op("dve", lambda e, b=b, sj=sj: e.tensor_copy(out=stg[:, sj, 0:256], in_=psum[:, b, 0:256]),
                                 reads=[("ps", b)], writes=[("stg", sj), ("ps", b)])
                            P.dma("sp", o_mkT[l, m * 128:(m + 1) * 128, :], stg[:, sj, 0:256], reads=[("stg", sj)], writes=[("o_mkT", l, m)])
                for mg in range(2):
                    s, wt = load_w(w_xv[l][:, mg * 512:(mg + 1) * 512], 8, 512)
                    for mt in range(2):
                        b = next_bank()
                        for k in range(8):
                            P.op("pe", lambda e, k=k, mt=mt, wt=wt, b=b: e.matmul(psum[:, b, :], lhsT=mnT[:, k, mt * 128:(mt + 1) * 128], rhs=wt[:, k, :],
                                                                                  start=(k == 0), stop=(k == 7)),
                                 reads=[("ring", s)] + KK, writes=[("ps", b)])
                        P.op("act", lambda e, mt=mt, mg=mg, b=b: e.activation(out=memv[:, mt, mg * 512:(mg + 1) * 512], in_=psum[:, b, :], func=AF.Copy),
                             reads=[("ps", b)], writes=["memk"])
                        if write_out:
                            sj = mt
                            P.op("dve", lambda e, b=b, sj=sj: e.tensor_copy(out=stg[:, sj, :], in_=psum[:, b, :]),
                                 reads=[("ps", b)], writes=[("stg", sj), ("ps", b)])
                            P.dma("sp", o_mv[l, mt * 128:(mt + 1) * 128, mg * 512:(mg + 1) * 512], stg[:, sj, :], reads=[("stg", sj)],
                                  writes=[("o_mv", l, mt, mg)])

        def xattn(l, ps_, tilesR):
            if not tilesR:
                return
            pt = [t for t in tilesR if t < 2]
            has_s = 2 in tilesR
            xsub = DBG.get("xsub")
            xon = lambda nm: xsub is None or nm in xsub
            rmsnorm(tilesR, gcol(l, G_XATTN, 0))

            def evac_qx(m, ti, c0, n, b):
                P.op("act", lambda e: e.activation(out=actT[:, m, c0:c0 + n], in_=psum[:, b, 0:n], func=AF.Copy, scale=1.0 / 16.0),
                     reads=[("ps", b)], writes=[("actT", ti), "bc"])
            linear_fm(lambda mg: w_xq[l][:, mg * 512:(mg + 1) * 512], hT, "hT", 8, 8, tilesR, evac_qx)
            if not xon("mem"):
                return

            def attend(c0, n, ti, mk, mv, mkey):
                def s_part(h):
                    ph = h % 2
                    bs = [next_bank() for _ in range(2)]
                    for mt in range(2):
                        for dc in range(2):
                            P.op("pe", lambda e, mt=mt, dc=dc, h=h, b=bs[mt]: e.matmul(
                                psum[:, b, 0:n], lhsT=mk[:, h * 2 + dc, mt * 128:(mt + 1) * 128], rhs=actT[:, h * 2 + dc, c0:c0 + n],
                                start=(dc == 0), stop=(dc == 1)), reads=[mkey, ("actT", ti), "bc"], writes=[("ps", bs[mt])])
                        P.op("act", lambda e, mt=mt, b=bs[mt], ph=ph: e.activation(out=pT[:, mt, ph * 512:ph * 512 + n], in_=psum[:, b, 0:n], func=AF.Exp),
                             reads=[("ps", bs[mt])], writes=[("pT", mt, ph)])

                def pv_part(h):
                    ph = h % 2
                    bd = next_bank()
                    bnum = [next_bank() for _ in range(2)]
                    for mt in range(2):
                        P.op("pe", lambda e, mt=mt, bd=bd, ph=ph: e.matmul(psum[:, bd, 0:n], lhsT=cb(CB_ONES), rhs=pT[:, mt, ph * 512:ph * 512 + n],
                                                                          start=(mt == 0), stop=(mt == 1)),
                             reads=["cbf", ("pT", mt, ph)], writes=[("ps", bd)])
                    for dc in range(2):
                        for mt in range(2):
                            P.op("pe", lambda e, mt=mt, dc=dc, h=h, bq_=bnum[dc], ph=ph: e.matmul(
                                psum[:, bq_, 0:n], lhsT=mv[:, mt, (h * 2 + dc) * 128:(h * 2 + dc + 1) * 128], rhs=pT[:, mt, ph * 512:ph * 512 + n],
                                start=(mt == 0), stop=(mt == 1)), reads=[mkey, ("pT", mt, ph)], writes=[("ps", bnum[dc])])
                    P.op("dve", lambda e, bd=bd: e.reciprocal(out=nrm1[:, 0:n], in_=psum[:, bd, 0:n]), reads=[("ps", bd)], writes=["nrm1"])
                    for dc in range(2):
                        P.op("dve", lambda e, dc=dc, h=h, bq_=bnum[dc]: e.tensor_tensor(out=hT[:, h * 2 + dc, c0:c0 + n], in0=psum[:, bq_, 0:n],
                                                                                      in1=nrm1[:, 0:n], op=ALU.mult),
                             reads=[("ps", bnum[dc]), "nrm1"], writes=[("hT", ti)])

                s_part(0)
                for h in range(4):
                    if h + 1 < 4:
                        s_part(h + 1)
                    pv_part(h)

            if pt:
                for ti in pt:
                    c0, n = TILES[ti]
                    if xon("att"):
                        attend(c0, n, ti, memk, memv, "memk")
            if has_s:
                for bq in range(4):
                    P.dma("pool", memv[:, :, :], cmv_in[l, bq].rearrange("(k p) n -> p k n", p=128), writes=["memk"])
                    mkn = vtok[:, 0:4, :].rearrange("p a n -> p (a n)").rearrange("p (k n) -> p k n", k=2)
                    P.dma("pool", mkn, cmk_in[l, bq].rearrange("(k p) n -> p k n", p=128), writes=[("vtok", 0)])
                    for c in range(8):
                        b = next_bank()
                        pv = psum[:, b, :].bitcast(BF16)
                        for mt in range(2):
                            P.op("pe", lambda e, c=c, mt=mt, pv=pv: e.matmul(pv[:, mt * 128:(mt + 1) * 128], lhsT=mkn[:, mt, c * 128:(c + 1) * 128],
                                                                             rhs=cb(CB_IDENT), start=True, stop=True, is_transpose=True),
                                 reads=[("vtok", 0), "cbf"], writes=[("ps", b)])
                        P.op("dve", lambda e, c=c, pv=pv: e.tensor_copy(out=memk[:, c, :], in_=pv[:, 0:256]), reads=[("ps", b)], writes=["memk"])
                    attend(TP + 16 * bq, 16, 2, memk, memv, "memk")
            if xon("xo"):
                linear_fm(lambda mg: w_xo[l][:, mg * 512:(mg + 1) * 512], hT, "hT", 8, 8, tilesR, add_to_x(1.0))

        def final_norm(ps_, tiles):
            pi = ps_["xidx"]
            if pi < 2:
                return

            def out_fn(ti, c0, n, tb, tkey):
                for k in range(8):
                    sj = k % 2
                    P.op("dve", lambda e, k=k, sj=sj: e.scalar_tensor_tensor(out=stg[:, sj, 0:n], in0=xT[:, k, c0:c0 + n], scalar=smc(SM_GFINAL + k),
                                                                            in1=tb[:, 0:n], op0=ALU.mult, op1=ALU.mult),
                         reads=[("xT", ti), tkey, "small"], writes=[("stg", sj)])
                    oc = (pi - 2) * TP + c0 if ti < 2 else SEG
                    P.dma("sp", o_yT[k * 128:(k + 1) * 128, oc:oc + n], stg[:, sj, 0:n], reads=[("stg", sj)], writes=[("o_y", pi, ti, k)])
            rmsnorm(tiles, 0, out_fn)

        ALLT = [0, 1]
        passes = [
            dict(name="H1", xidx=0, A=[[0, 1], [1], [], []], R=[[1], [], [], []]),
            dict(name="H2", xidx=1, A=[[0, 1], [0, 1], [0, 1], [1]], R=[[0, 1], [0, 1], [1], []]),
            dict(name="O1", xidx=2, A=[ALLT] * 4, R=[ALLT] * 4),
            dict(name="O2", xidx=3, A=[[0, 1, 2]] * 4, R=[[0, 1, 2]] * 4),
        ]
        for ps_ in passes:
            if DBG["passes"] is not None and ps_["name"] not in DBG["passes"]:
                continue
            st = DBG["stages"]
            on = lambda name: st is None or name in st
            allk = [("xT", 0), ("xT", 1)]
            P.dma("sp", xT[:, :, 0:TP], xT_in[ps_["xidx"]].rearrange("(c p) n -> p c n", p=128), writes=allk)
            if ps_["name"] == "O2":
                P.dma("sp", xT[:, :, TP:TOK], xsT_in.rearrange("(c p) n -> p c n", p=128), writes=[("xT", 2)])
            for l in range(L):
                A = ps_["A"][l]
                R = ps_["R"][l]
                if not A:
                    continue
                if on("ffn1"):
                    ffn(l, w_g1, w_u1, w_d1, G_FFN1, A)
                if on("win"):
                    rmsnorm(A, gcol(l, G_MIX, 0))
                    if DBG.get("sub") is None or "cin" in DBG["sub"]:
                        carry_in(l, ps_, R)
                    w_in_stage(l, ps_, A)
                    if DBG.get("sub") is None or "cout" in DBG["sub"]:
                        carry_out(l, ps_, A)
                if not R:
                    continue
                P.dma("pool", poolw[:, :, :], poolw_in[l], writes=["poolw"])
                if on("attn"):
                    attn_prompt(l, ps_, R)
                if on("sattn"):
                    attn_sample(l, ps_, R)
                if on("xattn"):
                    xattn_mem(l, ps_, R)
                if on("bc"):
                    bc_stage(l, ps_, R)
                if on("wout"):
                    linear_fm(lambda mg: w_out[l][:, mg * 512:(mg + 1) * 512], hT, "hT", 8, 8, R, add_to_x(1.0))
                if on("xattn"):
                    xattn(l, ps_, R)
                if on("ffn2"):
                    ffn(l, w_g2, w_u2, w_d2, G_FFN2, R)
            final_norm(ps_, ps_["A"][L - 1])
        P.final_waits()
        P.emit(block)
    return nc


def _fm(v):
    return np.ascontiguousarray(v.reshape(8, 128).T)


def kernel(x_prompt, x_sample, mem_prompt, cache_attn_k, cache_attn_v, state_conv, state_pool,
           cache_mem_k, cache_mem_v, g_ffn1, w_ffn1_gate, w_ffn1_up, w_ffn1_down, g_mix, w_in,
           rel_bias, conv_w, pool_w, pool_scale, g_heads, w_out, g_xattn, g_mem, w_xq, w_xk, w_xv,
           w_xo, g_ffn2, w_ffn2_gate, w_ffn2_up, w_ffn2_down, g_final):
    f32 = np.float32
    A = lambda a: np.ascontiguousarray(np.asarray(a, dtype=f32))
    x_prompt, x_sample, mem_prompt = A(x_prompt), A(x_sample), A(mem_prompt)
    cache_attn_k, cache_attn_v = A(cache_attn_k), A(cache_attn_v)
    cache_mem_k, cache_mem_v = A(cache_mem_k), A(cache_mem_v)
    state_conv, state_pool = A(state_conv), A(state_pool)
    rel_bias = A(rel_bias)

    cbf = np.zeros((128, 7, 128), f32)
    cbf[:, CB_IDENT] = np.eye(128, dtype=f32)
    cbf[0:64, CB_BLK64, 0:64] = 1.0 / 64
    cbf[64:128, CB_BLK64, 64:128] = 1.0 / 64
    cbf[:, CB_MEAN] = 1.0 / 1024
    cbf[:, CB_ONES] = 1.0
    cbf[0:64, CB_SM0, 64:128] = NEGM
    cbf[64:128, CB_SM4, 0:64] = NEGM
    cbf = cbf.reshape(128, 7 * 128)

    kj = np.arange(640)[:, None]
    idx_p = np.clip(512 + np.arange(128)[None, :] - kj, -128, 128) + 128
    bias_t = np.ascontiguousarray(rel_bias[:, :, idx_p].transpose(0, 2, 1, 3))
    idx_s = np.clip(512 + np.arange(16)[None, :] - kj, -128, 128) + 128
    sbias_t = np.ascontiguousarray(rel_bias[:, :, idx_s].transpose(0, 2, 1, 3))

    pw = np.zeros((L, 128, 2, 128), f32)
    for c in range(2):
        for g in range(2):
            pw[:, g * 64:(g + 1) * 64, c, g * 64:(g + 1) * 64] = np.asarray(pool_w, f32)[:, 2 * c + g]

    gl = [g_ffn1, g_mix, g_heads, g_xattn, g_mem, g_ffn2]
    windows = [2, 4, 8, 16]
    in_maps = []
    shared = dict(
        cbf_in=cbf, bias_in=bias_t, sbias_in=sbias_t, poolw_in=pw,
        w_g1=A(w_ffn1_gate), w_u1=A(w_ffn1_up), w_d1=A(w_ffn1_down),
        w_g2=A(w_ffn2_gate), w_u2=A(w_ffn2_up), w_d2=A(w_ffn2_down),
        w_in=A(w_in), w_out=A(w_out), w_xq=A(w_xq), w_xk=A(w_xk), w_xv=A(w_xv), w_xo=A(w_xo),
    )
    for c in range(NCORE):
        sq, seg = c // 4, c % 4
        S = seg * SEG
        small = np.zeros((128, SM_N), f32)
        for l in range(L):
            for wi, g in enumerate(gl):
                small[:, gcol(l, wi, 0):gcol(l, wi, 0) + 8] = _fm(np.asarray(g, f32)[l])
            for i in range(3):
                cwv = np.asarray(conv_w, f32)[l, i].reshape(2, 128).T
                small[:, SM_CONV + (l * 3 + i) * 2:SM_CONV + (l * 3 + i) * 2 + 2] = cwv
            small[:, SM_PSCALE + l * 2:SM_PSCALE + l * 2 + 2] = np.asarray(pool_scale, f32)[l].reshape(2, 128).T
        small[:, SM_GFINAL:SM_GFINAL + 8] = _fm(np.asarray(g_final, f32))
        for ch in range(2):
            for g in range(2):
                w = windows[2 * ch + g]
                small[g * 64:(g + 1) * 64, SM_INVW + ch] = 1.0 / w
                for pos in range(16):
                    cnt = min(w, pos + 1) if seg == 0 else w
                    small[g * 64:(g + 1) * 64, SM_INVCNT + ch * 16 + pos] = 1.0 / cnt
        small[:, SM_HMASK] = NEGM if seg == 0 else 0.0
        small[:, SM_TFLAG] = 0.0 if seg == 0 else 1.0
        small[:, SM_EPS] = EPS
        small[:, SM_ZERO] = 0.0

        xT = np.zeros((4, D, TP), f32)
        for p in range(4):
            t0 = S - 2048 + p * TP
            if t0 >= 0:
                xT[p] = x_prompt[sq, t0:t0 + TP, :].T
        bs = slice(4 * c, 4 * c + 4)
        xs = x_sample[bs].reshape(NS, D)
        shu = np.zeros((L, 128, 2, 4, 16), f32)
        shp = np.zeros((L, 128, 2, 4, 16), f32)
        sc = state_conv[:, bs]
        sp_ = state_pool[:, bs]
        shu[:, :, :, :, 14:16] = sc.reshape(L, 4, 2, 2, 128).transpose(0, 4, 3, 1, 2)
        shp[:, :, :, :, 1:16] = sp_.reshape(L, 4, 15, 2, 128).transpose(0, 4, 3, 1, 2)
        m = dict(shared)
        m.update(
            xT_in=xT, xsT_in=np.ascontiguousarray(xs.T), memT_in=np.ascontiguousarray(mem_prompt[sq].T),
            small_in=small, shu_in=shu, shp_in=shp,
            ck_in=np.ascontiguousarray(cache_attn_k[:, bs].reshape(L, 4, 512, 512)),
            cv_in=np.ascontiguousarray(cache_attn_v[:, bs].reshape(L, 4, 512, 512)),
            cmk_in=np.ascontiguousarray(cache_mem_k[:, bs].reshape(L, 4, 256, 1024)),
            cmv_in=np.ascontiguousarray(cache_mem_v[:, bs].reshape(L, 4, 256, 1024)),
        )
        in_maps.append(m)

    nc = build_program()
    res = run_bass_kernel_spmd(nc, in_maps, core_ids=list(range(NCORE)))
    R = res.results

    y_prompt = np.zeros((2, 8192, D), f32)
    y_sample = np.zeros((32, 16, D), f32)
    nk_p = np.zeros((L, 2, 512, 8, 64), f32); nv_p = np.zeros_like(nk_p)
    nc_p = np.zeros((L, 2, 2, 256), f32); np_p = np.zeros((L, 2, 15, 256), f32)
    mk_p = np.zeros((L, 2, 256, 4, 256), f32); mv_p = np.zeros_like(mk_p)
    nk_s = np.zeros((L, 32, 512, 8, 64), f32); nv_s = np.zeros_like(nk_s)
    nc_s = np.zeros((L, 32, 2, 256), f32); np_s = np.zeros((L, 32, 15, 256), f32)
    for c in range(NCORE):
        sq, seg = c // 4, c % 4
        r = R[c]
        yT = np.asarray(r["o_yT"])
        y_prompt[sq, seg * SEG:(seg + 1) * SEG] = yT[:, 0:SEG].T
        y_sample[4 * c:4 * c + 4] = yT[:, SEG:SEG + NS].T.reshape(4, 16, D)
        if seg == 3:
            nk_p[:, sq] = np.asarray(r["o_pk"]).reshape(L, 512, 8, 64)
            nv_p[:, sq] = np.asarray(r["o_pv"]).reshape(L, 512, 8, 64)
            pc = np.asarray(r["o_pc"])
            pp = np.asarray(r["o_pp"])
            nc_p[:, sq] = pc[:, :, :, 14:16].transpose(0, 3, 2, 1).reshape(L, 2, 256)
            np_p[:, sq] = pp[:, :, :, 1:16].transpose(0, 3, 2, 1).reshape(L, 15, 256)
        if seg == 0:
            mk_p[:, sq] = np.asarray(r["o_mkT"]).transpose(0, 2, 1).reshape(L, 256, 4, 256)
            mv_p[:, sq] = np.asarray(r["o_mv"]).reshape(L, 256, 4, 256)
        nk_s[:, 4 * c:4 * c + 4] = np.asarray(r["o_sk"]).reshape(L, 4, 512, 8, 64)
        nv_s[:, 4 * c:4 * c + 4] = np.asarray(r["o_sv"]).reshape(L, 4, 512, 8, 64)
        scc = np.asarray(r["o_sc"])
        spp = np.asarray(r["o_sp"])
        nc_s[:, 4 * c:4 * c + 4] = scc[:, :, :, :, 14:16].transpose(0, 3, 4, 2, 1).reshape(L, 4, 2, 256)
        np_s[:, 4 * c:4 * c + 4] = spp[:, :, :, :, 1:16].transpose(0, 3, 4, 2, 1).reshape(L, 4, 15, 256)
    return (y_prompt, y_sample, nk_p, nv_p, nc_p, np_p, mk_p, mv_p, nk_s, nv_s, nc_s, np_s)
```

```python
import numpy as np
import concourse.bass as bass
import concourse.mybir as mybir
from concourse.bass_utils import run_bass_kernel_spmd

F32 = mybir.dt.float32
BF16 = mybir.dt.bfloat16
AF = mybir.ActivationFunctionType
ALU = mybir.AluOpType

L = 4
D = 1024
DFF = 2816
NCORE = 8
SEG = 2048
TP = 1024
NS = 64
TOK = TP + NS
WBC = 16 + TP + 128
KTW = 512 + TOK
EPS = 1e-6
NEGM = -30000.0
TILES = [(0, 512), (512, 512), (1024, 64)]
DBG = dict(passes=None, stages=None)

G_FFN1, G_MIX, G_HEADS, G_XATTN, G_MEM, G_FFN2 = range(6)
def gcol(l, which, c): return (l * 6 + which) * 8 + c
SM_GFINAL = L * 6 * 8
SM_CONV = SM_GFINAL + 8
SM_PSCALE = SM_CONV + L * 3 * 2
SM_INVW = SM_PSCALE + L * 2
SM_INVCNT = SM_INVW + 2
SM_HMASK = SM_INVCNT + 32
SM_TFLAG = SM_HMASK + 1
SM_EPS = SM_TFLAG + 1
SM_ZERO = SM_EPS + 1
SM_N = SM_ZERO + 1
CB_IDENT, CB_BLK64, CB_MEAN, CB_ONES, CB_ZERO, CB_SM0, CB_SM4 = range(7)


class Prog:
    ENG = ["pe", "act", "dve", "pool", "sp"]

    def __init__(self, nc, sems, dma_sems):
        self.nc = nc
        self.ops = {e: [] for e in self.ENG}
        self.cnt = {e: 0 for e in self.ENG}
        self.sem = sems
        self.dma_sems = dma_sems
        self.dma_cnt = {q: [0] * len(dma_sems[q]) for q in dma_sems}
        self.dma_rr = {q: 0 for q in dma_sems}
        self.waited = {e: {} for e in self.ENG}
        self.last_w = {}
        self.last_r = {}

    def _deps(self, eng, reads, writes):
        toks = []
        for r in reads:
            t = self.last_w.get(r)
            if t is not None:
                toks.append(t)
        for w in writes:
            t = self.last_w.get(w)
            if t is not None:
                toks.append(t)
            toks.extend(self.last_r.get(w, []))
        need = {}
        for (key, semh, val, src) in toks:
            if src == eng and eng == "pe":
                continue
            if self.waited[eng].get(key, 0) >= val:
                continue
            if need.get(key, (None, 0))[1] < val:
                need[key] = (semh, val)
        for key, (semh, val) in need.items():
            self.waited[eng][key] = val
        return list(need.values())

    def _record(self, tok, reads, writes):
        for w in writes:
            self.last_w[w] = tok
            self.last_r[w] = []
        for r in reads:
            self.last_r.setdefault(r, []).append(tok)

    def op(self, eng, fn, reads=(), writes=()):
        waits = self._deps(eng, reads, writes)
        self.cnt[eng] += 1
        tok = (eng, self.sem[eng], self.cnt[eng], eng)
        self.ops[eng].append((waits, fn, (self.sem[eng], 1)))
        self._record(tok, reads, writes)

    def dma(self, q, out, in_, reads=(), writes=()):
        i = self.dma_rr[q]
        self.dma_rr[q] = (i + 1) % len(self.dma_sems[q])
        semh = self.dma_sems[q][i]
        key = (q, i)
        waits = self._deps(q, reads, writes)
        prev = self.dma_cnt[q][i]
        if prev > 0 and self.waited[q].get(key, 0) < prev:
            waits.append((semh, prev))
            self.waited[q][key] = prev
        self.dma_cnt[q][i] = prev + 16
        tok = (key, semh, prev + 16, None)
        self.ops[q].append((waits, lambda e, o=out, s=in_: e.dma_start(out=o, in_=s), (semh, 16)))
        self._record(tok, reads, writes)

    def barrier(self):
        for e in self.ENG:
            waits = []
            for o in self.ENG:
                if o != e and self.cnt[o] > 0 and self.waited[e].get(o, 0) < self.cnt[o]:
                    waits.append((self.sem[o], self.cnt[o]))
                    self.waited[e][o] = self.cnt[o]
            for q in self.dma_sems:
                for i, semh in enumerate(self.dma_sems[q]):
                    c = self.dma_cnt[q][i]
                    if c > 0 and self.waited[e].get((q, i), 0) < c:
                        waits.append((semh, c))
                        self.waited[e][(q, i)] = c
            if waits:
                self.ops[e].append((waits, None, None))

    def final_waits(self):
        waits = []
        for q in self.dma_sems:
            for i, semh in enumerate(self.dma_sems[q]):
                if self.dma_cnt[q][i] > 0:
                    waits.append((semh, self.dma_cnt[q][i]))
        for e in self.ENG:
            if self.cnt[e] > 0 and e != "sp":
                waits.append((self.sem[e], self.cnt[e]))
        self.ops["sp"].append((waits, None, None))

    def emit(self, block):
        handles = {"pe": block.tensor, "act": block.scalar, "dve": block.vector,
                   "pool": block.gpsimd, "sp": block.sync}
        for eng in self.ENG:
            ops = self.ops[eng]
            if not ops:
                continue

            def body(e, ops=ops):
                for waits, fn, inc in ops:
                    for semh, val in waits:
                        e.wait_ge(semh, val)
                    if fn is not None:
                        ins = fn(e)
                        ins.then_inc(inc[0], inc[1])
            handles[eng](body)


def build_program():
    nc = bass.Bass("TRN2", target_bir_lowering=False)
    dt = nc.dram_tensor

    def din(name, shape, dtype=F32):
        return dt(name, list(shape), dtype, kind="ExternalInput").ap()

    def dout(name, shape, dtype=F32):
        return dt(name, list(shape), dtype, kind="ExternalOutput").ap()

    xT_in = din("xT_in", [4, D, TP])
    xsT_in = din("xsT_in", [D, NS])
    memT_in = din("memT_in", [D, 256])
    small_in = din("small_in", [128, SM_N])
    cbf_in = din("cbf_in", [128, 7 * 128])
    bias_in = din("bias_in", [L, 640, 8, 128])
    sbias_in = din("sbias_in", [L, 640, 8, 16])
    poolw_in = din("poolw_in", [L, 128, 2, 128])
    shu_in = din("shu_in", [L, 128, 2, 4, 16])
    shp_in = din("shp_in", [L, 128, 2, 4, 16])
    ck_in = din("ck_in", [L, 4, 512, 512])
    cv_in = din("cv_in", [L, 4, 512, 512])
    cmk_in = din("cmk_in", [L, 4, 256, 1024])
    cmv_in = din("cmv_in", [L, 4, 256, 1024])
    w_g1 = din("w_g1", [L, D, DFF]); w_u1 = din("w_u1", [L, D, DFF]); w_d1 = din("w_d1", [L, DFF, D])
    w_g2 = din("w_g2", [L, D, DFF]); w_u2 = din("w_u2", [L, D, DFF]); w_d2 = din("w_d2", [L, DFF, D])
    w_in = din("w_in", [L, D, 2560]); w_out = din("w_out", [L, D, D])
    w_xq = din("w_xq", [L, D, D]); w_xk = din("w_xk", [L, D, D]); w_xv = din("w_xv", [L, D, D]); w_xo = din("w_xo", [L, D, D])

    o_yT = dout("o_yT", [D, SEG + NS])
    o_pk = dout("o_pk", [L, 512, 512]); o_pv = dout("o_pv", [L, 512, 512])
    o_pc = dout("o_pc", [L, 128, 2, 16]); o_pp = dout("o_pp", [L, 128, 2, 16])
    o_mkT = dout("o_mkT", [L, D, 256]); o_mv = dout("o_mv", [L, 256, D])
    o_sk = dout("o_sk", [L, 4, 512, 512]); o_sv = dout("o_sv", [L, 4, 512, 512])
    o_sc = dout("o_sc", [L, 128, 2, 4, 16]); o_sp = dout("o_sp", [L, 128, 2, 4, 16])

    car_k = [dt(f"car_k{l}", [128, 4, 512], BF16).ap() for l in range(L)]
    car_v = [dt(f"car_v{l}", [128, 4, 512], BF16).ap() for l in range(L)]
    car_u = [dt(f"car_u{l}", [128, 2, 16], F32).ap() for l in range(L)]
    car_p = [dt(f"car_p{l}", [128, 2, 16], F32).ap() for l in range(L)]

    import contextlib
    with contextlib.ExitStack() as es:
        def sb(name, shape, dtype):
            return es.enter_context(nc.sbuf_tensor(name, list(shape), dtype))

        xT = sb("xT", [128, 8, TOK], F32)
        hT = sb("hT", [128, 8, TOK], BF16)
        bcraw = sb("bcraw", [128, 2 * 2 * WBC * 2], BF16)
        upb = sb("upb", [128, 2, WBC], F32)
        tmpA = sb("tmpA", [128, 2, WBC], F32)
        pooled = sb("pooled", [128, 2, WBC], BF16)
        qT = sb("qT", [128, 8, TOK], BF16)
        kT = sb("kT", [128, 4, KTW], BF16)
        vtok = sb("vtok", [128, 12, 512], BF16)
        biasraw = sb("biasraw", [128, 5 * 8 * 128], BF16)
        pT = sb("pT", [128, 2, 1024], BF16)
        memk = sb("memk", [128, 8, 256], BF16)
        memv = sb("memv", [128, 2, 1024], BF16)
        svnew = sb("svnew", [128, 4, 512], BF16)
        ring = sb("ring", [128, 4, 4096], BF16)
        small = sb("small", [128, SM_N], F32)
        cbf = sb("cbf", [128, 7, 128], BF16)
        poolw = sb("poolw", [128, 2, 128], BF16)
        sqb = sb("sqb", [128, 2, 2, 512], BF16)
        nrm1 = sb("nrm1", [128, 512], F32)
        nrm2 = sb("nrm2", [128, 512], F32)
        stg = sb("stg", [128, 2, 512], F32)
        psum = es.enter_context(nc.psum_tensor("psum", [128, 8, 512], F32))

        bgcg = bcraw[:, :].bitcast(F32).rearrange("p (a c n) -> p a c n", a=2, c=2)
        bg = bgcg[:, 0]
        cg = bgcg[:, 1]
        actT = bcraw[:, 0:8 * TOK].rearrange("p (c n) -> p c n", c=8)
        bias = biasraw[:, :].rearrange("p (j h q) -> p j h q", j=5, h=8)
        skT = biasraw[:, 0:4 * 528].rearrange("p (c n) -> p c n", c=4)
        sV = biasraw[:, 2112:2112 + 2048].rearrange("p (k n) -> p k n", k=4)
        knat = pT[:, :, :].rearrange("p a n -> p (a n)").rearrange("p (k n) -> p k n", k=4)
        sbias = memv[:, 0, 0:640].rearrange("p (j h q) -> p j h q", j=5, h=8)
        spT = sqb[:, 0, :, :].rearrange("p a n -> p (a n)")[:, 0:640].rearrange("p (k h q) -> p k h q", k=5, h=8)

        sem_names = ["pe", "act", "dve", "pool", "sp"]
        sems = {e: es.enter_context(nc.semaphore("s_" + e)) for e in sem_names}
        dma_sems = {q: [es.enter_context(nc.semaphore(f"d_{q}{i}")) for i in range(n)]
                    for q, n in (("pool", 12), ("sp", 16))}
        block = es.enter_context(nc.Block())
        P = Prog(nc, sems, dma_sems)

        def smc(col, n=1, p0=0, p1=128):
            return small[p0:p1, col:col + n]

        def cb(i, p0=0, p1=128, c0=0, c1=128):
            return cbf[p0:p1, i, c0:c1]

        def bank(b):
            return psum[:, b, :]

        P.dma("sp", small[:, :], small_in[:, :], writes=["small"])
        P.dma("pool", cbf[:, :, :].rearrange("p a n -> p (a n)"), cbf_in[:, :], writes=["cbf"])

        zf = [(xT, "p a n -> p (a n)"), (hT, "p a n -> p (a n)"), (bcraw, None), (upb, "p a n -> p (a n)"), (tmpA, "p a n -> p (a n)"),
              (pooled, "p a n -> p (a n)"), (qT, "p a n -> p (a n)"), (kT, "p a n -> p (a n)"), (vtok, "p a n -> p (a n)"), (biasraw, None),
              (pT, "p a n -> p (a n)"), (memk, "p a n -> p (a n)"), (memv, "p a n -> p (a n)"), (svnew, "p a n -> p (a n)"),
              (sqb, "p a b n -> p (a b n)"), (nrm1, None), (nrm2, None), (stg, "p a n -> p (a n)")]
        for zi, (zt, pat) in enumerate(zf):
            zv = zt[:, :] if pat is None else zt[:].rearrange(pat)
            P.op("dve" if zi % 2 == 0 else "pool", lambda e, zv=zv: e.memset(zv, 0.0), writes=[("z", zi)])
        zkeys = [("z", zi) for zi in range(len(zf))]
        for l in range(L):
            P.dma("sp", car_k[l][:, :, :], kT[:, :, 0:512], reads=zkeys, writes=[("car_k", l)])
            P.dma("sp", car_v[l][:, :, :], vtok[:, 0:4, :], reads=zkeys, writes=[("car_v", l)])
            P.dma("sp", car_u[l][:, :, :], upb[:, :, 0:16], reads=zkeys, writes=[("car_u", l)])
            P.dma("sp", car_p[l][:, :, :], upb[:, :, 0:16], reads=zkeys, writes=[("car_p", l)])
        P.barrier()
        ring_i = [0]

        def load_w(src_ap, kc, ncols):
            s = ring_i[0]
            ring_i[0] = (s + 1) % 4
            dst = ring[:, s, 0:kc * ncols].rearrange("p (k n) -> p k n", k=kc)
            P.dma("pool", dst, src_ap.rearrange("(k p) n -> p k n", p=128), writes=[("ring", s)])
            return s, dst

        bank_rr = [0]

        def next_bank():
            b = bank_rr[0]
            bank_rr[0] = (b + 1) % 8
            return b

        def rmsnorm(tiles, gwhich_col0, out_fn=None):
            for ti in tiles:
                c0, n = TILES[ti]
                b = next_bank()
                for qd in range(4):
                    half = qd % 2
                    P.op("act", lambda e, half=half, qd=qd, c0=c0, n=n: e.activation(
                        out=sqb[:, half, :, 0:n], in_=xT[:, 2 * qd:2 * qd + 2, c0:c0 + n], func=AF.Square),
                        reads=[("xT", ti)], writes=[("sqb", half)])
                    for k2 in range(2):
                        k = 2 * qd + k2
                        P.op("pe", lambda e, half=half, k2=k2, k=k, b=b, n=n: e.matmul(
                            psum[:, b, 0:n], lhsT=cb(CB_MEAN), rhs=sqb[:, half, k2, 0:n], start=(k == 0), stop=(k == 7)),
                            reads=[("sqb", half), "cbf"], writes=[("ps", b)])
                tb, tkey = ((nrm1, "nrm1"), (nrm2, "nrm2"))[ti % 2]
                P.op("act", lambda e, b=b, n=n, tb=tb: e.activation(out=tb[:, 0:n], in_=psum[:, b, 0:n], func=AF.Ln,
                                                                    bias=smc(SM_EPS), scale=1.0),
                     reads=[("ps", b), "small"], writes=[tkey])
                P.op("act", lambda e, n=n, tb=tb: e.activation(out=tb[:, 0:n], in_=tb[:, 0:n], func=AF.Exp, scale=-0.5),
                     reads=[tkey], writes=[tkey])
                if out_fn is not None:
                    out_fn(ti, c0, n, tb, tkey)
                    continue
                for k in range(8):
                    P.op("dve", lambda e, k=k, c0=c0, n=n, tb=tb: e.scalar_tensor_tensor(
                        out=hT[:, k, c0:c0 + n], in0=xT[:, k, c0:c0 + n], scalar=smc(gwhich_col0 + k),
                        in1=tb[:, 0:n], op0=ALU.mult, op1=ALU.mult),
                        reads=[("xT", ti), tkey, "small"], writes=[("hT", ti)])

        def linear_fm(wsrc, src_buf, src_key, kc_n, m_chunks, tiles, evac):
            if not tiles:
                return
            for mg in range((m_chunks + 3) // 4):
                s, wt = load_w(wsrc(mg), kc_n, 512)
                for ti in tiles:
                    c0, n = TILES[ti]
                    for mi in range(min(4, m_chunks - mg * 4)):
                        m = mg * 4 + mi
                        b = next_bank()
                        for k in range(kc_n):
                            P.op("pe", lambda e, k=k, b=b, n=n, c0=c0, wt=wt, mi=mi: e.matmul(
                                psum[:, b, 0:n], lhsT=wt[:, k, mi * 128:(mi + 1) * 128], rhs=src_buf[:, k, c0:c0 + n],
                                start=(k == 0), stop=(k == kc_n - 1)),
                                reads=[("ring", s), (src_key, ti)], writes=[("ps", b)])
                        evac(m, ti, c0, n, b)

        def add_to_x(scale):
            def evac(m, ti, c0, n, b):
                P.op("dve", lambda e, m=m, c0=c0, n=n, b=b: e.scalar_tensor_tensor(
                    out=xT[:, m, c0:c0 + n], in0=psum[:, b, 0:n], scalar=scale, in1=xT[:, m, c0:c0 + n],
                    op0=ALU.mult, op1=ALU.add),
                    reads=[("ps", b), ("xT", ti)], writes=[("xT", ti)])
            return evac

        def ffn(l, wg, wu, wd, gwhich, tiles):
            if not tiles:
                return
            rmsnorm(tiles, gcol(l, gwhich, 0))
            groups = [(0, 8), (8, 16), (16, 22)]
            for (g0, g1) in groups:
                for sub in range(g0, g1, 4):
                    nch = min(4, g1 - sub)
                    sg, wgt = load_w(wg[l, :, sub * 128:(sub + nch) * 128], 8, nch * 128)
                    su, wut = load_w(wu[l, :, sub * 128:(sub + nch) * 128], 8, nch * 128)
                    for mi in range(nch):
                        mloc = sub - g0 + mi
                        for ti in tiles:
                            c0, n = TILES[ti]
                            ba = next_bank()
                            bb = next_bank()
                            for k in range(8):
                                P.op("pe", lambda e, k=k, ba=ba, n=n, c0=c0, wgt=wgt, mi=mi: e.matmul(
                                    psum[:, ba, 0:n], lhsT=wgt[:, k, mi * 128:(mi + 1) * 128], rhs=hT[:, k, c0:c0 + n],
                                    start=(k == 0), stop=(k == 7)),
                                    reads=[("ring", sg), ("hT", ti)], writes=[("ps", ba)])
                            for k in range(8):
                                P.op("pe", lambda e, k=k, bb=bb, n=n, c0=c0, wut=wut, mi=mi: e.matmul(
                                    psum[:, bb, 0:n], lhsT=wut[:, k, mi * 128:(mi + 1) * 128], rhs=hT[:, k, c0:c0 + n],
                                    start=(k == 0), stop=(k == 7)),
                                    reads=[("ring", su), ("hT", ti)], writes=[("ps", bb)])
                            P.op("act", lambda e, ba=ba, n=n: e.activation(out=nrm1[:, 0:n], in_=psum[:, ba, 0:n], func=AF.Silu),
                                 reads=[("ps", ba)], writes=["nrm1"])
                            P.op("dve", lambda e, bb=bb, n=n, c0=c0, mloc=mloc: e.tensor_tensor(
                                out=actT[:, mloc, c0:c0 + n], in0=psum[:, bb, 0:n], in1=nrm1[:, 0:n], op=ALU.mult),
                                reads=[("ps", bb), "nrm1"], writes=[("actT", ti), "bc"])
                slots = []
                for sub in range(g0, g1, 4):
                    nch = min(4, g1 - sub)
                    s, wt = load_w(wd[l, sub * 128:(sub + nch) * 128, :], nch, 1024)
                    slots.append((s, wt, nch))
                nk = g1 - g0
                for ti in tiles:
                    c0, n = TILES[ti]
                    for m2 in range(8):
                        b = next_bank()
                        kk = 0
                        for (s, wt, nch) in slots:
                            for k4 in range(nch):
                                P.op("pe", lambda e, k4=k4, kk=kk, b=b, n=n, c0=c0, wt=wt, m2=m2, nk=nk: e.matmul(
                                    psum[:, b, 0:n], lhsT=wt[:, k4, m2 * 128:(m2 + 1) * 128], rhs=actT[:, kk, c0:c0 + n],
                                    start=(kk == 0), stop=(kk == nk - 1)),
                                    reads=[("ring", s), ("actT", ti), "bc"], writes=[("ps", b)])
                                kk += 1
                        add_to_x(0.5)(m2, ti, c0, n, b)

        def bccol(ti):
            c0, n = TILES[ti]
            return 16 + c0, n

        def w_in_stage(l, ps_, tilesA):
            if not tilesA:
                return
            has_s = 2 in tilesA
            ptiles = [t for t in tilesA if t < 2]

            def bc_dst(buf, mloc, ti):
                if ti < 2:
                    c, n = bccol(ti)
                    return buf[:, mloc, c:c + n]
                return buf[:, mloc, 16 + TP:16 + TP + 128].rearrange("p (b s) -> p b s", b=4)[:, :, 16:32]

            def ps_src(b, ti, n):
                if ti < 2:
                    return psum[:, b, 0:n]
                return psum[:, b, 0:64].rearrange("p (b s) -> p b s", b=4)

            def evac_q(m, ti, c0, n, b):
                P.op("act", lambda e: e.activation(out=qT[0:64, 2 * m, c0:c0 + n], in_=psum[0:64, b, 0:n], func=AF.Copy, scale=0.125),
                     reads=[("ps", b)], writes=[("qT", ti)])
                P.op("act", lambda e: e.activation(out=qT[64:128, 2 * m + 1, c0:c0 + n], in_=psum[64:128, b, 0:n], func=AF.Copy, scale=0.125),
                     reads=[("ps", b)], writes=[("qT", ti)])

            def evac_k(m, ti, c0, n, b):
                P.op("act", lambda e: e.activation(out=kT[:, m, 512 + c0:512 + c0 + n], in_=psum[:, b, 0:n], func=AF.Copy),
                     reads=[("ps", b)], writes=[("kT", ti + 1)])

            def evac_bc(m, ti, c0, n, b):
                mm = m
                buf = bg if mm < 2 else cg
                P.op("act", lambda e: e.activation(out=bc_dst(buf, mm % 2, ti), in_=ps_src(b, ti, n), func=AF.Copy),
                     reads=[("ps", b)], writes=[("bc", ti), "bc"])

            def evac_hu(m, ti, c0, n, b):
                if m < 2:
                    P.op("dve", lambda e: e.tensor_tensor(out=bc_dst(cg, m, ti), in0=ps_src(b, ti, n), in1=bc_dst(cg, m, ti), op=ALU.mult),
                         reads=[("ps", b), ("bc", ti)], writes=[("bc", ti), "bc"])
                else:
                    P.op("act", lambda e: e.activation(out=bc_dst(upb, m - 2, ti), in_=ps_src(b, ti, n), func=AF.Copy),
                         reads=[("ps", b)], writes=[("up", ti)])

            wl = w_in[l]
            sub = DBG.get("sub")
            son = lambda nm: sub is None or nm in sub
            if son("q"):
                linear_fm(lambda mg: wl[:, 0:512], hT, "hT", 8, 4, tilesA, evac_q)
            if not son("k"):
                return
            s, wt = load_w(wl[:, 512:1024], 8, 512)
            for mi in range(4):
                for ti in tilesA:
                    c0, n = TILES[ti]
                    b = next_bank()
                    for k in range(8):
                        P.op("pe", lambda e, k=k, b=b, n=n, c0=c0, mi=mi, wt=wt: e.matmul(
                            psum[:, b, 0:n], lhsT=wt[:, k, mi * 128:(mi + 1) * 128], rhs=hT[:, k, c0:c0 + n],
                            start=(k == 0), stop=(k == 7)), reads=[("ring", s), ("hT", ti)], writes=[("ps", b)])
                    evac_k(mi, ti, c0, n, b)
            tok_out(l, ps_, s, wt, tilesA, o_pk, o_sk, False)
            if not son("v"):
                return
            s, wt = load_w(wl[:, 1024:1536], 8, 512)
            for ti in ptiles:
                c0, n = TILES[ti]
                for j in range(4):
                    b = next_bank()
                    cc = c0 + j * 128
                    for k in range(8):
                        P.op("pe", lambda e, k=k, b=b, cc=cc, wt=wt: e.matmul(
                            psum[:, b, :], lhsT=hT[:, k, cc:cc + 128], rhs=wt[:, k, :], start=(k == 0), stop=(k == 7)),
                            reads=[("ring", s), ("hT", ti)], writes=[("ps", b)])
                    vt = 4 + ti * 4 + j
                    P.op("dve", lambda e, b=b, vt=vt: e.tensor_copy(out=vtok[:, vt, :], in_=psum[:, b, :]),
                         reads=[("ps", b)], writes=[("vtok", ti + 1)])
            if has_s:
                for bq in range(4):
                    b = next_bank()
                    cc = TP + 16 * bq
                    for k in range(8):
                        P.op("pe", lambda e, k=k, b=b, cc=cc, wt=wt: e.matmul(
                            psum[0:16, b, :], lhsT=hT[:, k, cc:cc + 16], rhs=wt[:, k, :], start=(k == 0), stop=(k == 7)),
                            reads=[("ring", s), ("hT", 2)], writes=[("ps", b)])
                    P.op("dve", lambda e, b=b, bq=bq: e.tensor_copy(out=svnew[0:16, bq, :], in_=psum[0:16, b, :]),
                         reads=[("ps", b)], writes=["svnew"])
            tok_out(l, ps_, s, wt, tilesA, o_pv, o_sv, True)
            if not son("bc"):
                return
            linear_fm(lambda mg: wl[:, 1536:2048], hT, "hT", 8, 4, tilesA, evac_bc)
            if not son("hu"):
                return
            linear_fm(lambda mg: wl[:, 2048:2560], hT, "hT", 8, 4, tilesA, evac_hu)

        def tok_out(l, ps_, s, wt, tilesA, o_p, o_s, is_v):
            if ps_["name"] != "O2":
                return
            for j in range(4):
                b = next_bank()
                cc = 512 + j * 128
                for k in range(8):
                    P.op("pe", lambda e, k=k, b=b, cc=cc: e.matmul(
                        psum[:, b, :], lhsT=hT[:, k, cc:cc + 128], rhs=wt[:, k, :], start=(k == 0), stop=(k == 7)),
                        reads=[("ring", s), ("hT", 1)], writes=[("ps", b)])
                sj = j % 2
                P.op("act", lambda e, b=b, sj=sj: e.activation(out=stg[:, sj, :], in_=psum[:, b, :], func=AF.Copy),
                     reads=[("ps", b)], writes=[("stg", sj)])
                P.dma("sp", o_p[l, j * 128:(j + 1) * 128, :], stg[:, sj, :], reads=[("stg", sj)], writes=[("o_p", is_v, l, j)])
            b = next_bank()
            for k in range(8):
                P.op("pe", lambda e, k=k, b=b: e.matmul(
                    psum[0:64, b, :], lhsT=hT[:, k, TP:TP + 64], rhs=wt[:, k, :], start=(k == 0), stop=(k == 7)),
                    reads=[("ring", s), ("hT", 2)], writes=[("ps", b)])
            P.op("act", lambda e, b=b: e.activation(out=stg[0:64, 0, :], in_=psum[0:64, b, :], func=AF.Copy),
                 reads=[("ps", b)], writes=[("stg", 0)])
            for bq in range(4):
                P.dma("sp", o_s[l, bq, 496:512, :], stg[16 * bq:16 * bq + 16, 0, :], reads=[("stg", 0)],
                      writes=[("o_s", is_v, l, bq)])

        def carry_out(l, ps_, tilesA):
            if 1 not in tilesA or ps_["name"] == "O2":
                return
            P.dma("sp", car_k[l][:, :, :], kT[:, :, 512 + 512:512 + 1024], reads=[("kT", 2)], writes=[("car_k", l)])
            P.dma("sp", car_v[l][:, :, :], vtok[:, 8:12, :], reads=[("vtok", 2)], writes=[("car_v", l)])
            P.dma("sp", car_u[l][:, :, :], cg[:, :, 16 + TP - 16:16 + TP], reads=[("bc", 1), "bc"], writes=[("car_u", l)])
            P.dma("sp", car_p[l][:, :, :], upb[:, :, 16 + TP - 16:16 + TP], reads=[("up", 1)], writes=[("car_p", l)])

        def carry_in(l, ps_, tilesR):
            if 0 not in tilesR or ps_["name"] == "H1":
                return
            import os
            cv_ = os.environ.get("CIN", "k,v,u,p,f").split(",")
            if "k" in cv_:
                P.dma("sp", kT[:, :, 0:512], car_k[l][:, :, :], reads=[("car_k", l)], writes=[("kT", 0)])
            if "v" in cv_:
                P.dma("sp", vtok[:, 0:4, :], car_v[l][:, :, :], reads=[("car_v", l)], writes=[("vtok", 0)])
            if "u" in cv_:
                P.dma("sp", cg[:, :, 0:16], car_u[l][:, :, :], reads=[("car_u", l)], writes=[("bch", 0), "bc"])
            if "p" in cv_:
                P.dma("sp", upb[:, :, 0:16], car_p[l][:, :, :], reads=[("car_p", l)], writes=[("uph", 0)])
            if ps_["name"] == "O1" and "f" in cv_:
                P.op("act", lambda e: e.activation(out=cg[:, :, 0:16], in_=cg[:, :, 0:16], func=AF.Copy, scale=smc(SM_TFLAG)),
                     reads=[("bch", 0), "small"], writes=[("bch", 0), "bc"])
                P.op("act", lambda e: e.activation(out=upb[:, :, 0:16], in_=upb[:, :, 0:16], func=AF.Copy, scale=smc(SM_TFLAG)),
                     reads=[("uph", 0), "small"], writes=[("uph", 0)])

        hn_par = [0]

        def headnorm_bc(l, src, srctok, chunk0, dti, c, n, dstc0):
            for cc in range(2):
                b = next_bank()
                par = hn_par[0]
                hn_par[0] ^= 1
                tb, tkey = ((nrm1, "nrm1"), (nrm2, "nrm2"))[par]
                P.op("act", lambda e, cc=cc, par=par: e.activation(out=sqb[:, par, 0, 0:n], in_=src[:, cc, c:c + n], func=AF.Square),
                     reads=[srctok], writes=[("sqb", par)])
                P.op("pe", lambda e, b=b, par=par: e.matmul(psum[:, b, 0:n], lhsT=cb(CB_BLK64), rhs=sqb[:, par, 0, 0:n], start=True, stop=True),
                     reads=[("sqb", par), "cbf"], writes=[("ps", b)])
                P.op("act", lambda e, b=b, tb=tb: e.activation(out=tb[:, 0:n], in_=psum[:, b, 0:n], func=AF.Ln, bias=smc(SM_EPS), scale=1.0),
                     reads=[("ps", b), "small"], writes=[tkey])
                P.op("act", lambda e, tb=tb: e.activation(out=tb[:, 0:n], in_=tb[:, 0:n], func=AF.Exp, scale=-0.5), reads=[tkey], writes=[tkey])
                P.op("dve", lambda e, cc=cc, tb=tb: e.scalar_tensor_tensor(
                    out=hT[:, chunk0 + cc, dstc0:dstc0 + n], in0=src[:, cc, c:c + n], scalar=smc(gcol(l, G_HEADS, chunk0 + cc)),
                    in1=tb[:, 0:n], op0=ALU.mult, op1=ALU.mult),
                    reads=[srctok, tkey, "small"], writes=[("hT", dti)])

        def bc_stage(l, ps_, tilesR):
            if not tilesR:
                return
            pt = [t for t in tilesR if t < 2]
            has_s = 2 in tilesR
            regs = []
            if pt:
                a = 16 + TILES[pt[0]][0]
                bnd = 16 + TILES[pt[-1]][0] + TILES[pt[-1]][1]
                regs.append(("p", a, bnd))
            if has_s:
                regs.append(("s", 0, 0))
                sview_u = cg[:, :, 16 + TP:16 + TP + 128].rearrange("p c (b s) -> p c b s", b=4)
                sview_p = upb[:, :, 16 + TP:16 + TP + 128].rearrange("p c (b s) -> p c b s", b=4)
                for c in range(2):
                    P.dma("sp", sview_u[:, c, :, 0:16], shu_in[l, :, c, :, :], writes=[("bch", 2), "bc"])
                    P.dma("sp", sview_p[:, c, :, 0:16], shp_in[l, :, c, :, :], writes=[("uph", 2)])

            def view(buf, reg, c, sh, lo):
                kind, a, bnd = reg
                if kind == "p":
                    return buf[:, c, a - lo - sh:bnd - sh]
                v = buf[:, c, 16 + TP:16 + TP + 128].rearrange("p (b s) -> p b s", b=4)
                return v[:, :, 16 - lo - sh:32 - sh]

            rk = [("bc", t) for t in tilesR] + [("bch", 0), ("bch", 2), "bc"]
            uk = [("up", t) for t in tilesR] + [("uph", 0), ("uph", 2)]
            if ps_["name"] == "O2":
                P.dma("sp", o_pc[l, :, :, :], cg[:, :, 16 + TP - 16:16 + TP], reads=[("bc", 1), "bc"], writes=[("o_pc", l)])
                P.dma("sp", o_pp[l, :, :, :], upb[:, :, 16 + TP - 16:16 + TP], reads=[("up", 1)], writes=[("o_pp", l)])
                if has_s:
                    for c in range(2):
                        P.dma("sp", o_sc[l, :, c, :, :], cg[:, c, 16 + TP:16 + TP + 128].rearrange("p (b s) -> p b s", b=4)[:, :, 16:32],
                              reads=[("bc", 2), "bc"], writes=[("o_sc", l, c)])
                        P.dma("sp", o_sp[l, :, c, :, :], upb[:, c, 16 + TP:16 + TP + 128].rearrange("p (b s) -> p b s", b=4)[:, :, 16:32],
                              reads=[("up", 2)], writes=[("o_sp", l, c)])
            for reg in regs:
                for c in range(2):
                    cw = lambda i, c=c: smc(SM_CONV + (l * 3 + i) * 2 + c)
                    P.op("dve", lambda e, reg=reg, c=c, cw=cw: e.tensor_scalar(out=view(tmpA, reg, c, 0, 0), in0=view(cg, reg, c, 0, 0),
                                                                               scalar1=cw(2), scalar2=None, op0=ALU.mult),
                         reads=rk + ["small"], writes=["tmpA"])
                    for i, sh in ((1, 1), (0, 2)):
                        P.op("dve", lambda e, reg=reg, c=c, cw=cw, i=i, sh=sh: e.scalar_tensor_tensor(
                            out=view(tmpA, reg, c, 0, 0), in0=view(cg, reg, c, sh, 0), scalar=cw(i), in1=view(tmpA, reg, c, 0, 0),
                            op0=ALU.mult, op1=ALU.add), reads=rk + ["tmpA", "small"], writes=["tmpA"])
                    P.op("dve", lambda e, reg=reg, c=c: e.tensor_tensor(out=view(bg, reg, c, 0, 0), in0=view(bg, reg, c, 0, 0),
                                                                        in1=view(tmpA, reg, c, 0, 0), op=ALU.mult),
                         reads=rk + ["tmpA"], writes=[("bcb", 0), "bc"])
            for reg in regs:
                for c in range(2):
                    P.op("dve", lambda e, reg=reg, c=c: e.tensor_tensor(out=view(tmpA, reg, c, 0, 14), in0=view(upb, reg, c, 0, 14),
                                                                        in1=view(upb, reg, c, 1, 14), op=ALU.add),
                         reads=uk + [("bcb", 0)], writes=["tmpA"])
                    P.op("dve", lambda e, reg=reg, c=c: e.tensor_tensor(out=view(cg, reg, c, 0, 12), in0=view(tmpA, reg, c, 0, 12),
                                                                        in1=view(tmpA, reg, c, 2, 12), op=ALU.add),
                         reads=["tmpA", ("car_u", l), ("o_pc", l), ("o_sc", l, 0), ("o_sc", l, 1)], writes=["cgs", "bc"])
                for (p0, p1, sbuf_) in ((0, 64, tmpA), (64, 128, cg)):
                    P.op("dve", lambda e, reg=reg, p0=p0, p1=p1, sbuf_=sbuf_: e.scalar_tensor_tensor(
                        out=view(pooled, reg, 0, 0, 0)[p0:p1], in0=view(sbuf_, reg, 0, 0, 0)[p0:p1], scalar=smc(SM_INVW, 1, p0, p1),
                        in1=view(upb, reg, 0, 0, 0)[p0:p1], op0=ALU.mult, op1=ALU.subtract),
                        reads=["tmpA", "cgs", "small"] + uk, writes=["pooled"])
                P.op("dve", lambda e, reg=reg: e.tensor_tensor(out=view(tmpA, reg, 1, 0, 8), in0=view(cg, reg, 1, 0, 8),
                                                                in1=view(cg, reg, 1, 4, 8), op=ALU.add),
                     reads=["cgs", "pooled"], writes=["tmpA"])
                P.op("dve", lambda e, reg=reg: e.tensor_tensor(out=view(cg, reg, 1, 0, 0), in0=view(tmpA, reg, 1, 0, 0),
                                                                in1=view(tmpA, reg, 1, 8, 0), op=ALU.add),
                     reads=["tmpA"], writes=["cgs", "bc"])
                for (p0, p1, sbuf_) in ((0, 64, tmpA), (64, 128, cg)):
                    P.op("dve", lambda e, reg=reg, p0=p0, p1=p1, sbuf_=sbuf_: e.scalar_tensor_tensor(
                        out=view(pooled, reg, 1, 0, 0)[p0:p1], in0=view(sbuf_, reg, 1, 0, 0)[p0:p1], scalar=smc(SM_INVW + 1, 1, p0, p1),
                        in1=view(upb, reg, 1, 0, 0)[p0:p1], op0=ALU.mult, op1=ALU.subtract),
                        reads=["tmpA", "cgs", "small"] + uk, writes=["pooled"])
                if reg[0] == "p" and ps_["name"] == "O1" and reg[1] == 16 and not DBG.get("nocorr"):
                    for c, lo_buf, hi_buf in ((0, tmpA, cg), (1, tmpA, cg)):
                        for (p0, p1, sbuf_) in ((0, 64, lo_buf), (64, 128, hi_buf)):
                            P.op("dve", lambda e, c=c, p0=p0, p1=p1, sbuf_=sbuf_: e.tensor_tensor(
                                out=nrm1[p0:p1, 0:16], in0=sbuf_[p0:p1, c, 16:32], in1=small[p0:p1, SM_INVCNT + 16 * c:SM_INVCNT + 16 * c + 16], op=ALU.mult),
                                reads=["tmpA", "cgs", "small"], writes=["nrm1"])
                            P.op("dve", lambda e, c=c, p0=p0, p1=p1: e.tensor_tensor(
                                out=pooled[p0:p1, c, 16:32], in0=nrm1[p0:p1, 0:16], in1=upb[p0:p1, c, 16:32], op=ALU.subtract),
                                reads=["nrm1"] + uk, writes=["pooled"])
            for ti in tilesR:
                for c in range(2):
                    b = next_bank()
                    if ti < 2:
                        cc, n = bccol(ti)
                        rhs = pooled[:, c, cc:cc + n]
                        dst = upb[:, c, cc:cc + n]
                        src = psum[:, b, 0:n]
                    else:
                        n = 64
                        rhs = pooled[:, c, 16 + TP:16 + TP + 128].rearrange("p (b s) -> p b s", b=4)[:, :, 16:32]
                        dst = upb[:, c, 16 + TP:16 + TP + 128].rearrange("p (b s) -> p b s", b=4)[:, :, 16:32]
                        src = psum[:, b, 0:64].rearrange("p (b s) -> p b s", b=4)
                    P.op("pe", lambda e, c=c, b=b, rhs=rhs, n=n: e.matmul(psum[:, b, 0:n], lhsT=poolw[:, c, :], rhs=rhs, start=True, stop=True),
                         reads=["pooled", "poolw"], writes=[("ps", b)])
                    P.op("dve", lambda e, c=c, dst=dst, src=src: e.tensor_scalar(out=dst, in0=src, scalar1=smc(SM_PSCALE + l * 2 + c), scalar2=None, op0=ALU.mult),
                         reads=[("ps", b), "small", ("car_p", l), ("o_pp", l), ("o_sp", l, 0), ("o_sp", l, 1)], writes=[("up", ti)])
            for ti in tilesR:
                if ti < 2:
                    cc, n = bccol(ti)
                    headnorm_bc(l, bg, ("bcb", 0), 4, ti, cc, n, TILES[ti][0])
                    headnorm_bc(l, upb, ("up", ti), 6, ti, cc, n, TILES[ti][0])
                else:
                    for (src, key, ch0) in ((bg, ("bcb", 0), 4), (upb, ("up", 2), 6)):
                        for c in range(2):
                            P.op("dve", lambda e, src=src, c=c: e.tensor_copy(
                                out=tmpA[:, c, 0:64].rearrange("p (b s) -> p b s", b=4),
                                in_=src[:, c, 16 + TP:16 + TP + 128].rearrange("p (b s) -> p b s", b=4)[:, :, 16:32]),
                                reads=[key, "pooled"], writes=["tmpA"])
                        headnorm_bc(l, tmpA, "tmpA", ch0, 2, 0, 64, TP)

        def attn_epilogue(l, bn, bd, ncols, dst_cols, ti):
            w = 4 * ncols
            b = next_bank_misc()
            P.op("act", lambda e: e.activation(out=sqb[:, 1, 0, 0:w], in_=psum[:, bn, 0:w], func=AF.Square),
                 reads=[("ps", bn)], writes=[("sqb", 1)])
            P.op("act", lambda e: e.activation(out=nrm2[:, 0:w], in_=psum[:, bd, 0:w], func=AF.Square, scale=1e-3),
                 reads=[("ps", bd)], writes=["nrm2"])
            P.op("pe", lambda e: e.matmul(psum[:, b, 0:w], lhsT=cb(CB_BLK64), rhs=sqb[:, 1, 0, 0:w], start=True, stop=True),
                 reads=[("sqb", 1), "cbf"], writes=[("ps", b)])
            P.op("dve", lambda e: e.tensor_tensor(out=nrm1[:, 0:w], in0=psum[:, b, 0:w], in1=nrm2[:, 0:w], op=ALU.add),
                 reads=[("ps", b), "nrm2"], writes=["nrm1"])
            P.op("act", lambda e: e.activation(out=nrm2[:, 0:w], in_=nrm1[:, 0:w], func=AF.Ln),
                 reads=["nrm1"], writes=["nrm2"])
            P.op("act", lambda e: e.activation(out=nrm1[:, 0:w], in_=nrm2[:, 0:w], func=AF.Exp, scale=-0.5), reads=["nrm2"], writes=["nrm1"])
            for hp in range(4):
                P.op("dve", lambda e, hp=hp: e.scalar_tensor_tensor(
                    out=hT[:, hp, dst_cols:dst_cols + ncols], in0=psum[:, bn, hp * ncols:(hp + 1) * ncols],
                    scalar=smc(gcol(l, G_HEADS, hp)), in1=nrm1[:, hp * ncols:(hp + 1) * ncols], op0=ALU.mult, op1=ALU.mult),
                    reads=[("ps", bn), "nrm1", "small"], writes=[("hT", ti)])

        misc_rr = [0]

        def next_bank_misc():
            b = misc_rr[0]
            misc_rr[0] = (b + 1) % 4
            return b

        def attn_prompt(l, ps_, tilesR):
            pt = [t for t in tilesR if t < 2]
            if not pt:
                return
            P.dma("pool", biasraw[:, :].rearrange("p (j x) -> p j x", j=5),
                  bias_in[l].rearrange("(j p) h q -> p j (h q)", p=128), writes=["bias"])
            for (j, cbi) in ((0, CB_SM0), (4, CB_SM4)):
                for h in range(8):
                    P.op("dve", lambda e, j=j, h=h, cbi=cbi: e.tensor_tensor(out=bias[:, j, h, :], in0=bias[:, j, h, :], in1=cb(cbi), op=ALU.add),
                         reads=["bias", "cbf"], writes=["bias"])
            gpar = 0
            for ti in pt:
                for g4 in range(4):
                    gi = ti * 4 + g4
                    q0 = gi * 128
                    bn = 4 + 2 * gpar
                    bd = 5 + 2 * gpar
                    gpar ^= 1
                    for bz in (bn, bd):
                        P.op("pe", lambda e, bz=bz: e.matmul(psum[:, bz, :], lhsT=cb(CB_ZERO), rhs=cbf[:, 0:4, :].rearrange("p a n -> p (a n)"), start=True, stop=False),
                             reads=["cbf"], writes=[("ps", bz)])
                    def emit_S(jt, gi=gi, q0=q0, ti=ti):
                        kt = gi + jt
                        kreg = ("kT", 0) if kt < 4 else ("kT", 1 + (kt - 4) // 4)
                        masked = (ps_["name"] == "O1" and kt < 4)
                        pbuf = jt % 2
                        for hb in range(2):
                            b = next_bank_misc()
                            P.op("pe", lambda e, b=b, jt=jt, hb=hb: e.matmul(
                                psum[:, b, :], lhsT=cb(CB_IDENT), rhs=bias[:, jt, hb * 4:hb * 4 + 4, :].rearrange("p h q -> p (h q)"),
                                start=True, stop=False), reads=["bias", "cbf"], writes=[("ps", b)])
                            for h4 in range(4):
                                h = hb * 4 + h4
                                P.op("pe", lambda e, b=b, h=h, h4=h4, kt=kt, q0=q0: e.matmul(
                                    psum[:, b, h4 * 128:(h4 + 1) * 128], lhsT=kT[:, h // 2, kt * 128:(kt + 1) * 128],
                                    rhs=qT[:, h, q0:q0 + 128], start=False, stop=(h4 == 3)),
                                    reads=[kreg, ("qT", ti)], writes=[("ps", b)])
                            mcol = SM_HMASK if masked else SM_ZERO
                            P.op("act", lambda e, b=b, pbuf=pbuf, hb=hb, mcol=mcol: e.activation(
                                out=pT[:, pbuf, hb * 512:(hb + 1) * 512], in_=psum[:, b, :], func=AF.Exp, bias=smc(mcol), scale=1.0),
                                reads=[("ps", b), "small"], writes=[("pT", pbuf, hb)])

                    def emit_PV(jt, gi=gi, bn=bn, bd=bd):
                        kt = gi + jt
                        vreg = ("vtok", 0) if kt < 4 else ("vtok", 1 + (kt - 4) // 4)
                        pbuf = jt % 2
                        for h in range(8):
                            hp, r0 = h // 2, (h % 2) * 64
                            P.op("pe", lambda e, h=h, hp=hp, r0=r0, kt=kt, pbuf=pbuf, bn=bn, jt=jt: e.matmul(
                                psum[r0:r0 + 64, bn, hp * 128:(hp + 1) * 128], lhsT=vtok[:, kt, h * 64:(h + 1) * 64],
                                rhs=pT[:, pbuf, h * 128:(h + 1) * 128], start=False, stop=(jt == 4 and h >= 6)),
                                reads=[vreg, ("pT", pbuf, h // 4)], writes=[("ps", bn)])
                            P.op("pe", lambda e, h=h, hp=hp, r0=r0, pbuf=pbuf, bd=bd, jt=jt: e.matmul(
                                psum[r0:r0 + 64, bd, hp * 128:(hp + 1) * 128], lhsT=cb(CB_ONES, 0, 128, 0, 64),
                                rhs=pT[:, pbuf, h * 128:(h + 1) * 128], start=False, stop=(jt == 4 and h >= 6)),
                                reads=["cbf", ("pT", pbuf, h // 4)], writes=[("ps", bd)])

                    emit_S(0)
                    for jt in range(5):
                        if jt + 1 < 5:
                            emit_S(jt + 1)
                        emit_PV(jt)
                    attn_epilogue(l, bn, bd, 128, q0, ti)

        def attn_sample(l, ps_, tilesR):
            if 2 not in tilesR:
                return
            P.dma("pool", sbias[:, :, :, :].rearrange("p j h q -> p j (h q)"),
                  sbias_in[l].rearrange("(j p) h q -> p j (h q)", p=128), writes=["memk"])
            cs_i = 0
            for bq in range(4):
                for (src_c, dst_o) in ((ck_in, o_sk), (cv_in, o_sv)):
                    for r0 in range(0, 496, 128):
                        nr = min(128, 496 - r0)
                        sj = cs_i % 2
                        cs_i += 1
                        P.dma("sp", stg[0:nr, sj, :], src_c[l, bq, 16 + r0:16 + r0 + nr, :], writes=[("stg", sj)])
                        P.dma("sp", dst_o[l, bq, r0:r0 + nr, :], stg[0:nr, sj, :], reads=[("stg", sj)], writes=[("o_sc_rows", l, bq, r0, id(dst_o))])
                P.dma("pool", knat[:, :, :], ck_in[l, bq].rearrange("(k p) n -> p k n", p=128),
                      writes=[("pT", 0, 0), ("pT", 0, 1), ("pT", 1, 0), ("pT", 1, 1)])
                P.dma("pool", sV[:, :, :], cv_in[l, bq].rearrange("(k p) n -> p k n", p=128), writes=["bias"])
                for c in range(4):
                    b = next_bank_misc()
                    pv = psum[:, b, :].bitcast(BF16)
                    for k4 in range(4):
                        P.op("pe", lambda e, c=c, k4=k4, pv=pv: e.matmul(pv[:, k4 * 128:(k4 + 1) * 128], lhsT=knat[:, k4, c * 128:(c + 1) * 128],
                                                                         rhs=cb(CB_IDENT), start=True, stop=True, is_transpose=True),
                             reads=[("pT", 0, 0), ("pT", 0, 1), ("pT", 1, 0), ("pT", 1, 1), "cbf"], writes=[("ps", b)])
                    P.op("dve", lambda e, c=c, pv=pv: e.tensor_copy(out=skT[:, c, 0:512], in_=pv[:, 0:512]),
                         reads=[("ps", b)], writes=["bias"])
                bn, bd = 4, 5
                for bz in (bn, bd):
                    P.op("pe", lambda e, bz=bz: e.matmul(psum[:, bz, 0:64], lhsT=cb(CB_ZERO), rhs=cbf[:, 0, 0:64], start=True, stop=False),
                         reads=["cbf"], writes=[("ps", bz)])
                qc = TP + 16 * bq
                for kt in range(5):
                    b = next_bank_misc()
                    np_ = 128 if kt < 4 else 16
                    P.op("pe", lambda e, b=b, kt=kt, np_=np_: e.matmul(
                        psum[0:np_, b, 0:128], lhsT=cb(CB_IDENT, 0, np_, 0, np_), rhs=sbias[0:np_, kt, :, :].rearrange("p h q -> p (h q)"),
                        start=True, stop=False), reads=["memk", "cbf"], writes=[("ps", b)])
                    for h in range(8):
                        r0 = (h % 2) * 64
                        if kt < 4:
                            lhs = skT[:, h // 2, kt * 128:(kt + 1) * 128]
                        else:
                            lhs = kT[:, h // 2, 512 + qc:512 + qc + 16]
                        P.op("pe", lambda e, b=b, h=h, r0=r0, lhs=lhs, np_=np_, qc=qc: e.matmul(
                            psum[0:np_, b, h * 16:(h + 1) * 16], lhsT=lhs, rhs=qT[:, h, qc:qc + 16],
                            start=False, stop=(h == 7)), reads=["bias", ("kT", 3), ("qT", 2)], writes=[("ps", b)])
                    P.op("act", lambda e, b=b, kt=kt, np_=np_: e.activation(
                        out=spT[0:np_, kt, :, :].rearrange("p h q -> p (h q)"), in_=psum[0:np_, b, 0:128], func=AF.Exp),
                        reads=[("ps", b)], writes=[("sqb", 0)])
                    for h in range(8):
                        hp, r0 = h // 2, (h % 2) * 64
                        if kt < 4:
                            lv = sV[:, kt, h * 64:(h + 1) * 64]
                        else:
                            lv = svnew[0:16, bq, h * 64:(h + 1) * 64]
                        P.op("pe", lambda e, h=h, hp=hp, r0=r0, lv=lv, kt=kt, np_=np_: e.matmul(
                            psum[r0:r0 + 64, bn, hp * 16:(hp + 1) * 16], lhsT=lv, rhs=spT[0:np_, kt, h, :], start=False, stop=(kt == 4 and h >= 6)),
                            reads=["bias", "svnew", ("sqb", 0)], writes=[("ps", bn)])
                        P.op("pe", lambda e, h=h, hp=hp, r0=r0, kt=kt, np_=np_: e.matmul(
                            psum[r0:r0 + 64, bd, hp * 16:(hp + 1) * 16], lhsT=cb(CB_ONES, 0, np_, 0, 64), rhs=spT[0:np_, kt, h, :],
                            start=False, stop=(kt == 4 and h >= 6)), reads=["cbf", ("sqb", 0)], writes=[("ps", bd)])
                attn_epilogue(l, bn, bd, 16, qc, 2)

        def xattn_mem(l, ps_, tilesR):
            pt = [t for t in tilesR if t < 2]
            if pt:
                KK = [("kT", i) for i in range(4)]
                memf = kT[:, :, :].rearrange("p a n -> p (a n)").bitcast(F32)[:, 0:2048].rearrange("p (c n) -> p c n", c=8)
                P.dma("sp", memf, memT_in.rearrange("(c p) n -> p c n", p=128), writes=KK)
                b = next_bank()
                for qd in range(4):
                    half = qd % 2
                    P.op("act", lambda e, half=half, qd=qd: e.activation(out=sqb[:, half, :, 0:256], in_=memf[:, 2 * qd:2 * qd + 2, :], func=AF.Square),
                         reads=KK, writes=[("sqb", half)])
                    for k2 in range(2):
                        k = 2 * qd + k2
                        P.op("pe", lambda e, half=half, k2=k2, k=k, b=b: e.matmul(psum[:, b, 0:256], lhsT=cb(CB_MEAN), rhs=sqb[:, half, k2, 0:256],
                                                                           start=(k == 0), stop=(k == 7)),
                             reads=[("sqb", half), "cbf"], writes=[("ps", b)])
                P.op("act", lambda e, b=b: e.activation(out=nrm1[:, 0:256], in_=psum[:, b, 0:256], func=AF.Ln, bias=smc(SM_EPS), scale=1.0),
                     reads=[("ps", b), "small"], writes=["nrm1"])
                P.op("act", lambda e: e.activation(out=nrm2[:, 0:256], in_=nrm1[:, 0:256], func=AF.Exp, scale=-0.5), reads=["nrm1"], writes=["nrm2"])
                mnT = kT[:, :, :].rearrange("p a n -> p (a n)")[:, 4096:6144].rearrange("p (c n) -> p c n", c=8)
                for k in range(8):
                    P.op("dve", lambda e, k=k: e.scalar_tensor_tensor(out=mnT[:, k, :], in0=memf[:, k, :], scalar=smc(gcol(l, G_MEM, k)),
                                                                      in1=nrm2[:, 0:256], op0=ALU.mult, op1=ALU.mult),
                         reads=KK + ["nrm2", "small"], writes=KK)
                write_out = ps_["name"] == "O1"
                for mg in range(2):
                    s, wt = load_w(w_xk[l][:, mg * 512:(mg + 1) * 512], 8, 512)
                    for mi in range(4):
                        m = mg * 4 + mi
                        b = next_bank()
                        for k in range(8):
                            P.op("pe", lambda e, k=k, mi=mi, wt=wt, b=b: e.matmul(psum[:, b, 0:256], lhsT=wt[:, k, mi * 128:(mi + 1) * 128], rhs=mnT[:, k, :],
                                                                                  start=(k == 0), stop=(k == 7)),
                                 reads=[("ring", s)] + KK, writes=[("ps", b)])
                        P.op("act", lambda e, m=m, b=b: e.activation(out=memk[:, m, :], in_=psum[:, b, 0:256], func=AF.Copy),
                             reads=[("ps", b)], writes=["memk"])
                        if write_out:
                            sj = m % 2
                            P.op("dve", lambda e, b=b, sj=sj: e.tensor_copy(out=stg[:, sj, 0:256], in_=psum[:, b, 0:256]),
                                 reads=[("ps", b)], writes=[("stg", sj), ("ps", b)])
                            P.dma("sp", o_mkT[l, m * 128:(m + 1) * 128, :], stg[:, sj, 0:256], reads=[("stg", sj)], writes=[("o_mkT", l, m)])
                for mg in range(2):
                    s, wt = load_w(w_xv[l][:, mg * 512:(mg + 1) * 512], 8, 512)
                    for mt in range(2):
                        b = next_bank()
                        for k in range(8):
                            P.op("pe", lambda e, k=k, mt=mt, wt=wt, b=b: e.matmul(psum[:, b, :], lhsT=mnT[:, k, mt * 128:(mt + 1) * 128], rhs=wt[:, k, :],
                                                                                  start=(k == 0), stop=(k == 7)),
                                 reads=[("ring", s)] + KK, writes=[("ps", b)])
                        P.op("act", lambda e, mt=mt, mg=mg, b=b: e.activation(out=memv[:, mt, mg * 512:(mg + 1) * 512], in_=psum[:, b, :], func=AF.Copy),
                             reads=[("ps", b)], writes=["memk"])
                        if write_out:
                            sj = mt
                            P.op("dve", lambda e, b=b, sj=sj: e.tensor_copy(out=stg[:, sj, :], in_=psum[:, b, :]),
                                 reads=[("ps", b)], writes=[("stg", sj), ("ps", b)])
                            P.dma("sp", o_mv[l, mt * 128:(mt + 1) * 128, mg * 512:(mg + 1) * 512], stg[:, sj, :], reads=[("stg", sj)],
                                  writes=[("o_mv", l, mt, mg)])

        def xattn(l, ps_, tilesR):
            if not tilesR:
                return
            pt = [t for t in tilesR if t < 2]
            has_s = 2 in tilesR
            xsub = DBG.get("xsub")
            xon = lambda nm: xsub is None or nm in xsub
            rmsnorm(tilesR, gcol(l, G_XATTN, 0))

            def evac_qx(m, ti, c0, n, b):
                P.op("act", lambda e: e.activation(out=actT[:, m, c0:c0 + n], in_=psum[:, b, 0:n], func=AF.Copy, scale=1.0 / 16.0),
                     reads=[("ps", b)], writes=[("actT", ti), "bc"])
            linear_fm(lambda mg: w_xq[l][:, mg * 512:(mg + 1) * 512], hT, "hT", 8, 8, tilesR, evac_qx)
            if not xon("mem"):
                return

            def attend(c0, n, ti, mk, mv, mkey):
                def s_part(h):
                    ph = h % 2
                    bs = [next_bank() for _ in range(2)]
                    for mt in range(2):
                        for dc in range(2):
                            P.op("pe", lambda e, mt=mt, dc=dc, h=h, b=bs[mt]: e.matmul(
                                psum[:, b, 0:n], lhsT=mk[:, h * 2 + dc, mt * 128:(mt + 1) * 128], rhs=actT[:, h * 2 + dc, c0:c0 + n],
                                start=(dc == 0), stop=(dc == 1)), reads=[mkey, ("actT", ti), "bc"], writes=[("ps", bs[mt])])
                        P.op("act", lambda e, mt=mt, b=bs[mt], ph=ph: e.activation(out=pT[:, mt, ph * 512:ph * 512 + n], in_=psum[:, b, 0:n], func=AF.Exp),
                             reads=[("ps", bs[mt])], writes=[("pT", mt, ph)])

                def pv_part(h):
                    ph = h % 2
                    bd = next_bank()
                    bnum = [next_bank() for _ in range(2)]
                    for mt in range(2):
                        P.op("pe", lambda e, mt=mt, bd=bd, ph=ph: e.matmul(psum[:, bd, 0:n], lhsT=cb(CB_ONES), rhs=pT[:, mt, ph * 512:ph * 512 + n],
                                                                          start=(mt == 0), stop=(mt == 1)),
                             reads=["cbf", ("pT", mt, ph)], writes=[("ps", bd)])
                    for dc in range(2):
                        for mt in range(2):
                            P.op("pe", lambda e, mt=mt, dc=dc, h=h, bq_=bnum[dc], ph=ph: e.matmul(
                                psum[:, bq_, 0:n], lhsT=mv[:, mt, (h * 2 + dc) * 128:(h * 2 + dc + 1) * 128], rhs=pT[:, mt, ph * 512:ph * 512 + n],
                                start=(mt == 0), stop=(mt == 1)), reads=[mkey, ("pT", mt, ph)], writes=[("ps", bnum[dc])])
                    P.op("act", lambda e, bd=bd: e.activation(out=nrm1[:, 0:n], in_=psum[:, bd, 0:n], func=AF.Ln), reads=[("ps", bd)], writes=["nrm1"])
                    P.op("act", lambda e: e.activation(out=nrm1[:, 0:n], in_=nrm1[:, 0:n], func=AF.Exp, scale=-1.0), reads=["nrm1"], writes=["nrm1"])
                    for dc in range(2):
                        P.op("dve", lambda e, dc=dc, h=h, bq_=bnum[dc]: e.tensor_tensor(out=hT[:, h * 2 + dc, c0:c0 + n], in0=psum[:, bq_, 0:n],
                                                                                      in1=nrm1[:, 0:n], op=ALU.mult),
                             reads=[("ps", bnum[dc]), "nrm1"], writes=[("hT", ti)])

                s_part(0)
                for h in range(4):
                    if h + 1 < 4:
                        s_part(h + 1)
                    pv_part(h)

            if pt:
                for ti in pt:
                    c0, n = TILES[ti]
                    if xon("att"):
                        attend(c0, n, ti, memk, memv, "memk")
            if has_s:
                for bq in range(4):
                    P.dma("pool", memv[:, :, :], cmv_in[l, bq].rearrange("(k p) n -> p k n", p=128), writes=["memk"])
                    mkn = vtok[:, 0:4, :].rearrange("p a n -> p (a n)").rearrange("p (k n) -> p k n", k=2)
                    P.dma("pool", mkn, cmk_in[l, bq].rearrange("(k p) n -> p k n", p=128), writes=[("vtok", 0)])
                    for c in range(8):
                        b = next_bank()
                        pv = psum[:, b, :].bitcast(BF16)
                        for mt in range(2):
                            P.op("pe", lambda e, c=c, mt=mt, pv=pv: e.matmul(pv[:, mt * 128:(mt + 1) * 128], lhsT=mkn[:, mt, c * 128:(c + 1) * 128],
                                                                             rhs=cb(CB_IDENT), start=True, stop=True, is_transpose=True),
                                 reads=[("vtok", 0), "cbf"], writes=[("ps", b)])
                        P.op("dve", lambda e, c=c, pv=pv: e.tensor_copy(out=memk[:, c, :], in_=pv[:, 0:256]), reads=[("ps", b)], writes=["memk"])
                    attend(TP + 16 * bq, 16, 2, memk, memv, "memk")
            if xon("xo"):
                linear_fm(lambda mg: w_xo[l][:, mg * 512:(mg + 1) * 512], hT, "hT", 8, 8, tilesR, add_to_x(1.0))

        def final_norm(ps_, tiles):
            pi = ps_["xidx"]
            if pi < 2:
                return

            def out_fn(ti, c0, n, tb, tkey):
                for k in range(8):
                    sj = k % 2
                    P.op("dve", lambda e, k=k, sj=sj: e.scalar_tensor_tensor(out=stg[:, sj, 0:n], in0=xT[:, k, c0:c0 + n], scalar=smc(SM_GFINAL + k),
                                                                            in1=tb[:, 0:n], op0=ALU.mult, op1=ALU.mult),
                         reads=[("xT", ti), tkey, "small"], writes=[("stg", sj)])
                    oc = (pi - 2) * TP + c0 if ti < 2 else SEG
                    P.dma("sp", o_yT[k * 128:(k + 1) * 128, oc:oc + n], stg[:, sj, 0:n], reads=[("stg", sj)], writes=[("o_y", pi, ti, k)])
            rmsnorm(tiles, 0, out_fn)

        ALLT = [0, 1]
        passes = [
            dict(name="H1", xidx=0, A=[[0, 1], [1], [], []], R=[[1], [], [], []]),
            dict(name="H2", xidx=1, A=[[0, 1], [0, 1], [0, 1], [1]], R=[[0, 1], [0, 1], [1], []]),
            dict(name="O1", xidx=2, A=[ALLT] * 4, R=[ALLT] * 4),
            dict(name="O2", xidx=3, A=[[0, 1, 2]] * 4, R=[[0, 1, 2]] * 4),
        ]
        for ps_ in passes:
            if DBG["passes"] is not None and ps_["name"] not in DBG["passes"]:
                continue
            st = DBG["stages"]
            on = lambda name: st is None or name in st
            allk = [("xT", 0), ("xT", 1)]
            P.dma("sp", xT[:, :, 0:TP], xT_in[ps_["xidx"]].rearrange("(c p) n -> p c n", p=128), writes=allk)
            if ps_["name"] == "O2":
                P.dma("sp", xT[:, :, TP:TOK], xsT_in.rearrange("(c p) n -> p c n", p=128), writes=[("xT", 2)])
            for l in range(L):
                A = ps_["A"][l]
                R = ps_["R"][l]
                if not A:
                    continue
                if on("ffn1"):
                    ffn(l, w_g1, w_u1, w_d1, G_FFN1, A)
                if on("win"):
                    rmsnorm(A, gcol(l, G_MIX, 0))
                    if DBG.get("sub") is None or "cin" in DBG["sub"]:
                        carry_in(l, ps_, R)
                    w_in_stage(l, ps_, A)
                    if DBG.get("sub") is None or "cout" in DBG["sub"]:
                        carry_out(l, ps_, A)
                if not R:
                    continue
                P.dma("pool", poolw[:, :, :], poolw_in[l], writes=["poolw"])
                if on("attn"):
                    attn_prompt(l, ps_, R)
                if on("sattn"):
                    attn_sample(l, ps_, R)
                if on("xattn"):
                    xattn_mem(l, ps_, R)
                if on("bc"):
                    bc_stage(l, ps_, R)
                if on("wout"):
                    linear_fm(lambda mg: w_out[l][:, mg * 512:(mg + 1) * 512], hT, "hT", 8, 8, R, add_to_x(1.0))
                if on("xattn"):
                    xattn(l, ps_, R)
                if on("ffn2"):
                    ffn(l, w_g2, w_u2, w_d2, G_FFN2, R)
            final_norm(ps_, ps_["A"][L - 1])
        P.final_waits()
        P.emit(block)
    return nc


def _fm(v):
    return np.ascontiguousarray(v.reshape(8, 128).T)


def kernel(x_prompt, x_sample, mem_prompt, cache_attn_k, cache_attn_v, state_conv, state_pool,
           cache_mem_k, cache_mem_v, g_ffn1, w_ffn1_gate, w_ffn1_up, w_ffn1_down, g_mix, w_in,
           rel_bias, conv_w, pool_w, pool_scale, g_heads, w_out, g_xattn, g_mem, w_xq, w_xk, w_xv,
           w_xo, g_ffn2, w_ffn2_gate, w_ffn2_up, w_ffn2_down, g_final):
    f32 = np.float32
    A = lambda a: np.ascontiguousarray(np.asarray(a, dtype=f32))
    x_prompt, x_sample, mem_prompt = A(x_prompt), A(x_sample), A(mem_prompt)
    cache_attn_k, cache_attn_v = A(cache_attn_k), A(cache_attn_v)
    cache_mem_k, cache_mem_v = A(cache_mem_k), A(cache_mem_v)
    state_conv, state_pool = A(state_conv), A(state_pool)
    rel_bias = A(rel_bias)

    cbf = np.zeros((128, 7, 128), f32)
    cbf[:, CB_IDENT] = np.eye(128, dtype=f32)
    cbf[0:64, CB_BLK64, 0:64] = 1.0 / 64
    cbf[64:128, CB_BLK64, 64:128] = 1.0 / 64
    cbf[:, CB_MEAN] = 1.0 / 1024
    cbf[:, CB_ONES] = 1.0
    cbf[0:64, CB_SM0, 64:128] = NEGM
    cbf[64:128, CB_SM4, 0:64] = NEGM
    cbf = cbf.reshape(128, 7 * 128)

    kj = np.arange(640)[:, None]
    idx_p = np.clip(512 + np.arange(128)[None, :] - kj, -128, 128) + 128
    bias_t = np.ascontiguousarray(rel_bias[:, :, idx_p].transpose(0, 2, 1, 3))
    idx_s = np.clip(512 + np.arange(16)[None, :] - kj, -128, 128) + 128
    sbias_t = np.ascontiguousarray(rel_bias[:, :, idx_s].transpose(0, 2, 1, 3))

    pw = np.zeros((L, 128, 2, 128), f32)
    for c in range(2):
        for g in range(2):
            pw[:, g * 64:(g + 1) * 64, c, g * 64:(g + 1) * 64] = np.asarray(pool_w, f32)[:, 2 * c + g]

    gl = [g_ffn1, g_mix, g_heads, g_xattn, g_mem, g_ffn2]
    windows = [2, 4, 8, 16]
    in_maps = []
    shared = dict(
        cbf_in=cbf, bias_in=bias_t, sbias_in=sbias_t, poolw_in=pw,
        w_g1=A(w_ffn1_gate), w_u1=A(w_ffn1_up), w_d1=A(w_ffn1_down),
        w_g2=A(w_ffn2_gate), w_u2=A(w_ffn2_up), w_d2=A(w_ffn2_down),
        w_in=A(w_in), w_out=A(w_out), w_xq=A(w_xq), w_xk=A(w_xk), w_xv=A(w_xv), w_xo=A(w_xo),
    )
    for c in range(NCORE):
        sq, seg = c // 4, c % 4
        S = seg * SEG
        small = np.zeros((128, SM_N), f32)
        for l in range(L):
            for wi, g in enumerate(gl):
                small[:, gcol(l, wi, 0):gcol(l, wi, 0) + 8] = _fm(np.asarray(g, f32)[l])
            for i in range(3):
                cwv = np.asarray(conv_w, f32)[l, i].reshape(2, 128).T
                small[:, SM_CONV + (l * 3 + i) * 2:SM_CONV + (l * 3 + i) * 2 + 2] = cwv
            small[:, SM_PSCALE + l * 2:SM_PSCALE + l * 2 + 2] = np.asarray(pool_scale, f32)[l].reshape(2, 128).T
        small[:, SM_GFINAL:SM_GFINAL + 8] = _fm(np.asarray(g_final, f32))
        for ch in range(2):
            for g in range(2):
                w = windows[2 * ch + g]
                small[g * 64:(g + 1) * 64, SM_INVW + ch] = 1.0 / w
                for pos in range(16):
                    cnt = min(w, pos + 1) if seg == 0 else w
                    small[g * 64:(g + 1) * 64, SM_INVCNT + ch * 16 + pos] = 1.0 / cnt
        small[:, SM_HMASK] = NEGM if seg == 0 else 0.0
        small[:, SM_TFLAG] = 0.0 if seg == 0 else 1.0
        small[:, SM_EPS] = EPS
        small[:, SM_ZERO] = 0.0

        xT = np.zeros((4, D, TP), f32)
        for p in range(4):
            t0 = S - 2048 + p * TP
            if t0 >= 0:
                xT[p] = x_prompt[sq, t0:t0 + TP, :].T
        bs = slice(4 * c, 4 * c + 4)
        xs = x_sample[bs].reshape(NS, D)
        shu = np.zeros((L, 128, 2, 4, 16), f32)
        shp = np.zeros((L, 128, 2, 4, 16), f32)
        sc = state_conv[:, bs]
        sp_ = state_pool[:, bs]
        shu[:, :, :, :, 14:16] = sc.reshape(L, 4, 2, 2, 128).transpose(0, 4, 3, 1, 2)
        shp[:, :, :, :, 1:16] = sp_.reshape(L, 4, 15, 2, 128).transpose(0, 4, 3, 1, 2)
        m = dict(shared)
        m.update(
            xT_in=xT, xsT_in=np.ascontiguousarray(xs.T), memT_in=np.ascontiguousarray(mem_prompt[sq].T),
            small_in=small, shu_in=shu, shp_in=shp,
            ck_in=np.ascontiguousarray(cache_attn_k[:, bs].reshape(L, 4, 512, 512)),
            cv_in=np.ascontiguousarray(cache_attn_v[:, bs].reshape(L, 4, 512, 512)),
            cmk_in=np.ascontiguousarray(cache_mem_k[:, bs].reshape(L, 4, 256, 1024)),
            cmv_in=np.ascontiguousarray(cache_mem_v[:, bs].reshape(L, 4, 256, 1024)),
        )
        in_maps.append(m)

    nc = build_program()
    res = run_bass_kernel_spmd(nc, in_maps, core_ids=list(range(NCORE)))
    R = res.results

    y_prompt = np.zeros((2, 8192, D), f32)
    y_sample = np.zeros((32, 16, D), f32)
    nk_p = np.zeros((L, 2, 512, 8, 64), f32); nv_p = np.zeros_like(nk_p)
    nc_p = np.zeros((L, 2, 2, 256), f32); np_p = np.zeros((L, 2, 15, 256), f32)
    mk_p = np.zeros((L, 2, 256, 4, 256), f32); mv_p = np.zeros_like(mk_p)
    nk_s = np.zeros((L, 32, 512, 8, 64), f32); nv_s = np.zeros_like(nk_s)
    nc_s = np.zeros((L, 32, 2, 256), f32); np_s = np.zeros((L, 32, 15, 256), f32)
    for c in range(NCORE):
        sq, seg = c // 4, c % 4
        r = R[c]
        yT = np.asarray(r["o_yT"])
        y_prompt[sq, seg * SEG:(seg + 1) * SEG] = yT[:, 0:SEG].T
        y_sample[4 * c:4 * c + 4] = yT[:, SEG:SEG + NS].T.reshape(4, 16, D)
        if seg == 3:
            nk_p[:, sq] = np.asarray(r["o_pk"]).reshape(L, 512, 8, 64)
            nv_p[:, sq] = np.asarray(r["o_pv"]).reshape(L, 512, 8, 64)
            pc = np.asarray(r["o_pc"])
            pp = np.asarray(r["o_pp"])
            nc_p[:, sq] = pc[:, :, :, 14:16].transpose(0, 3, 2, 1).reshape(L, 2, 256)
            np_p[:, sq] = pp[:, :, :, 1:16].transpose(0, 3, 2, 1).reshape(L, 15, 256)
        if seg == 0:
            mk_p[:, sq] = np.asarray(r["o_mkT"]).transpose(0, 2, 1).reshape(L, 256, 4, 256)
            mv_p[:, sq] = np.asarray(r["o_mv"]).reshape(L, 256, 4, 256)
        nk_s[:, 4 * c:4 * c + 4] = np.asarray(r["o_sk"]).reshape(L, 4, 512, 8, 64)
        nv_s[:, 4 * c:4 * c + 4] = np.asarray(r["o_sv"]).reshape(L, 4, 512, 8, 64)
        scc = np.asarray(r["o_sc"])
        spp = np.asarray(r["o_sp"])
        nc_s[:, 4 * c:4 * c + 4] = scc[:, :, :, :, 14:16].transpose(0, 3, 4, 2, 1).reshape(L, 4, 2, 256)
        np_s[:, 4 * c:4 * c + 4] = spp[:, :, :, :, 1:16].transpose(0, 3, 4, 2, 1).reshape(L, 4, 15, 256)
    return (y_prompt, y_sample, nk_p, nv_p, nc_p, np_p, mk_p, mv_p, nk_s, nv_s, nc_s, np_s)
```

```python
import numpy as np
import concourse.bass as bass
import concourse.mybir as mybir
from concourse.bass_utils import run_bass_kernel_spmd

F32 = mybir.dt.float32
BF16 = mybir.dt.bfloat16
AF = mybir.ActivationFunctionType
ALU = mybir.AluOpType

L = 4
D = 1024
DFF = 2816
NCORE = 8
SEG = 2048
TP = 1024
NS = 64
TOK = TP + NS
WBC = 16 + TP + 128
KTW = 512 + TOK
EPS = 1e-6
NEGM = -30000.0
TILES = [(0, 512), (512, 512), (1024, 64)]
DBG = dict(passes=None, stages=None)

G_FFN1, G_MIX, G_HEADS, G_XATTN, G_MEM, G_FFN2 = range(6)
def gcol(l, which, c): return (l * 6 + which) * 8 + c
SM_GFINAL = L * 6 * 8
SM_CONV = SM_GFINAL + 8
SM_PSCALE = SM_CONV + L * 3 * 2
SM_INVW = SM_PSCALE + L * 2
SM_INVCNT = SM_INVW + 2
SM_HMASK = SM_INVCNT + 32
SM_TFLAG = SM_HMASK + 1
SM_EPS = SM_TFLAG + 1
SM_ZERO = SM_EPS + 1
SM_N = SM_ZERO + 1
CB_IDENT, CB_BLK64, CB_MEAN, CB_ONES, CB_ZERO, CB_SM0, CB_SM4 = range(7)


class Prog:
    ENG = ["pe", "act", "dve", "pool", "sp"]

    def __init__(self, nc, sems, dma_sems):
        self.nc = nc
        self.ops = {e: [] for e in self.ENG}
        self.cnt = {e: 0 for e in self.ENG}
        self.sem = sems
        self.dma_sems = dma_sems
        self.dma_cnt = {q: [0] * len(dma_sems[q]) for q in dma_sems}
        self.dma_rr = {q: 0 for q in dma_sems}
        self.waited = {e: {} for e in self.ENG}
        self.last_w = {}
        self.last_r = {}

    def _deps(self, eng, reads, writes):
        toks = []
        for r in reads:
            t = self.last_w.get(r)
            if t is not None:
                toks.append(t)
        for w in writes:
            t = self.last_w.get(w)
            if t is not None:
                toks.append(t)
            toks.extend(self.last_r.get(w, []))
        need = {}
        for (key, semh, val, src) in toks:
            if src == eng and eng == "pe":
                continue
            if self.waited[eng].get(key, 0) >= val:
                continue
            if need.get(key, (None, 0))[1] < val:
                need[key] = (semh, val)
        for key, (semh, val) in need.items():
            self.waited[eng][key] = val
        return list(need.values())

    def _record(self, tok, reads, writes):
        for w in writes:
            self.last_w[w] = tok
            self.last_r[w] = []
        for r in reads:
            self.last_r.setdefault(r, []).append(tok)

    def op(self, eng, fn, reads=(), writes=(), inc=True):
        waits = self._deps(eng, reads, writes)
        if inc or eng != "pe":
            self.cnt[eng] += 1
            val = self.cnt[eng]
            self.ops[eng].append((waits, fn, (self.sem[eng], 1)))
        else:
            val = self.cnt[eng] + 1
            self.ops[eng].append((waits, fn, None))
        tok = (eng, self.sem[eng], val, eng)
        self._record(tok, reads, writes)

    def dma(self, q, out, in_, reads=(), writes=()):
        i = self.dma_rr[q]
        self.dma_rr[q] = (i + 1) % len(self.dma_sems[q])
        semh = self.dma_sems[q][i]
        key = (q, i)
        waits = self._deps(q, reads, writes)
        prev = self.dma_cnt[q][i]
        if prev > 0 and self.waited[q].get(key, 0) < prev:
            waits.append((semh, prev))
            self.waited[q][key] = prev
        self.dma_cnt[q][i] = prev + 16
        tok = (key, semh, prev + 16, None)
        self.ops[q].append((waits, lambda e, o=out, s=in_: e.dma_start(out=o, in_=s), (semh, 16)))
        self._record(tok, reads, writes)

    def barrier(self):
        for e in self.ENG:
            waits = []
            for o in self.ENG:
                if o != e and self.cnt[o] > 0 and self.waited[e].get(o, 0) < self.cnt[o]:
                    waits.append((self.sem[o], self.cnt[o]))
                    self.waited[e][o] = self.cnt[o]
            for q in self.dma_sems:
                for i, semh in enumerate(self.dma_sems[q]):
                    c = self.dma_cnt[q][i]
                    if c > 0 and self.waited[e].get((q, i), 0) < c:
                        waits.append((semh, c))
                        self.waited[e][(q, i)] = c
            if waits:
                self.ops[e].append((waits, None, None))

    def final_waits(self):
        waits = []
        for q in self.dma_sems:
            for i, semh in enumerate(self.dma_sems[q]):
                if self.dma_cnt[q][i] > 0:
                    waits.append((semh, self.dma_cnt[q][i]))
        for e in self.ENG:
            if self.cnt[e] > 0 and e != "sp":
                waits.append((self.sem[e], self.cnt[e]))
        self.ops["sp"].append((waits, None, None))

    def emit(self, block):
        handles = {"pe": block.tensor, "act": block.scalar, "dve": block.vector,
                   "pool": block.gpsimd, "sp": block.sync}
        for eng in self.ENG:
            ops = self.ops[eng]
            if not ops:
                continue

            def body(e, ops=ops):
                for waits, fn, inc in ops:
                    for semh, val in waits:
                        e.wait_ge(semh, val)
                    if fn is not None:
                        ins = fn(e)
                        if inc is not None:
                            ins.then_inc(inc[0], inc[1])
            handles[eng](body)


def build_program():
    nc = bass.Bass("TRN2", target_bir_lowering=False)
    dt = nc.dram_tensor

    def din(name, shape, dtype=F32):
        return dt(name, list(shape), dtype, kind="ExternalInput").ap()

    def dout(name, shape, dtype=F32):
        return dt(name, list(shape), dtype, kind="ExternalOutput").ap()

    xT_in = din("xT_in", [4, D, TP])
    xsT_in = din("xsT_in", [D, NS])
    memT_in = din("memT_in", [D, 256])
    small_in = din("small_in", [128, SM_N])
    cbf_in = din("cbf_in", [128, 7 * 128])
    bias_in = din("bias_in", [L, 640, 8, 128])
    sbias_in = din("sbias_in", [L, 640, 8, 16])
    poolw_in = din("poolw_in", [L, 128, 2, 128])
    shu_in = din("shu_in", [L, 128, 2, 4, 16])
    shp_in = din("shp_in", [L, 128, 2, 4, 16])
    ck_in = din("ck_in", [L, 4, 512, 512])
    cv_in = din("cv_in", [L, 4, 512, 512])
    cmk_in = din("cmk_in", [L, 4, 256, 1024])
    cmv_in = din("cmv_in", [L, 4, 256, 1024])
    w_g1 = din("w_g1", [L, D, DFF]); w_u1 = din("w_u1", [L, D, DFF]); w_d1 = din("w_d1", [L, DFF, D])
    w_g2 = din("w_g2", [L, D, DFF]); w_u2 = din("w_u2", [L, D, DFF]); w_d2 = din("w_d2", [L, DFF, D])
    w_in = din("w_in", [L, D, 2560]); w_out = din("w_out", [L, D, D])
    w_xq = din("w_xq", [L, D, D]); w_xk = din("w_xk", [L, D, D]); w_xv = din("w_xv", [L, D, D]); w_xo = din("w_xo", [L, D, D])

    o_yT = dout("o_yT", [D, SEG + NS])
    o_pk = dout("o_pk", [L, 512, 512]); o_pv = dout("o_pv", [L, 512, 512])
    o_pc = dout("o_pc", [L, 128, 2, 16]); o_pp = dout("o_pp", [L, 128, 2, 16])
    o_mkT = dout("o_mkT", [L, D, 256]); o_mv = dout("o_mv", [L, 256, D])
    o_sk = dout("o_sk", [L, 4, 512, 512]); o_sv = dout("o_sv", [L, 4, 512, 512])
    o_sc = dout("o_sc", [L, 128, 2, 4, 16]); o_sp = dout("o_sp", [L, 128, 2, 4, 16])

    car_k = [dt(f"car_k{l}", [128, 4, 512], BF16).ap() for l in range(L)]
    car_v = [dt(f"car_v{l}", [128, 4, 512], BF16).ap() for l in range(L)]
    car_u = [dt(f"car_u{l}", [128, 2, 16], F32).ap() for l in range(L)]
    car_p = [dt(f"car_p{l}", [128, 2, 16], F32).ap() for l in range(L)]

    import contextlib
    with contextlib.ExitStack() as es:
        def sb(name, shape, dtype):
            return es.enter_context(nc.sbuf_tensor(name, list(shape), dtype))

        xT = sb("xT", [128, 8, TOK], F32)
        hT = sb("hT", [128, 8, TOK], BF16)
        bcraw = sb("bcraw", [128, 2 * 2 * WBC * 2], BF16)
        upb = sb("upb", [128, 2, WBC], F32)
        tmpA = sb("tmpA", [128, 2, WBC], F32)
        pooled = sb("pooled", [128, 2, WBC], BF16)
        qT = sb("qT", [128, 8, TOK], BF16)
        kT = sb("kT", [128, 4, KTW], BF16)
        vtok = sb("vtok", [128, 12, 512], BF16)
        biasraw = sb("biasraw", [128, 5 * 8 * 128], BF16)
        pT = sb("pT", [128, 2, 1024], BF16)
        memk = sb("memk", [128, 8, 256], BF16)
        memv = sb("memv", [128, 2, 1024], BF16)
        svnew = sb("svnew", [128, 4, 512], BF16)
        ring = sb("ring", [128, 4, 4096], BF16)
        small = sb("small", [128, SM_N], F32)
        cbf = sb("cbf", [128, 7, 128], BF16)
        poolw = sb("poolw", [128, 2, 128], BF16)
        sqb = sb("sqb", [128, 2, 2, 512], BF16)
        nrm1 = sb("nrm1", [128, 512], F32)
        nrm2 = sb("nrm2", [128, 512], F32)
        stg = sb("stg", [128, 2, 512], F32)
        psum = es.enter_context(nc.psum_tensor("psum", [128, 8, 512], F32))

        bgcg = bcraw[:, :].bitcast(F32).rearrange("p (a c n) -> p a c n", a=2, c=2)
        bg = bgcg[:, 0]
        cg = bgcg[:, 1]
        actT = bcraw[:, 0:8 * TOK].rearrange("p (c n) -> p c n", c=8)
        bias = biasraw[:, :].rearrange("p (j h q) -> p j h q", j=5, h=8)
        skT = biasraw[:, 0:4 * 528].rearrange("p (c n) -> p c n", c=4)
        sV = biasraw[:, 2112:2112 + 2048].rearrange("p (k n) -> p k n", k=4)
        knat = pT[:, :, :].rearrange("p a n -> p (a n)").rearrange("p (k n) -> p k n", k=4)
        sbias = memv[:, 0, 0:640].rearrange("p (j h q) -> p j h q", j=5, h=8)
        spT = sqb[:, 0, :, :].rearrange("p a n -> p (a n)")[:, 0:640].rearrange("p (k h q) -> p k h q", k=5, h=8)

        sem_names = ["pe", "act", "dve", "pool", "sp"]
        sems = {e: es.enter_context(nc.semaphore("s_" + e)) for e in sem_names}
        dma_sems = {q: [es.enter_context(nc.semaphore(f"d_{q}{i}")) for i in range(n)]
                    for q, n in (("pool", 12), ("sp", 16))}
        block = es.enter_context(nc.Block())
        P = Prog(nc, sems, dma_sems)

        def smc(col, n=1, p0=0, p1=128):
            return small[p0:p1, col:col + n]

        def cb(i, p0=0, p1=128, c0=0, c1=128):
            return cbf[p0:p1, i, c0:c1]

        def bank(b):
            return psum[:, b, :]

        P.dma("sp", small[:, :], small_in[:, :], writes=["small"])
        P.dma("pool", cbf[:, :, :].rearrange("p a n -> p (a n)"), cbf_in[:, :], writes=["cbf"])

        zf = [(xT, "p a n -> p (a n)"), (hT, "p a n -> p (a n)"), (bcraw, None), (upb, "p a n -> p (a n)"), (tmpA, "p a n -> p (a n)"),
              (pooled, "p a n -> p (a n)"), (qT, "p a n -> p (a n)"), (kT, "p a n -> p (a n)"), (vtok, "p a n -> p (a n)"), (biasraw, None),
              (pT, "p a n -> p (a n)"), (memk, "p a n -> p (a n)"), (memv, "p a n -> p (a n)"), (svnew, "p a n -> p (a n)"),
              (sqb, "p a b n -> p (a b n)"), (nrm1, None), (nrm2, None), (stg, "p a n -> p (a n)")]
        for zi, (zt, pat) in enumerate(zf):
            zv = zt[:, :] if pat is None else zt[:].rearrange(pat)
            P.op("dve" if zi % 2 == 0 else "pool", lambda e, zv=zv: e.memset(zv, 0.0), writes=[("z", zi)])
        zkeys = [("z", zi) for zi in range(len(zf))]
        for l in range(L):
            P.dma("sp", car_k[l][:, :, :], kT[:, :, 0:512], reads=zkeys, writes=[("car_k", l)])
            P.dma("sp", car_v[l][:, :, :], vtok[:, 0:4, :], reads=zkeys, writes=[("car_v", l)])
            P.dma("sp", car_u[l][:, :, :], upb[:, :, 0:16], reads=zkeys, writes=[("car_u", l)])
            P.dma("sp", car_p[l][:, :, :], upb[:, :, 0:16], reads=zkeys, writes=[("car_p", l)])
        P.barrier()
        ring_i = [0]

        def load_w(src_ap, kc, ncols):
            s = ring_i[0]
            ring_i[0] = (s + 1) % 4
            dst = ring[:, s, 0:kc * ncols].rearrange("p (k n) -> p k n", k=kc)
            P.dma("pool", dst, src_ap.rearrange("(k p) n -> p k n", p=128), writes=[("ring", s)])
            return s, dst

        bank_rr = [0]

        def next_bank():
            b = bank_rr[0]
            bank_rr[0] = (b + 1) % 8
            return b

        def rmsnorm(tiles, gwhich_col0, out_fn=None):
            for ti in tiles:
                c0, n = TILES[ti]
                b = next_bank()
                for qd in range(4):
                    half = qd % 2
                    P.op("act", lambda e, half=half, qd=qd, c0=c0, n=n: e.activation(
                        out=sqb[:, half, :, 0:n], in_=xT[:, 2 * qd:2 * qd + 2, c0:c0 + n], func=AF.Square),
                        reads=[("xT", ti)], writes=[("sqb", half)])
                    for k2 in range(2):
                        k = 2 * qd + k2
                        P.op("pe", lambda e, half=half, k2=k2, k=k, b=b, n=n: e.matmul(
                            psum[:, b, 0:n], lhsT=cb(CB_MEAN), rhs=sqb[:, half, k2, 0:n], start=(k == 0), stop=(k == 7)),
                            reads=[("sqb", half), "cbf"], writes=[("ps", b)])
                tb, tkey = ((nrm1, "nrm1"), (nrm2, "nrm2"))[ti % 2]
                P.op("act", lambda e, b=b, n=n, tb=tb: e.activation(out=tb[:, 0:n], in_=psum[:, b, 0:n], func=AF.Ln,
                                                                    bias=smc(SM_EPS), scale=1.0),
                     reads=[("ps", b), "small"], writes=[tkey])
                P.op("act", lambda e, n=n, tb=tb: e.activation(out=tb[:, 0:n], in_=tb[:, 0:n], func=AF.Exp, scale=-0.5),
                     reads=[tkey], writes=[tkey])
                if out_fn is not None:
                    out_fn(ti, c0, n, tb, tkey)
                    continue
                for k in range(8):
                    P.op("dve", lambda e, k=k, c0=c0, n=n, tb=tb: e.scalar_tensor_tensor(
                        out=hT[:, k, c0:c0 + n], in0=xT[:, k, c0:c0 + n], scalar=smc(gwhich_col0 + k),
                        in1=tb[:, 0:n], op0=ALU.mult, op1=ALU.mult),
                        reads=[("xT", ti), tkey, "small"], writes=[("hT", ti)])

        def linear_fm(wsrc, src_buf, src_key, kc_n, m_chunks, tiles, evac):
            if not tiles:
                return
            for mg in range((m_chunks + 3) // 4):
                s, wt = load_w(wsrc(mg), kc_n, 512)
                for ti in tiles:
                    c0, n = TILES[ti]
                    for mi in range(min(4, m_chunks - mg * 4)):
                        m = mg * 4 + mi
                        b = next_bank()
                        for k in range(kc_n):
                            P.op("pe", lambda e, k=k, b=b, n=n, c0=c0, wt=wt, mi=mi: e.matmul(
                                psum[:, b, 0:n], lhsT=wt[:, k, mi * 128:(mi + 1) * 128], rhs=src_buf[:, k, c0:c0 + n],
                                start=(k == 0), stop=(k == kc_n - 1)),
                                reads=[("ring", s), (src_key, ti)], writes=[("ps", b)], inc=(k == kc_n - 1))
                        evac(m, ti, c0, n, b)

        def add_to_x(scale):
            def evac(m, ti, c0, n, b):
                P.op("dve", lambda e, m=m, c0=c0, n=n, b=b: e.scalar_tensor_tensor(
                    out=xT[:, m, c0:c0 + n], in0=psum[:, b, 0:n], scalar=scale, in1=xT[:, m, c0:c0 + n],
                    op0=ALU.mult, op1=ALU.add),
                    reads=[("ps", b), ("xT", ti)], writes=[("xT", ti)])
            return evac

        def ffn(l, wg, wu, wd, gwhich, tiles):
            if not tiles:
                return
            rmsnorm(tiles, gcol(l, gwhich, 0))
            groups = [(0, 8), (8, 16), (16, 22)]
            for (g0, g1) in groups:
                for sub in range(g0, g1, 4):
                    nch = min(4, g1 - sub)
                    sg, wgt = load_w(wg[l, :, sub * 128:(sub + nch) * 128], 8, nch * 128)
                    su, wut = load_w(wu[l, :, sub * 128:(sub + nch) * 128], 8, nch * 128)
                    for mi in range(nch):
                        mloc = sub - g0 + mi
                        for ti in tiles:
                            c0, n = TILES[ti]
                            ba = next_bank()
                            bb = next_bank()
                            for k in range(8):
                                P.op("pe", lambda e, k=k, ba=ba, n=n, c0=c0, wgt=wgt, mi=mi: e.matmul(
                                    psum[:, ba, 0:n], lhsT=wgt[:, k, mi * 128:(mi + 1) * 128], rhs=hT[:, k, c0:c0 + n],
                                    start=(k == 0), stop=(k == 7)),
                                    reads=[("ring", sg), ("hT", ti)], writes=[("ps", ba)], inc=(k == 7))
                            for k in range(8):
                                P.op("pe", lambda e, k=k, bb=bb, n=n, c0=c0, wut=wut, mi=mi: e.matmul(
                                    psum[:, bb, 0:n], lhsT=wut[:, k, mi * 128:(mi + 1) * 128], rhs=hT[:, k, c0:c0 + n],
                                    start=(k == 0), stop=(k == 7)),
                                    reads=[("ring", su), ("hT", ti)], writes=[("ps", bb)], inc=(k == 7))
                            P.op("act", lambda e, ba=ba, n=n: e.activation(out=nrm1[:, 0:n], in_=psum[:, ba, 0:n], func=AF.Silu),
                                 reads=[("ps", ba)], writes=["nrm1"])
                            P.op("dve", lambda e, bb=bb, n=n, c0=c0, mloc=mloc: e.tensor_tensor(
                                out=actT[:, mloc, c0:c0 + n], in0=psum[:, bb, 0:n], in1=nrm1[:, 0:n], op=ALU.mult),
                                reads=[("ps", bb), "nrm1"], writes=[("actT", ti), "bc"])
                slots = []
                for sub in range(g0, g1, 4):
                    nch = min(4, g1 - sub)
                    s, wt = load_w(wd[l, sub * 128:(sub + nch) * 128, :], nch, 1024)
                    slots.append((s, wt, nch))
                nk = g1 - g0
                for ti in tiles:
                    c0, n = TILES[ti]
                    for m2 in range(8):
                        b = next_bank()
                        kk = 0
                        for (s, wt, nch) in slots:
                            for k4 in range(nch):
                                P.op("pe", lambda e, k4=k4, kk=kk, b=b, n=n, c0=c0, wt=wt, m2=m2, nk=nk: e.matmul(
                                    psum[:, b, 0:n], lhsT=wt[:, k4, m2 * 128:(m2 + 1) * 128], rhs=actT[:, kk, c0:c0 + n],
                                    start=(kk == 0), stop=(kk == nk - 1)),
                                    reads=[("ring", s), ("actT", ti), "bc"], writes=[("ps", b)], inc=(kk == nk - 1))
                                kk += 1
                        add_to_x(0.5)(m2, ti, c0, n, b)

        def bccol(ti):
            c0, n = TILES[ti]
            return 16 + c0, n

        def w_in_stage(l, ps_, tilesA):
            if not tilesA:
                return
            has_s = 2 in tilesA
            ptiles = [t for t in tilesA if t < 2]

            def bc_dst(buf, mloc, ti):
                if ti < 2:
                    c, n = bccol(ti)
                    return buf[:, mloc, c:c + n]
                return buf[:, mloc, 16 + TP:16 + TP + 128].rearrange("p (b s) -> p b s", b=4)[:, :, 16:32]

            def ps_src(b, ti, n):
                if ti < 2:
                    return psum[:, b, 0:n]
                return psum[:, b, 0:64].rearrange("p (b s) -> p b s", b=4)

            def evac_q(m, ti, c0, n, b):
                P.op("act", lambda e: e.activation(out=qT[0:64, 2 * m, c0:c0 + n], in_=psum[0:64, b, 0:n], func=AF.Copy, scale=0.125),
                     reads=[("ps", b)], writes=[("qT", ti)])
                P.op("act", lambda e: e.activation(out=qT[64:128, 2 * m + 1, c0:c0 + n], in_=psum[64:128, b, 0:n], func=AF.Copy, scale=0.125),
                     reads=[("ps", b)], writes=[("qT", ti)])

            def evac_k(m, ti, c0, n, b):
                P.op("act", lambda e: e.activation(out=kT[:, m, 512 + c0:512 + c0 + n], in_=psum[:, b, 0:n], func=AF.Copy),
                     reads=[("ps", b)], writes=[("kT", ti + 1)])

            def evac_bc(m, ti, c0, n, b):
                mm = m
                buf = bg if mm < 2 else cg
                P.op("act", lambda e: e.activation(out=bc_dst(buf, mm % 2, ti), in_=ps_src(b, ti, n), func=AF.Copy),
                     reads=[("ps", b)], writes=[("bc", ti), "bc"])

            def evac_hu(m, ti, c0, n, b):
                if m < 2:
                    P.op("dve", lambda e: e.tensor_tensor(out=bc_dst(cg, m, ti), in0=ps_src(b, ti, n), in1=bc_dst(cg, m, ti), op=ALU.mult),
                         reads=[("ps", b), ("bc", ti)], writes=[("bc", ti), "bc"])
                else:
                    P.op("act", lambda e: e.activation(out=bc_dst(upb, m - 2, ti), in_=ps_src(b, ti, n), func=AF.Copy),
                         reads=[("ps", b)], writes=[("up", ti)])

            wl = w_in[l]
            sub = DBG.get("sub")
            son = lambda nm: sub is None or nm in sub
            if son("q"):
                linear_fm(lambda mg: wl[:, 0:512], hT, "hT", 8, 4, tilesA, evac_q)
            if not son("k"):
                return
            s, wt = load_w(wl[:, 512:1024], 8, 512)
            for mi in range(4):
                for ti in tilesA:
                    c0, n = TILES[ti]
                    b = next_bank()
                    for k in range(8):
                        P.op("pe", lambda e, k=k, b=b, n=n, c0=c0, mi=mi, wt=wt: e.matmul(
                            psum[:, b, 0:n], lhsT=wt[:, k, mi * 128:(mi + 1) * 128], rhs=hT[:, k, c0:c0 + n],
                            start=(k == 0), stop=(k == 7)), reads=[("ring", s), ("hT", ti)], writes=[("ps", b)])
                    evac_k(mi, ti, c0, n, b)
            tok_out(l, ps_, s, wt, tilesA, o_pk, o_sk, False)
            if not son("v"):
                return
            s, wt = load_w(wl[:, 1024:1536], 8, 512)
            for ti in ptiles:
                c0, n = TILES[ti]
                for j in range(4):
                    b = next_bank()
                    cc = c0 + j * 128
                    for k in range(8):
                        P.op("pe", lambda e, k=k, b=b, cc=cc, wt=wt: e.matmul(
                            psum[:, b, :], lhsT=hT[:, k, cc:cc + 128], rhs=wt[:, k, :], start=(k == 0), stop=(k == 7)),
                            reads=[("ring", s), ("hT", ti)], writes=[("ps", b)])
                    vt = 4 + ti * 4 + j
                    P.op("dve", lambda e, b=b, vt=vt: e.tensor_copy(out=vtok[:, vt, :], in_=psum[:, b, :]),
                         reads=[("ps", b)], writes=[("vtok", ti + 1)])
            if has_s:
                for bq in range(4):
                    b = next_bank()
                    cc = TP + 16 * bq
                    for k in range(8):
                        P.op("pe", lambda e, k=k, b=b, cc=cc, wt=wt: e.matmul(
                            psum[0:16, b, :], lhsT=hT[:, k, cc:cc + 16], rhs=wt[:, k, :], start=(k == 0), stop=(k == 7)),
                            reads=[("ring", s), ("hT", 2)], writes=[("ps", b)])
                    P.op("dve", lambda e, b=b, bq=bq: e.tensor_copy(out=svnew[0:16, bq, :], in_=psum[0:16, b, :]),
                         reads=[("ps", b)], writes=["svnew"])
            tok_out(l, ps_, s, wt, tilesA, o_pv, o_sv, True)
            if not son("bc"):
                return
            linear_fm(lambda mg: wl[:, 1536:2048], hT, "hT", 8, 4, tilesA, evac_bc)
            if not son("hu"):
                return
            linear_fm(lambda mg: wl[:, 2048:2560], hT, "hT", 8, 4, tilesA, evac_hu)

        def tok_out(l, ps_, s, wt, tilesA, o_p, o_s, is_v):
            if ps_["name"] != "O2":
                return
            for j in range(4):
                b = next_bank()
                cc = 512 + j * 128
                for k in range(8):
                    P.op("pe", lambda e, k=k, b=b, cc=cc: e.matmul(
                        psum[:, b, :], lhsT=hT[:, k, cc:cc + 128], rhs=wt[:, k, :], start=(k == 0), stop=(k == 7)),
                        reads=[("ring", s), ("hT", 1)], writes=[("ps", b)])
                sj = j % 2
                P.op("act", lambda e, b=b, sj=sj: e.activation(out=stg[:, sj, :], in_=psum[:, b, :], func=AF.Copy),
                     reads=[("ps", b)], writes=[("stg", sj)])
                P.dma("sp", o_p[l, j * 128:(j + 1) * 128, :], stg[:, sj, :], reads=[("stg", sj)], writes=[("o_p", is_v, l, j)])
            b = next_bank()
            for k in range(8):
                P.op("pe", lambda e, k=k, b=b: e.matmul(
                    psum[0:64, b, :], lhsT=hT[:, k, TP:TP + 64], rhs=wt[:, k, :], start=(k == 0), stop=(k == 7)),
                    reads=[("ring", s), ("hT", 2)], writes=[("ps", b)])
            P.op("act", lambda e, b=b: e.activation(out=stg[0:64, 0, :], in_=psum[0:64, b, :], func=AF.Copy),
                 reads=[("ps", b)], writes=[("stg", 0)])
            for bq in range(4):
                P.dma("sp", o_s[l, bq, 496:512, :], stg[16 * bq:16 * bq + 16, 0, :], reads=[("stg", 0)],
                      writes=[("o_s", is_v, l, bq)])

        def carry_out(l, ps_, tilesA):
            if 1 not in tilesA or ps_["name"] == "O2":
                return
            P.dma("sp", car_k[l][:, :, :], kT[:, :, 512 + 512:512 + 1024], reads=[("kT", 2)], writes=[("car_k", l)])
            P.dma("sp", car_v[l][:, :, :], vtok[:, 8:12, :], reads=[("vtok", 2)], writes=[("car_v", l)])
            P.dma("sp", car_u[l][:, :, :], cg[:, :, 16 + TP - 16:16 + TP], reads=[("bc", 1), "bc"], writes=[("car_u", l)])
            P.dma("sp", car_p[l][:, :, :], upb[:, :, 16 + TP - 16:16 + TP], reads=[("up", 1)], writes=[("car_p", l)])

        def carry_in(l, ps_, tilesR):
            if 0 not in tilesR or ps_["name"] == "H1":
                return
            import os
            cv_ = os.environ.get("CIN", "k,v,u,p,f").split(",")
            if "k" in cv_:
                P.dma("sp", kT[:, :, 0:512], car_k[l][:, :, :], reads=[("car_k", l)], writes=[("kT", 0)])
            if "v" in cv_:
                P.dma("sp", vtok[:, 0:4, :], car_v[l][:, :, :], reads=[("car_v", l)], writes=[("vtok", 0)])
            if "u" in cv_:
                P.dma("sp", cg[:, :, 0:16], car_u[l][:, :, :], reads=[("car_u", l)], writes=[("bch", 0), "bc"])
            if "p" in cv_:
                P.dma("sp", upb[:, :, 0:16], car_p[l][:, :, :], reads=[("car_p", l)], writes=[("uph", 0)])
            if ps_["name"] == "O1" and "f" in cv_:
                P.op("act", lambda e: e.activation(out=cg[:, :, 0:16], in_=cg[:, :, 0:16], func=AF.Copy, scale=smc(SM_TFLAG)),
                     reads=[("bch", 0), "small"], writes=[("bch", 0), "bc"])
                P.op("act", lambda e: e.activation(out=upb[:, :, 0:16], in_=upb[:, :, 0:16], func=AF.Copy, scale=smc(SM_TFLAG)),
                     reads=[("uph", 0), "small"], writes=[("uph", 0)])

        hn_par = [0]

        def headnorm_bc(l, src, srctok, chunk0, dti, c, n, dstc0):
            for cc in range(2):
                b = next_bank()
                par = hn_par[0]
                hn_par[0] ^= 1
                tb, tkey = ((nrm1, "nrm1"), (nrm2, "nrm2"))[par]
                P.op("act", lambda e, cc=cc, par=par: e.activation(out=sqb[:, par, 0, 0:n], in_=src[:, cc, c:c + n], func=AF.Square),
                     reads=[srctok], writes=[("sqb", par)])
                P.op("pe", lambda e, b=b, par=par: e.matmul(psum[:, b, 0:n], lhsT=cb(CB_BLK64), rhs=sqb[:, par, 0, 0:n], start=True, stop=True),
                     reads=[("sqb", par), "cbf"], writes=[("ps", b)])
                P.op("act", lambda e, b=b, tb=tb: e.activation(out=tb[:, 0:n], in_=psum[:, b, 0:n], func=AF.Ln, bias=smc(SM_EPS), scale=1.0),
                     reads=[("ps", b), "small"], writes=[tkey])
                P.op("act", lambda e, tb=tb: e.activation(out=tb[:, 0:n], in_=tb[:, 0:n], func=AF.Exp, scale=-0.5), reads=[tkey], writes=[tkey])
                P.op("dve", lambda e, cc=cc, tb=tb: e.scalar_tensor_tensor(
                    out=hT[:, chunk0 + cc, dstc0:dstc0 + n], in0=src[:, cc, c:c + n], scalar=smc(gcol(l, G_HEADS, chunk0 + cc)),
                    in1=tb[:, 0:n], op0=ALU.mult, op1=ALU.mult),
                    reads=[srctok, tkey, "small"], writes=[("hT", dti)])

        def bc_stage(l, ps_, tilesR):
            if not tilesR:
                return
            pt = [t for t in tilesR if t < 2]
            has_s = 2 in tilesR
            regs = []
            if pt:
                a = 16 + TILES[pt[0]][0]
                bnd = 16 + TILES[pt[-1]][0] + TILES[pt[-1]][1]
                regs.append(("p", a, bnd))
            if has_s:
                regs.append(("s", 0, 0))
                sview_u = cg[:, :, 16 + TP:16 + TP + 128].rearrange("p c (b s) -> p c b s", b=4)
                sview_p = upb[:, :, 16 + TP:16 + TP + 128].rearrange("p c (b s) -> p c b s", b=4)
                for c in range(2):
                    P.dma("sp", sview_u[:, c, :, 0:16], shu_in[l, :, c, :, :], writes=[("bch", 2), "bc"])
                    P.dma("sp", sview_p[:, c, :, 0:16], shp_in[l, :, c, :, :], writes=[("uph", 2)])

            def view(buf, reg, c, sh, lo):
                kind, a, bnd = reg
                if kind == "p":
                    return buf[:, c, a - lo - sh:bnd - sh]
                v = buf[:, c, 16 + TP:16 + TP + 128].rearrange("p (b s) -> p b s", b=4)
                return v[:, :, 16 - lo - sh:32 - sh]

            rk = [("bc", t) for t in tilesR] + [("bch", 0), ("bch", 2), "bc"]
            uk = [("up", t) for t in tilesR] + [("uph", 0), ("uph", 2)]
            if ps_["name"] == "O2":
                P.dma("sp", o_pc[l, :, :, :], cg[:, :, 16 + TP - 16:16 + TP], reads=[("bc", 1), "bc"], writes=[("o_pc", l)])
                P.dma("sp", o_pp[l, :, :, :], upb[:, :, 16 + TP - 16:16 + TP], reads=[("up", 1)], writes=[("o_pp", l)])
                if has_s:
                    for c in range(2):
                        P.dma("sp", o_sc[l, :, c, :, :], cg[:, c, 16 + TP:16 + TP + 128].rearrange("p (b s) -> p b s", b=4)[:, :, 16:32],
                              reads=[("bc", 2), "bc"], writes=[("o_sc", l, c)])
                        P.dma("sp", o_sp[l, :, c, :, :], upb[:, c, 16 + TP:16 + TP + 128].rearrange("p (b s) -> p b s", b=4)[:, :, 16:32],
                              reads=[("up", 2)], writes=[("o_sp", l, c)])
            for reg in regs:
                for c in range(2):
                    cw = lambda i, c=c: smc(SM_CONV + (l * 3 + i) * 2 + c)
                    P.op("dve", lambda e, reg=reg, c=c, cw=cw: e.tensor_scalar(out=view(tmpA, reg, c, 0, 0), in0=view(cg, reg, c, 0, 0),
                                                                               scalar1=cw(2), scalar2=None, op0=ALU.mult),
                         reads=rk + ["small"], writes=["tmpA"])
                    for i, sh in ((1, 1), (0, 2)):
                        P.op("dve", lambda e, reg=reg, c=c, cw=cw, i=i, sh=sh: e.scalar_tensor_tensor(
                            out=view(tmpA, reg, c, 0, 0), in0=view(cg, reg, c, sh, 0), scalar=cw(i), in1=view(tmpA, reg, c, 0, 0),
                            op0=ALU.mult, op1=ALU.add), reads=rk + ["tmpA", "small"], writes=["tmpA"])
                    P.op("dve", lambda e, reg=reg, c=c: e.tensor_tensor(out=view(bg, reg, c, 0, 0), in0=view(bg, reg, c, 0, 0),
                                                                        in1=view(tmpA, reg, c, 0, 0), op=ALU.mult),
                         reads=rk + ["tmpA"], writes=[("bcb", 0), "bc"])
            for reg in regs:
                for c in range(2):
                    P.op("dve", lambda e, reg=reg, c=c: e.tensor_tensor(out=view(tmpA, reg, c, 0, 14), in0=view(upb, reg, c, 0, 14),
                                                                        in1=view(upb, reg, c, 1, 14), op=ALU.add),
                         reads=uk + [("bcb", 0)], writes=["tmpA"])
                    P.op("dve", lambda e, reg=reg, c=c: e.tensor_tensor(out=view(cg, reg, c, 0, 12), in0=view(tmpA, reg, c, 0, 12),
                                                                        in1=view(tmpA, reg, c, 2, 12), op=ALU.add),
                         reads=["tmpA", ("car_u", l), ("o_pc", l), ("o_sc", l, 0), ("o_sc", l, 1)], writes=["cgs", "bc"])
                for (p0, p1, sbuf_) in ((0, 64, tmpA), (64, 128, cg)):
                    P.op("dve", lambda e, reg=reg, p0=p0, p1=p1, sbuf_=sbuf_: e.scalar_tensor_tensor(
                        out=view(pooled, reg, 0, 0, 0)[p0:p1], in0=view(sbuf_, reg, 0, 0, 0)[p0:p1], scalar=smc(SM_INVW, 1, p0, p1),
                        in1=view(upb, reg, 0, 0, 0)[p0:p1], op0=ALU.mult, op1=ALU.subtract),
                        reads=["tmpA", "cgs", "small"] + uk, writes=["pooled"])
                P.op("dve", lambda e, reg=reg: e.tensor_tensor(out=view(tmpA, reg, 1, 0, 8), in0=view(cg, reg, 1, 0, 8),
                                                                in1=view(cg, reg, 1, 4, 8), op=ALU.add),
                     reads=["cgs", "pooled"], writes=["tmpA"])
                P.op("dve", lambda e, reg=reg: e.tensor_tensor(out=view(cg, reg, 1, 0, 0), in0=view(tmpA, reg, 1, 0, 0),
                                                                in1=view(tmpA, reg, 1, 8, 0), op=ALU.add),
                     reads=["tmpA"], writes=["cgs", "bc"])
                for (p0, p1, sbuf_) in ((0, 64, tmpA), (64, 128, cg)):
                    P.op("dve", lambda e, reg=reg, p0=p0, p1=p1, sbuf_=sbuf_: e.scalar_tensor_tensor(
                        out=view(pooled, reg, 1, 0, 0)[p0:p1], in0=view(sbuf_, reg, 1, 0, 0)[p0:p1], scalar=smc(SM_INVW + 1, 1, p0, p1),
                        in1=view(upb, reg, 1, 0, 0)[p0:p1], op0=ALU.mult, op1=ALU.subtract),
                        reads=["tmpA", "cgs", "small"] + uk, writes=["pooled"])
                if reg[0] == "p" and ps_["name"] == "O1" and reg[1] == 16 and not DBG.get("nocorr"):
                    for c, lo_buf, hi_buf in ((0, tmpA, cg), (1, tmpA, cg)):
                        for (p0, p1, sbuf_) in ((0, 64, lo_buf), (64, 128, hi_buf)):
                            P.op("dve", lambda e, c=c, p0=p0, p1=p1, sbuf_=sbuf_: e.tensor_tensor(
                                out=nrm1[p0:p1, 0:16], in0=sbuf_[p0:p1, c, 16:32], in1=small[p0:p1, SM_INVCNT + 16 * c:SM_INVCNT + 16 * c + 16], op=ALU.mult),
                                reads=["tmpA", "cgs", "small"], writes=["nrm1"])
                            P.op("dve", lambda e, c=c, p0=p0, p1=p1: e.tensor_tensor(
                                out=pooled[p0:p1, c, 16:32], in0=nrm1[p0:p1, 0:16], in1=upb[p0:p1, c, 16:32], op=ALU.subtract),
                                reads=["nrm1"] + uk, writes=["pooled"])
            for ti in tilesR:
                for c in range(2):
                    b = next_bank()
                    if ti < 2:
                        cc, n = bccol(ti)
                        rhs = pooled[:, c, cc:cc + n]
                        dst = upb[:, c, cc:cc + n]
                        src = psum[:, b, 0:n]
                    else:
                        n = 64
                        rhs = pooled[:, c, 16 + TP:16 + TP + 128].rearrange("p (b s) -> p b s", b=4)[:, :, 16:32]
                        dst = upb[:, c, 16 + TP:16 + TP + 128].rearrange("p (b s) -> p b s", b=4)[:, :, 16:32]
                        src = psum[:, b, 0:64].rearrange("p (b s) -> p b s", b=4)
                    P.op("pe", lambda e, c=c, b=b, rhs=rhs, n=n: e.matmul(psum[:, b, 0:n], lhsT=poolw[:, c, :], rhs=rhs, start=True, stop=True),
                         reads=["pooled", "poolw"], writes=[("ps", b)])
                    P.op("dve", lambda e, c=c, dst=dst, src=src: e.tensor_scalar(out=dst, in0=src, scalar1=smc(SM_PSCALE + l * 2 + c), scalar2=None, op0=ALU.mult),
                         reads=[("ps", b), "small", ("car_p", l), ("o_pp", l), ("o_sp", l, 0), ("o_sp", l, 1)], writes=[("up", ti)])
            for ti in tilesR:
                if ti < 2:
                    cc, n = bccol(ti)
                    headnorm_bc(l, bg, ("bcb", 0), 4, ti, cc, n, TILES[ti][0])
                    headnorm_bc(l, upb, ("up", ti), 6, ti, cc, n, TILES[ti][0])
                else:
                    for (src, key, ch0) in ((bg, ("bcb", 0), 4), (upb, ("up", 2), 6)):
                        for c in range(2):
                            P.op("dve", lambda e, src=src, c=c: e.tensor_copy(
                                out=tmpA[:, c, 0:64].rearrange("p (b s) -> p b s", b=4),
                                in_=src[:, c, 16 + TP:16 + TP + 128].rearrange("p (b s) -> p b s", b=4)[:, :, 16:32]),
                                reads=[key, "pooled"], writes=["tmpA"])
                        headnorm_bc(l, tmpA, "tmpA", ch0, 2, 0, 64, TP)

        def attn_epilogue(l, bn, bd, ncols, dst_cols, ti):
            w = 4 * ncols
            b = next_bank_misc()
            P.op("act", lambda e: e.activation(out=sqb[:, 1, 0, 0:w], in_=psum[:, bn, 0:w], func=AF.Square),
                 reads=[("ps", bn)], writes=[("sqb", 1)])
            P.op("act", lambda e: e.activation(out=nrm2[:, 0:w], in_=psum[:, bd, 0:w], func=AF.Square, scale=1e-3),
                 reads=[("ps", bd)], writes=["nrm2"])
            P.op("pe", lambda e: e.matmul(psum[:, b, 0:w], lhsT=cb(CB_BLK64), rhs=sqb[:, 1, 0, 0:w], start=True, stop=True),
                 reads=[("sqb", 1), "cbf"], writes=[("ps", b)])
            P.op("dve", lambda e: e.tensor_tensor(out=nrm1[:, 0:w], in0=psum[:, b, 0:w], in1=nrm2[:, 0:w], op=ALU.add),
                 reads=[("ps", b), "nrm2"], writes=["nrm1"])
            P.op("act", lambda e: e.activation(out=nrm2[:, 0:w], in_=nrm1[:, 0:w], func=AF.Ln),
                 reads=["nrm1"], writes=["nrm2"])
            P.op("act", lambda e: e.activation(out=nrm1[:, 0:w], in_=nrm2[:, 0:w], func=AF.Exp, scale=-0.5), reads=["nrm2"], writes=["nrm1"])
            for hp in range(4):
                P.op("dve", lambda e, hp=hp: e.scalar_tensor_tensor(
                    out=hT[:, hp, dst_cols:dst_cols + ncols], in0=psum[:, bn, hp * ncols:(hp + 1) * ncols],
                    scalar=smc(gcol(l, G_HEADS, hp)), in1=nrm1[:, hp * ncols:(hp + 1) * ncols], op0=ALU.mult, op1=ALU.mult),
                    reads=[("ps", bn), "nrm1", "small"], writes=[("hT", ti)])

        misc_rr = [0]

        def next_bank_misc():
            b = misc_rr[0]
            misc_rr[0] = (b + 1) % 4
            return b

        def attn_prompt(l, ps_, tilesR):
            pt = [t for t in tilesR if t < 2]
            if not pt:
                return
            P.dma("pool", biasraw[:, :].rearrange("p (j x) -> p j x", j=5),
                  bias_in[l].rearrange("(j p) h q -> p j (h q)", p=128), writes=["bias"])
            for (j, cbi) in ((0, CB_SM0), (4, CB_SM4)):
                for h in range(8):
                    P.op("dve", lambda e, j=j, h=h, cbi=cbi: e.tensor_tensor(out=bias[:, j, h, :], in0=bias[:, j, h, :], in1=cb(cbi), op=ALU.add),
                         reads=["bias", "cbf"], writes=["bias"])
            gpar = 0
            for ti in pt:
                for g4 in range(4):
                    gi = ti * 4 + g4
                    q0 = gi * 128
                    bn = 4 + 2 * gpar
                    bd = 5 + 2 * gpar
                    gpar ^= 1
                    for bz in (bn, bd):
                        P.op("pe", lambda e, bz=bz: e.matmul(psum[:, bz, :], lhsT=cb(CB_ZERO), rhs=cbf[:, 0:4, :].rearrange("p a n -> p (a n)"), start=True, stop=False),
                             reads=["cbf"], writes=[("ps", bz)])
                    def emit_S(jt, gi=gi, q0=q0, ti=ti):
                        kt = gi + jt
                        kreg = ("kT", 0) if kt < 4 else ("kT", 1 + (kt - 4) // 4)
                        masked = (ps_["name"] == "O1" and kt < 4)
                        pbuf = jt % 2
                        for hb in range(2):
                            b = next_bank_misc()
                            P.op("pe", lambda e, b=b, jt=jt, hb=hb: e.matmul(
                                psum[:, b, :], lhsT=cb(CB_IDENT), rhs=bias[:, jt, hb * 4:hb * 4 + 4, :].rearrange("p h q -> p (h q)"),
                                start=True, stop=False), reads=["bias", "cbf"], writes=[("ps", b)])
                            for h4 in range(4):
                                h = hb * 4 + h4
                                P.op("pe", lambda e, b=b, h=h, h4=h4, kt=kt, q0=q0: e.matmul(
                                    psum[:, b, h4 * 128:(h4 + 1) * 128], lhsT=kT[:, h // 2, kt * 128:(kt + 1) * 128],
                                    rhs=qT[:, h, q0:q0 + 128], start=False, stop=(h4 == 3)),
                                    reads=[kreg, ("qT", ti)], writes=[("ps", b)], inc=(h4 == 3))
                            mcol = SM_HMASK if masked else SM_ZERO
                            P.op("act", lambda e, b=b, pbuf=pbuf, hb=hb, mcol=mcol: e.activation(
                                out=pT[:, pbuf, hb * 512:(hb + 1) * 512], in_=psum[:, b, :], func=AF.Exp, bias=smc(mcol), scale=1.0),
                                reads=[("ps", b), "small"], writes=[("pT", pbuf, hb)])

                    def emit_PV(jt, gi=gi, bn=bn, bd=bd):
                        kt = gi + jt
                        vreg = ("vtok", 0) if kt < 4 else ("vtok", 1 + (kt - 4) // 4)
                        pbuf = jt % 2
                        for h in range(8):
                            hp, r0 = h // 2, (h % 2) * 64
                            P.op("pe", lambda e, h=h, hp=hp, r0=r0, kt=kt, pbuf=pbuf, bn=bn, jt=jt: e.matmul(
                                psum[r0:r0 + 64, bn, hp * 128:(hp + 1) * 128], lhsT=vtok[:, kt, h * 64:(h + 1) * 64],
                                rhs=pT[:, pbuf, h * 128:(h + 1) * 128], start=False, stop=(jt == 4 and h >= 6)),
                                reads=[vreg, ("pT", pbuf, h // 4)], writes=[("ps", bn)], inc=False)
                            P.op("pe", lambda e, h=h, hp=hp, r0=r0, pbuf=pbuf, bd=bd, jt=jt: e.matmul(
                                psum[r0:r0 + 64, bd, hp * 128:(hp + 1) * 128], lhsT=cb(CB_ONES, 0, 128, 0, 64),
                                rhs=pT[:, pbuf, h * 128:(h + 1) * 128], start=False, stop=(jt == 4 and h >= 6)),
                                reads=["cbf", ("pT", pbuf, h // 4)], writes=[("ps", bd)], inc=(h == 7))

                    emit_S(0)
                    for jt in range(5):
                        if jt + 1 < 5:
                            emit_S(jt + 1)
                        emit_PV(jt)
                    attn_epilogue(l, bn, bd, 128, q0, ti)

        def attn_sample(l, ps_, tilesR):
            if 2 not in tilesR:
                return
            P.dma("pool", sbias[:, :, :, :].rearrange("p j h q -> p j (h q)"),
                  sbias_in[l].rearrange("(j p) h q -> p j (h q)", p=128), writes=["memk"])
            cs_i = 0
            for bq in range(4):
                for (src_c, dst_o) in ((ck_in, o_sk), (cv_in, o_sv)):
                    for r0 in range(0, 496, 128):
                        nr = min(128, 496 - r0)
                        sj = cs_i % 2
                        cs_i += 1
                        P.dma("sp", stg[0:nr, sj, :], src_c[l, bq, 16 + r0:16 + r0 + nr, :], writes=[("stg", sj)])
                        P.dma("sp", dst_o[l, bq, r0:r0 + nr, :], stg[0:nr, sj, :], reads=[("stg", sj)], writes=[("o_sc_rows", l, bq, r0, id(dst_o))])
                P.dma("pool", knat[:, :, :], ck_in[l, bq].rearrange("(k p) n -> p k n", p=128),
                      writes=[("pT", 0, 0), ("pT", 0, 1), ("pT", 1, 0), ("pT", 1, 1)])
                P.dma("pool", sV[:, :, :], cv_in[l, bq].rearrange("(k p) n -> p k n", p=128), writes=["bias"])
                for c in range(4):
                    b = next_bank_misc()
                    pv = psum[:, b, :].bitcast(BF16)
                    for k4 in range(4):
                        P.op("pe", lambda e, c=c, k4=k4, pv=pv: e.matmul(pv[:, k4 * 128:(k4 + 1) * 128], lhsT=knat[:, k4, c * 128:(c + 1) * 128],
                                                                         rhs=cb(CB_IDENT), start=True, stop=True, is_transpose=True),
                             reads=[("pT", 0, 0), ("pT", 0, 1), ("pT", 1, 0), ("pT", 1, 1), "cbf"], writes=[("ps", b)])
                    P.op("dve", lambda e, c=c, pv=pv: e.tensor_copy(out=skT[:, c, 0:512], in_=pv[:, 0:512]),
                         reads=[("ps", b)], writes=["bias"])
                bn, bd = 4, 5
                for bz in (bn, bd):
                    P.op("pe", lambda e, bz=bz: e.matmul(psum[:, bz, 0:64], lhsT=cb(CB_ZERO), rhs=cbf[:, 0, 0:64], start=True, stop=False),
                         reads=["cbf"], writes=[("ps", bz)])
                qc = TP + 16 * bq
                for kt in range(5):
                    b = next_bank_misc()
                    np_ = 128 if kt < 4 else 16
                    P.op("pe", lambda e, b=b, kt=kt, np_=np_: e.matmul(
                        psum[0:np_, b, 0:128], lhsT=cb(CB_IDENT, 0, np_, 0, np_), rhs=sbias[0:np_, kt, :, :].rearrange("p h q -> p (h q)"),
                        start=True, stop=False), reads=["memk", "cbf"], writes=[("ps", b)])
                    for h in range(8):
                        r0 = (h % 2) * 64
                        if kt < 4:
                            lhs = skT[:, h // 2, kt * 128:(kt + 1) * 128]
                        else:
                            lhs = kT[:, h // 2, 512 + qc:512 + qc + 16]
                        P.op("pe", lambda e, b=b, h=h, r0=r0, lhs=lhs, np_=np_, qc=qc: e.matmul(
                            psum[0:np_, b, h * 16:(h + 1) * 16], lhsT=lhs, rhs=qT[:, h, qc:qc + 16],
                            start=False, stop=(h == 7)), reads=["bias", ("kT", 3), ("qT", 2)], writes=[("ps", b)])
                    P.op("act", lambda e, b=b, kt=kt, np_=np_: e.activation(
                        out=spT[0:np_, kt, :, :].rearrange("p h q -> p (h q)"), in_=psum[0:np_, b, 0:128], func=AF.Exp),
                        reads=[("ps", b)], writes=[("sqb", 0)])
                    for h in range(8):
                        hp, r0 = h // 2, (h % 2) * 64
                        if kt < 4:
                            lv = sV[:, kt, h * 64:(h + 1) * 64]
                        else:
                            lv = svnew[0:16, bq, h * 64:(h + 1) * 64]
                        P.op("pe", lambda e, h=h, hp=hp, r0=r0, lv=lv, kt=kt, np_=np_: e.matmul(
                            psum[r0:r0 + 64, bn, hp * 16:(hp + 1) * 16], lhsT=lv, rhs=spT[0:np_, kt, h, :], start=False, stop=(kt == 4 and h >= 6)),
                            reads=["bias", "svnew", ("sqb", 0)], writes=[("ps", bn)])
                        P.op("pe", lambda e, h=h, hp=hp, r0=r0, kt=kt, np_=np_: e.matmul(
                            psum[r0:r0 + 64, bd, hp * 16:(hp + 1) * 16], lhsT=cb(CB_ONES, 0, np_, 0, 64), rhs=spT[0:np_, kt, h, :],
                            start=False, stop=(kt == 4 and h >= 6)), reads=["cbf", ("sqb", 0)], writes=[("ps", bd)])
                attn_epilogue(l, bn, bd, 16, qc, 2)

        def xattn_mem(l, ps_, tilesR):
            pt = [t for t in tilesR if t < 2]
            if pt:
                KK = [("kT", i) for i in range(4)]
                memf = kT[:, :, :].rearrange("p a n -> p (a n)").bitcast(F32)[:, 0:2048].rearrange("p (c n) -> p c n", c=8)
                P.dma("sp", memf, memT_in.rearrange("(c p) n -> p c n", p=128), writes=KK)
                b = next_bank()
                for qd in range(4):
                    half = qd % 2
                    P.op("act", lambda e, half=half, qd=qd: e.activation(out=sqb[:, half, :, 0:256], in_=memf[:, 2 * qd:2 * qd + 2, :], func=AF.Square),
                         reads=KK, writes=[("sqb", half)])
                    for k2 in range(2):
                        k = 2 * qd + k2
                        P.op("pe", lambda e, half=half, k2=k2, k=k, b=b: e.matmul(psum[:, b, 0:256], lhsT=cb(CB_MEAN), rhs=sqb[:, half, k2, 0:256],
                                                                           start=(k == 0), stop=(k == 7)),
                             reads=[("sqb", half), "cbf"], writes=[("ps", b)])
                P.op("act", lambda e, b=b: e.activation(out=nrm1[:, 0:256], in_=psum[:, b, 0:256], func=AF.Ln, bias=smc(SM_EPS), scale=1.0),
                     reads=[("ps", b), "small"], writes=["nrm1"])
                P.op("act", lambda e: e.activation(out=nrm2[:, 0:256], in_=nrm1[:, 0:256], func=AF.Exp, scale=-0.5), reads=["nrm1"], writes=["nrm2"])
                mnT = kT[:, :, :].rearrange("p a n -> p (a n)")[:, 4096:6144].rearrange("p (c n) -> p c n", c=8)
                for k in range(8):
                    P.op("dve", lambda e, k=k: e.scalar_tensor_tensor(out=mnT[:, k, :], in0=memf[:, k, :], scalar=smc(gcol(l, G_MEM, k)),
                                                                      in1=nrm2[:, 0:256], op0=ALU.mult, op1=ALU.mult),
                         reads=KK + ["nrm2", "small"], writes=KK)
                write_out = ps_["name"] == "O1"
                for mg in range(2):
                    s, wt = load_w(w_xk[l][:, mg * 512:(mg + 1) * 512], 8, 512)
                    for mi in range(4):
                        m = mg * 4 + mi
                        b = next_bank()
                        for k in range(8):
                            P.op("pe", lambda e, k=k, mi=mi, wt=wt, b=b: e.matmul(psum[:, b, 0:256], lhsT=wt[:, k, mi * 128:(mi + 1) * 128], rhs=mnT[:, k, :],
                                                                                  start=(k == 0), stop=(k == 7)),
                                 reads=[("ring", s)] + KK, writes=[("ps", b)])
                        P.op("act", lambda e, m=m, b=b: e.activation(out=memk[:, m, :], in_=psum[:, b, 0:256], func=AF.Copy),
                             reads=[("ps", b)], writes=["memk"])
                        if write_out:
                            sj = m % 2
                            P.op("dve", lambda e, b=b, sj=sj: e.tensor_copy(out=stg[:, sj, 0:256], in_=psum[:, b, 0:256]),
                                 reads=[("ps", b)], writes=[("stg", sj), ("ps", b)])
                            P.dma("sp", o_mkT[l, m * 128:(m + 1) * 128, :], stg[:, sj, 0:256], reads=[("stg", sj)], writes=[("o_mkT", l, m)])
                for mg in range(2):
                    s, wt = load_w(w_xv[l][:, mg * 512:(mg + 1) * 512], 8, 512)
                    for mt in range(2):
                        b = next_bank()
                        for k in range(8):
                            P.op("pe", lambda e, k=k, mt=mt, wt=wt, b=b: e.matmul(psum[:, b, :], lhsT=mnT[:, k, mt * 128:(mt + 1) * 128], rhs=wt[:, k, :],
                                                                                  start=(k == 0), stop=(k == 7)),
                                 reads=[("ring", s)] + KK, writes=[("ps", b)])
                        P.op("act", lambda e, mt=mt, mg=mg, b=b: e.activation(out=memv[:, mt, mg * 512:(mg + 1) * 512], in_=psum[:, b, :], func=AF.Copy),
                             reads=[("ps", b)], writes=["memk"])
                        if write_out:
                            sj = mt
                            P.op("dve", lambda e, b=b, sj=sj: e.tensor_copy(out=stg[:, sj, :], in_=psum[:, b, :]),
                                 reads=[("ps", b)], writes=[("stg", sj), ("ps", b)])
                            P.dma("sp", o_mv[l, mt * 128:(mt + 1) * 128, mg * 512:(mg + 1) * 512], stg[:, sj, :], reads=[("stg", sj)],
                                  writes=[("o_mv", l, mt, mg)])

        def xattn(l, ps_, tilesR):
            if not tilesR:
                return
            pt = [t for t in tilesR if t < 2]
            has_s = 2 in tilesR
            xsub = DBG.get("xsub")
            xon = lambda nm: xsub is None or nm in xsub
            rmsnorm(tilesR, gcol(l, G_XATTN, 0))

            def evac_qx(m, ti, c0, n, b):
                P.op("act", lambda e: e.activation(out=actT[:, m, c0:c0 + n], in_=psum[:, b, 0:n], func=AF.Copy, scale=1.0 / 16.0),
                     reads=[("ps", b)], writes=[("actT", ti), "bc"])
            linear_fm(lambda mg: w_xq[l][:, mg * 512:(mg + 1) * 512], hT, "hT", 8, 8, tilesR, evac_qx)
            if not xon("mem"):
                return

            def attend(c0, n, ti, mk, mv, mkey):
                def s_part(h):
                    ph = h % 2
                    bs = [next_bank() for _ in range(2)]
                    for mt in range(2):
                        for dc in range(2):
                            P.op("pe", lambda e, mt=mt, dc=dc, h=h, b=bs[mt]: e.matmul(
                                psum[:, b, 0:n], lhsT=mk[:, h * 2 + dc, mt * 128:(mt + 1) * 128], rhs=actT[:, h * 2 + dc, c0:c0 + n],
                                start=(dc == 0), stop=(dc == 1)), reads=[mkey, ("actT", ti), "bc"], writes=[("ps", bs[mt])])
                        P.op("act", lambda e, mt=mt, b=bs[mt], ph=ph: e.activation(out=pT[:, mt, ph * 512:ph * 512 + n], in_=psum[:, b, 0:n], func=AF.Exp),
                             reads=[("ps", bs[mt])], writes=[("pT", mt, ph)])

                def pv_part(h):
                    ph = h % 2
                    bd = next_bank()
                    bnum = [next_bank() for _ in range(2)]
                    for mt in range(2):
                        P.op("pe", lambda e, mt=mt, bd=bd, ph=ph: e.matmul(psum[:, bd, 0:n], lhsT=cb(CB_ONES), rhs=pT[:, mt, ph * 512:ph * 512 + n],
                                                                          start=(mt == 0), stop=(mt == 1)),
                             reads=["cbf", ("pT", mt, ph)], writes=[("ps", bd)])
                    for dc in range(2):
                        for mt in range(2):
                            P.op("pe", lambda e, mt=mt, dc=dc, h=h, bq_=bnum[dc], ph=ph: e.matmul(
                                psum[:, bq_, 0:n], lhsT=mv[:, mt, (h * 2 + dc) * 128:(h * 2 + dc + 1) * 128], rhs=pT[:, mt, ph * 512:ph * 512 + n],
                                start=(mt == 0), stop=(mt == 1)), reads=[mkey, ("pT", mt, ph)], writes=[("ps", bnum[dc])])
                    P.op("act", lambda e, bd=bd: e.activation(out=nrm1[:, 0:n], in_=psum[:, bd, 0:n], func=AF.Ln), reads=[("ps", bd)], writes=["nrm1"])
                    P.op("act", lambda e: e.activation(out=nrm1[:, 0:n], in_=nrm1[:, 0:n], func=AF.Exp, scale=-1.0), reads=["nrm1"], writes=["nrm1"])
                    for dc in range(2):
                        P.op("dve", lambda e, dc=dc, h=h, bq_=bnum[dc]: e.tensor_tensor(out=hT[:, h * 2 + dc, c0:c0 + n], in0=psum[:, bq_, 0:n],
                                                                                      in1=nrm1[:, 0:n], op=ALU.mult),
                             reads=[("ps", bnum[dc]), "nrm1"], writes=[("hT", ti)])

                s_part(0)
                for h in range(4):
                    if h + 1 < 4:
                        s_part(h + 1)
                    pv_part(h)

            if pt:
                for ti in pt:
                    c0, n = TILES[ti]
                    if xon("att"):
                        attend(c0, n, ti, memk, memv, "memk")
            if has_s:
                for bq in range(4):
                    P.dma("pool", memv[:, :, :], cmv_in[l, bq].rearrange("(k p) n -> p k n", p=128), writes=["memk"])
                    mkn = vtok[:, 0:4, :].rearrange("p a n -> p (a n)").rearrange("p (k n) -> p k n", k=2)
                    P.dma("pool", mkn, cmk_in[l, bq].rearrange("(k p) n -> p k n", p=128), writes=[("vtok", 0)])
                    for c in range(8):
                        b = next_bank()
                        pv = psum[:, b, :].bitcast(BF16)
                        for mt in range(2):
                            P.op("pe", lambda e, c=c, mt=mt, pv=pv: e.matmul(pv[:, mt * 128:(mt + 1) * 128], lhsT=mkn[:, mt, c * 128:(c + 1) * 128],
                                                                             rhs=cb(CB_IDENT), start=True, stop=True, is_transpose=True),
                                 reads=[("vtok", 0), "cbf"], writes=[("ps", b)])
                        P.op("dve", lambda e, c=c, pv=pv: e.tensor_copy(out=memk[:, c, :], in_=pv[:, 0:256]), reads=[("ps", b)], writes=["memk"])
                    attend(TP + 16 * bq, 16, 2, memk, memv, "memk")
            if xon("xo"):
                linear_fm(lambda mg: w_xo[l][:, mg * 512:(mg + 1) * 512], hT, "hT", 8, 8, tilesR, add_to_x(1.0))

        def final_norm(ps_, tiles):
            pi = ps_["xidx"]
            if pi < 2:
                return

            def out_fn(ti, c0, n, tb, tkey):
                for k in range(8):
                    sj = k % 2
                    P.op("dve", lambda e, k=k, sj=sj: e.scalar_tensor_tensor(out=stg[:, sj, 0:n], in0=xT[:, k, c0:c0 + n], scalar=smc(SM_GFINAL + k),
                                                                            in1=tb[:, 0:n], op0=ALU.mult, op1=ALU.mult),
                         reads=[("xT", ti), tkey, "small"], writes=[("stg", sj)])
                    oc = (pi - 2) * TP + c0 if ti < 2 else SEG
                    P.dma("sp", o_yT[k * 128:(k + 1) * 128, oc:oc + n], stg[:, sj, 0:n], reads=[("stg", sj)], writes=[("o_y", pi, ti, k)])
            rmsnorm(tiles, 0, out_fn)

        ALLT = [0, 1]
        passes = [
            dict(name="H1", xidx=0, A=[[0, 1], [1], [], []], R=[[1], [], [], []]),
            dict(name="H2", xidx=1, A=[[0, 1], [0, 1], [0, 1], [1]], R=[[0, 1], [0, 1], [1], []]),
            dict(name="O1", xidx=2, A=[ALLT] * 4, R=[ALLT] * 4),
            dict(name="O2", xidx=3, A=[[0, 1, 2]] * 4, R=[[0, 1, 2]] * 4),
        ]
        for ps_ in passes:
            if DBG["passes"] is not None and ps_["name"] not in DBG["passes"]:
                continue
            st = DBG["stages"]
            on = lambda name: st is None or name in st
            allk = [("xT", 0), ("xT", 1)]
            P.dma("sp", xT[:, :, 0:TP], xT_in[ps_["xidx"]].rearrange("(c p) n -> p c n", p=128), writes=allk)
            if ps_["name"] == "O2":
                P.dma("sp", xT[:, :, TP:TOK], xsT_in.rearrange("(c p) n -> p c n", p=128), writes=[("xT", 2)])
            for l in range(L):
                A = ps_["A"][l]
                R = ps_["R"][l]
                if not A:
                    continue
                if on("ffn1"):
                    ffn(l, w_g1, w_u1, w_d1, G_FFN1, A)
                if on("win"):
                    rmsnorm(A, gcol(l, G_MIX, 0))
                    if DBG.get("sub") is None or "cin" in DBG["sub"]:
                        carry_in(l, ps_, R)
                    w_in_stage(l, ps_, A)
                    if DBG.get("sub") is None or "cout" in DBG["sub"]:
                        carry_out(l, ps_, A)
                if not R:
                    continue
                P.dma("pool", poolw[:, :, :], poolw_in[l], writes=["poolw"])
                if on("attn"):
                    attn_prompt(l, ps_, R)
                if on("sattn"):
                    attn_sample(l, ps_, R)
                if on("xattn"):
                    xattn_mem(l, ps_, R)
                if on("bc"):
                    bc_stage(l, ps_, R)
                if on("wout"):
                    linear_fm(lambda mg: w_out[l][:, mg * 512:(mg + 1) * 512], hT, "hT", 8, 8, R, add_to_x(1.0))
                if on("xattn"):
                    xattn(l, ps_, R)
                if on("ffn2"):
                    ffn(l, w_g2, w_u2, w_d2, G_FFN2, R)
            final_norm(ps_, ps_["A"][L - 1])
        P.final_waits()
        P.emit(block)
    return nc


def _fm(v):
    return np.ascontiguousarray(v.reshape(8, 128).T)


def kernel(x_prompt, x_sample, mem_prompt, cache_attn_k, cache_attn_v, state_conv, state_pool,
           cache_mem_k, cache_mem_v, g_ffn1, w_ffn1_gate, w_ffn1_up, w_ffn1_down, g_mix, w_in,
           rel_bias, conv_w, pool_w, pool_scale, g_heads, w_out, g_xattn, g_mem, w_xq, w_xk, w_xv,
           w_xo, g_ffn2, w_ffn2_gate, w_ffn2_up, w_ffn2_down, g_final):
    f32 = np.float32
    A = lambda a: np.ascontiguousarray(np.asarray(a, dtype=f32))
    x_prompt, x_sample, mem_prompt = A(x_prompt), A(x_sample), A(mem_prompt)
    cache_attn_k, cache_attn_v = A(cache_attn_k), A(cache_attn_v)
    cache_mem_k, cache_mem_v = A(cache_mem_k), A(cache_mem_v)
    state_conv, state_pool = A(state_conv), A(state_pool)
    rel_bias = A(rel_bias)

    cbf = np.zeros((128, 7, 128), f32)
    cbf[:, CB_IDENT] = np.eye(128, dtype=f32)
    cbf[0:64, CB_BLK64, 0:64] = 1.0 / 64
    cbf[64:128, CB_BLK64, 64:128] = 1.0 / 64
    cbf[:, CB_MEAN] = 1.0 / 1024
    cbf[:, CB_ONES] = 1.0
    cbf[0:64, CB_SM0, 64:128] = NEGM
    cbf[64:128, CB_SM4, 0:64] = NEGM
    cbf = cbf.reshape(128, 7 * 128)

    kj = np.arange(640)[:, None]
    idx_p = np.clip(512 + np.arange(128)[None, :] - kj, -128, 128) + 128
    bias_t = np.ascontiguousarray(rel_bias[:, :, idx_p].transpose(0, 2, 1, 3))
    idx_s = np.clip(512 + np.arange(16)[None, :] - kj, -128, 128) + 128
    sbias_t = np.ascontiguousarray(rel_bias[:, :, idx_s].transpose(0, 2, 1, 3))

    pw = np.zeros((L, 128, 2, 128), f32)
    for c in range(2):
        for g in range(2):
            pw[:, g * 64:(g + 1) * 64, c, g * 64:(g + 1) * 64] = np.asarray(pool_w, f32)[:, 2 * c + g]

    gl = [g_ffn1, g_mix, g_heads, g_xattn, g_mem, g_ffn2]
    windows = [2, 4, 8, 16]
    in_maps = []
    shared = dict(
        cbf_in=cbf, bias_in=bias_t, sbias_in=sbias_t, poolw_in=pw,
        w_g1=A(w_ffn1_gate), w_u1=A(w_ffn1_up), w_d1=A(w_ffn1_down),
        w_g2=A(w_ffn2_gate), w_u2=A(w_ffn2_up), w_d2=A(w_ffn2_down),
        w_in=A(w_in), w_out=A(w_out), w_xq=A(w_xq), w_xk=A(w_xk), w_xv=A(w_xv), w_xo=A(w_xo),
    )
    for c in range(NCORE):
        sq, seg = c // 4, c % 4
        S = seg * SEG
        small = np.zeros((128, SM_N), f32)
        for l in range(L):
            for wi, g in enumerate(gl):
                small[:, gcol(l, wi, 0):gcol(l, wi, 0) + 8] = _fm(np.asarray(g, f32)[l])
            for i in range(3):
                cwv = np.asarray(conv_w, f32)[l, i].reshape(2, 128).T
                small[:, SM_CONV + (l * 3 + i) * 2:SM_CONV + (l * 3 + i) * 2 + 2] = cwv
            small[:, SM_PSCALE + l * 2:SM_PSCALE + l * 2 + 2] = np.asarray(pool_scale, f32)[l].reshape(2, 128).T
        small[:, SM_GFINAL:SM_GFINAL + 8] = _fm(np.asarray(g_final, f32))
        for ch in range(2):
            for g in range(2):
                w = windows[2 * ch + g]
                small[g * 64:(g + 1) * 64, SM_INVW + ch] = 1.0 / w
                for pos in range(16):
                    cnt = min(w, pos + 1) if seg == 0 else w
                    small[g * 64:(g + 1) * 64, SM_INVCNT + ch * 16 + pos] = 1.0 / cnt
        small[:, SM_HMASK] = NEGM if seg == 0 else 0.0
        small[:, SM_TFLAG] = 0.0 if seg == 0 else 1.0
        small[:, SM_EPS] = EPS
        small[:, SM_ZERO] = 0.0

        xT = np.zeros((4, D, TP), f32)
        for p in range(4):
            t0 = S - 2048 + p * TP
            if t0 >= 0:
                xT[p] = x_prompt[sq, t0:t0 + TP, :].T
        bs = slice(4 * c, 4 * c + 4)
        xs = x_sample[bs].reshape(NS, D)
        shu = np.zeros((L, 128, 2, 4, 16), f32)
        shp = np.zeros((L, 128, 2, 4, 16), f32)
        sc = state_conv[:, bs]
        sp_ = state_pool[:, bs]
        shu[:, :, :, :, 14:16] = sc.reshape(L, 4, 2, 2, 128).transpose(0, 4, 3, 1, 2)
        shp[:, :, :, :, 1:16] = sp_.reshape(L, 4, 15, 2, 128).transpose(0, 4, 3, 1, 2)
        m = dict(shared)
        m.update(
            xT_in=xT, xsT_in=np.ascontiguousarray(xs.T), memT_in=np.ascontiguousarray(mem_prompt[sq].T),
            small_in=small, shu_in=shu, shp_in=shp,
            ck_in=np.ascontiguousarray(cache_attn_k[:, bs].reshape(L, 4, 512, 512)),
            cv_in=np.ascontiguousarray(cache_attn_v[:, bs].reshape(L, 4, 512, 512)),
            cmk_in=np.ascontiguousarray(cache_mem_k[:, bs].reshape(L, 4, 256, 1024)),
            cmv_in=np.ascontiguousarray(cache_mem_v[:, bs].reshape(L, 4, 256, 1024)),
        )
        in_maps.append(m)

    nc = build_program()
    res = run_bass_kernel_spmd(nc, in_maps, core_ids=list(range(NCORE)))
    R = res.results

    y_prompt = np.zeros((2, 8192, D), f32)
    y_sample = np.zeros((32, 16, D), f32)
    nk_p = np.zeros((L, 2, 512, 8, 64), f32); nv_p = np.zeros_like(nk_p)
    nc_p = np.zeros((L, 2, 2, 256), f32); np_p = np.zeros((L, 2, 15, 256), f32)
    mk_p = np.zeros((L, 2, 256, 4, 256), f32); mv_p = np.zeros_like(mk_p)
    nk_s = np.zeros((L, 32, 512, 8, 64), f32); nv_s = np.zeros_like(nk_s)
    nc_s = np.zeros((L, 32, 2, 256), f32); np_s = np.zeros((L, 32, 15, 256), f32)
    for c in range(NCORE):
        sq, seg = c // 4, c % 4
        r = R[c]
        yT = np.asarray(r["o_yT"])
        y_prompt[sq, seg * SEG:(seg + 1) * SEG] = yT[:, 0:SEG].T
        y_sample[4 * c:4 * c + 4] = yT[:, SEG:SEG + NS].T.reshape(4, 16, D)
        if seg == 3:
            nk_p[:, sq] = np.asarray(r["o_pk"]).reshape(L, 512, 8, 64)
            nv_p[:, sq] = np.asarray(r["o_pv"]).reshape(L, 512, 8, 64)
            pc = np.asarray(r["o_pc"])
            pp = np.asarray(r["o_pp"])
            nc_p[:, sq] = pc[:, :, :, 14:16].transpose(0, 3, 2, 1).reshape(L, 2, 256)
            np_p[:, sq] = pp[:, :, :, 1:16].transpose(0, 3, 2, 1).reshape(L, 15, 256)
        if seg == 0:
            mk_p[:, sq] = np.asarray(r["o_mkT"]).transpose(0, 2, 1).reshape(L, 256, 4, 256)
            mv_p[:, sq] = np.asarray(r["o_mv"]).reshape(L, 256, 4, 256)
        nk_s[:, 4 * c:4 * c + 4] = np.asarray(r["o_sk"]).reshape(L, 4, 512, 8, 64)
        nv_s[:, 4 * c:4 * c + 4] = np.asarray(r["o_sv"]).reshape(L, 4, 512, 8, 64)
        scc = np.asarray(r["o_sc"])
        spp = np.asarray(r["o_sp"])
        nc_s[:, 4 * c:4 * c + 4] = scc[:, :, :, :, 14:16].transpose(0, 3, 4, 2, 1).reshape(L, 4, 2, 256)
        np_s[:, 4 * c:4 * c + 4] = spp[:, :, :, :, 1:16].transpose(0, 3, 4, 2, 1).reshape(L, 4, 15, 256)
    return (y_prompt, y_sample, nk_p, nv_p, nc_p, np_p, mk_p, mv_p, nk_s, nv_s, nc_s, np_s)
```
